# Optimizing a Trainium2 kernel written in Bass

```python
import jax
import jax.numpy as jnp
from jax import lax
import numpy as np

D_MODEL = 1024
BATCH = 8
SEQ = 2048
DEPTH = 2

D_MIX = D_MODEL
HEAD_DIM = 64
A_WIDTH = D_MIX // 4
A_BLOCKS = A_WIDTH // HEAD_DIM
A_BLOCK = HEAD_DIM
CONV_W = 4
LRU_C = 8.0
B_WIDTH = 3 * D_MIX // 8
B_HEADS = B_WIDTH // HEAD_DIM
B_DK = HEAD_DIM
B_DV = HEAD_DIM
B_FDIM = B_HEADS * B_DK
CHUNK = 64
C_WIDTH = D_MIX - A_WIDTH - B_WIDTH
C_HEADS = C_WIDTH // HEAD_DIM
C_HEAD = HEAD_DIM
C_LORA_W = 64
C_LORA_A = 64
C_LORA_G = 128
C_COLS = 3 * C_WIDTH + C_LORA_W + C_LORA_A + C_LORA_G
P_SIZES = (A_WIDTH, A_WIDTH, B_FDIM, B_FDIM, B_WIDTH, B_WIDTH, C_COLS)
P_WIDTH = 2 * A_WIDTH + 2 * B_FDIM + 2 * B_WIDTH + C_COLS
D_FF = -(-8 * D_MODEL // (3 * 256)) * 256
NORM_EPS = 1e-6
GN_EPS = 64e-5

kernel_name = 'hybrid_rglru_hgrn2_rwkv7_block'


def rmsnorm(x, gain=None, eps=NORM_EPS):
    xf = x.astype(jnp.float32)
    y = xf * lax.rsqrt(jnp.mean(xf * xf, axis=-1, keepdims=True) + eps)
    if gain is not None:
        y = y * gain.astype(jnp.float32)
    return y.astype(x.dtype)


def _causal_conv(u, w, b):
    T = u.shape[1]
    up = jnp.pad(u, ((0, 0), (CONV_W - 1, 0), (0, 0)))
    out = b
    for j in range(CONV_W):
        out = out + w[j] * up[:, j:j + T]
    return out


def _lin_combine(left, right):
    a1, b1 = left
    a2, b2 = right
    return a1 * a2, a2 * b1 + b2


def rglru_mixer(xa, ya, conv_w, conv_b, rg_w, rg_b, ig_w, ig_b, lam):
    dt = xa.dtype
    Bn, T, _ = xa.shape
    u = _causal_conv(xa, conv_w, conv_b)
    ub = u.reshape(Bn, T, A_BLOCKS, A_BLOCK)
    r = jax.nn.sigmoid(jnp.einsum('btgi,gij->btgj', ub, rg_w).reshape(Bn, T, A_WIDTH) + rg_b)
    i = jax.nn.sigmoid(jnp.einsum('btgi,gij->btgj', ub, ig_w).reshape(Bn, T, A_WIDTH) + ig_b)
    log_a = -LRU_C * r.astype(jnp.float32) * jax.nn.softplus(-lam.astype(jnp.float32))
    a = jnp.exp(log_a)
    mult = jnp.sqrt(-jnp.expm1(2.0 * log_a))
    mult = mult.at[:, 0].set(1.0)
    b = mult * (i * u).astype(jnp.float32)
    _, h = lax.associative_scan(_lin_combine, (a, b), axis=1)
    y = h.astype(dt) * jax.nn.gelu(ya)
    return rmsnorm(y.reshape(Bn, T, A_BLOCKS, A_BLOCK)).reshape(Bn, T, A_WIDTH)


def hgrn2_mixer(q, f_logit, v, g, lb):
    dt = q.dtype
    Bn, T, _ = q.shape
    nC = T // CHUNK
    lb = lb.astype(jnp.float32)
    log_f = jnp.logaddexp(jnp.log(lb), jnp.log1p(-lb) + jax.nn.log_sigmoid(f_logit.astype(jnp.float32)))
    k = -jnp.expm1(log_f)

    def to_chunks(z, dh):
        return z.reshape(Bn, nC, CHUNK, B_HEADS, dh).transpose(1, 0, 3, 2, 4)

    qc = to_chunks(q.astype(jnp.float32) * (B_DK ** -0.5), B_DK)
    kc = to_chunks(k, B_DK)
    vc = to_chunks(v.astype(jnp.float32), B_DV)
    lfc = to_chunks(log_f, B_DK)
    mask = jnp.tril(jnp.ones((CHUNK, CHUNK), dtype=bool))[:, :, None]

    def step(S, inp):
        qb, kb, vb, lfb = inp
        bcum = jnp.cumsum(lfb, axis=2)
        o_inter = jnp.einsum('bhtk,bhkv->bhtv', qb * jnp.exp(bcum), S)
        diff = bcum[:, :, :, None, :] - bcum[:, :, None, :, :]
        decay = jnp.exp(jnp.where(mask, diff, -jnp.inf))
        A = jnp.einsum('bhtk,bhsk,bhtsk->bhts', qb, kb, decay)
        o_intra = jnp.einsum('bhts,bhsv->bhtv', A, vb)
        blast = bcum[:, :, -1:, :]
        k_dec = kb * jnp.exp(blast - bcum)
        S = jnp.exp(blast[:, :, 0, :])[..., None] * S + jnp.einsum('bhsk,bhsv->bhkv', k_dec, vb)
        return S, o_inter + o_intra

    S0 = jnp.zeros((Bn, B_HEADS, B_DK, B_DV), jnp.float32)
    _, o = lax.scan(step, S0, (qc, kc, vc, lfc))
    o = o.transpose(1, 0, 3, 2, 4).reshape(Bn, T, B_HEADS, B_DV)
    o = rmsnorm(o).reshape(Bn, T, B_WIDTH)
    return (o * jax.nn.silu(g.astype(jnp.float32))).astype(dt)


def rwkv7_mixer(pc, mu, w0, w_up, a0, a_up, g_up, k_k, k_a, r_k, lnx_w, lnx_b):
    dt = pc.dtype
    Bn, T, _ = pc.shape
    prev = jnp.pad(pc[:, :-1], ((0, 0), (1, 0), (0, 0)))
    ps = (pc + (prev - pc) * mu).astype(jnp.float32)
    r, k, v, wl, al, gl = jnp.split(
        ps, [C_WIDTH, 2 * C_WIDTH, 3 * C_WIDTH, 3 * C_WIDTH + C_LORA_W,
             3 * C_WIDTH + C_LORA_W + C_LORA_A], axis=-1)
    w = -jax.nn.softplus(-(w0 + jnp.tanh(wl) @ w_up)) - 0.5
    decay = jnp.exp(-jnp.exp(w))
    a = jax.nn.sigmoid(a0 + al @ a_up)
    g = jax.nn.sigmoid(gl) @ g_up

    def heads(z):
        return z.reshape(Bn, T, C_HEADS, C_HEAD)

    kk = heads(k * k_k)
    kk = kk * lax.rsqrt(jnp.sum(kk * kk, axis=-1, keepdims=True) + 1e-12)
    k = k * (1.0 + (a - 1.0) * k_a)
    rh, kh, vh, wh, ah = heads(r), heads(k), heads(v), heads(decay), heads(a)

    def step(S, inp):
        rt, wt, kt, vt, kkt, at = inp
        sa = jnp.einsum('bhvk,bhk->bhv', S, -kkt)
        S = S * wt[:, :, None, :] + sa[..., None] * (kkt * at)[:, :, None, :] + vt[..., None] * kt[:, :, None, :]
        return S, jnp.einsum('bhvk,bhk->bhv', S, rt)

    xs = tuple(jnp.moveaxis(z, 1, 0) for z in (rh, wh, kh, vh, kk, ah))
    S0 = jnp.zeros((Bn, C_HEADS, C_HEAD, C_HEAD), jnp.float32)
    _, y = lax.scan(step, S0, xs)
    y = jnp.moveaxis(y, 0, 1)
    mean = jnp.mean(y, axis=-1, keepdims=True)
    var = jnp.mean(jnp.square(y - mean), axis=-1, keepdims=True)
    y = ((y - mean) * lax.rsqrt(var + GN_EPS)).reshape(Bn, T, C_WIDTH) * lnx_w + lnx_b
    bonus = (jnp.sum(rh * kh * r_k, axis=-1, keepdims=True) * vh).reshape(Bn, T, C_WIDTH)
    return ((y + bonus) * g).astype(dt)


def swiglu(h, w_gate, w_up, w_down):
    return (jax.nn.silu(h @ w_gate) * (h @ w_up)) @ w_down


def setup_inputs(seed: int = 0) -> dict:
    key = jax.random.key(seed)
    ks = jax.random.split(key, 32)
    f32 = jnp.float32
    nrm = lambda k, shape, s: jax.random.normal(k, shape, f32) * s
    L = DEPTH
    u = jax.random.uniform(ks[10], (L, A_WIDTH), f32, 0.9, 0.999)
    s = u ** (1.0 / LRU_C)
    return {
        'x': nrm(ks[0], (BATCH, SEQ, D_MODEL), 1.0),
        'c': nrm(ks[1], (BATCH, D_MODEL), 1.0),
        'norm1_g': 1.0 + nrm(ks[2], (L, D_MODEL), 0.02),
        'norm2_g': 1.0 + nrm(ks[3], (L, D_MODEL), 0.02),
        'ada_w': nrm(ks[4], (L, D_MODEL, 6 * D_MODEL), 0.5 * D_MODEL ** -0.5),
        'ada_b': nrm(ks[5], (L, 6 * D_MODEL), 0.02),
        'w_in': nrm(ks[6], (L, D_MODEL, P_WIDTH), D_MODEL ** -0.5),
        'conv_w': nrm(ks[7], (L, CONV_W, A_WIDTH), CONV_W ** -0.5),
        'conv_b': nrm(ks[8], (L, A_WIDTH), 0.02),
        'rg_w': nrm(ks[9], (L, A_BLOCKS, A_BLOCK, A_BLOCK), A_BLOCK ** -0.5),
        'rg_b': nrm(ks[11], (L, A_WIDTH), 0.02),
        'ig_w': nrm(ks[12], (L, A_BLOCKS, A_BLOCK, A_BLOCK), A_BLOCK ** -0.5),
        'ig_b': nrm(ks[13], (L, A_WIDTH), 0.02),
        'lru_lam': jnp.log(s) - jnp.log1p(-s),
        'hgrn_lb': nrm(ks[14], (L, B_FDIM), 0.5),
        'rwkv_mu': jax.random.uniform(ks[15], (L, C_COLS), f32, 0.0, 1.0),
        'rwkv_w0': jax.random.uniform(ks[16], (L, C_WIDTH), f32, -6.0, -1.0),
        'rwkv_w_up': nrm(ks[17], (L, C_LORA_W, C_WIDTH), 0.1),
        'rwkv_a0': nrm(ks[18], (L, C_WIDTH), 0.1),
        'rwkv_a_up': nrm(ks[19], (L, C_LORA_A, C_WIDTH), C_LORA_A ** -0.5),
        'rwkv_g_up': nrm(ks[20], (L, C_LORA_G, C_WIDTH), C_LORA_G ** -0.5),
        'rwkv_k_k': 0.85 + nrm(ks[21], (L, C_WIDTH), 0.05),
        'rwkv_k_a': 1.0 + nrm(ks[22], (L, C_WIDTH), 0.05),
        'rwkv_r_k': nrm(ks[23], (L, C_HEADS, C_HEAD), 0.1),
        'rwkv_lnx_w': 1.0 + nrm(ks[24], (L, C_WIDTH), 0.02),
        'rwkv_lnx_b': nrm(ks[25], (L, C_WIDTH), 0.02),
        'mix_beta': 1.0 + nrm(ks[26], (L, D_MIX), 0.02),
        'w_out': nrm(ks[27], (L, D_MIX, D_MODEL), D_MIX ** -0.5),
        'ffn_w_gate': nrm(ks[28], (L, D_MODEL, D_FF), D_MODEL ** -0.5),
        'ffn_w_up': nrm(ks[29], (L, D_MODEL, D_FF), D_MODEL ** -0.5),
        'ffn_w_down': nrm(ks[30], (L, D_FF, D_MODEL), D_FF ** -0.5),
        'final_g': 1.0 + nrm(ks[31], (D_MODEL,), 0.02),
    }


def reference(x, c, norm1_g, norm2_g, ada_w, ada_b, w_in, conv_w, conv_b, rg_w, rg_b,
              ig_w, ig_b, lru_lam, hgrn_lb, rwkv_mu, rwkv_w0, rwkv_w_up, rwkv_a0, rwkv_a_up,
              rwkv_g_up, rwkv_k_k, rwkv_k_a, rwkv_r_k, rwkv_lnx_w, rwkv_lnx_b, mix_beta, w_out,
              ffn_w_gate, ffn_w_up, ffn_w_down, final_g):
    lb_all = jnp.cumsum(jax.nn.softmax(hgrn_lb.astype(jnp.float32), axis=0), axis=0)
    lb_all = lb_all - lb_all[0]
    cs = jax.nn.silu(c)
    offsets = []
    acc = 0
    for sz in P_SIZES[:-1]:
        acc += sz
        offsets.append(acc)
    for l in range(DEPTH):
        mod = (cs @ ada_w[l] + ada_b[l])[:, None, :]
        sh1, sc1, g1, sh2, sc2, g2 = jnp.split(mod, 6, axis=-1)
        h = rmsnorm(x, norm1_g[l]) * (1.0 + sc1) + sh1
        p = h @ w_in[l]
        pa_x, pa_y, pb_q, pb_f, pb_v, pb_g, pc = jnp.split(p, offsets, axis=-1)
        ya = rglru_mixer(pa_x, pa_y, conv_w[l], conv_b[l], rg_w[l], rg_b[l], ig_w[l], ig_b[l], lru_lam[l])
        yb = hgrn2_mixer(pb_q, pb_f, pb_v, pb_g, lb_all[l])
        yc = rwkv7_mixer(pc, rwkv_mu[l], rwkv_w0[l], rwkv_w_up[l], rwkv_a0[l], rwkv_a_up[l],
                         rwkv_g_up[l], rwkv_k_k[l], rwkv_k_a[l], rwkv_r_k[l], rwkv_lnx_w[l], rwkv_lnx_b[l])
        y = jnp.concatenate([ya, yb, yc], axis=-1) * mix_beta[l]
        x = x + g1 * (y @ w_out[l])
        h = rmsnorm(x, norm2_g[l]) * (1.0 + sc2) + sh2
        x = x + g2 * swiglu(h, ffn_w_gate[l], ffn_w_up[l], ffn_w_down[l])
    return rmsnorm(x, final_g)
```

```python
import math
STOPB = 99
from contextlib import ExitStack
import numpy as np
import concourse.bass as bass
import concourse.mybir as mybir
from concourse.bass_utils import run_bass_kernel_spmd

F32 = mybir.dt.float32
BF16 = mybir.dt.bfloat16
ALU = mybir.AluOpType
AF = mybir.ActivationFunctionType

L = 2
D = 1024
T = 2048
NB = 8
PW = 3456
DFF = 2816
TB = 256
NTB = T // TB
CH = 64
NCH = TB // CH
TBF = 1024
NTBF = T // TBF
ENGS = ("pe", "act", "dve", "pool", "sp")
NDMA = 8


class Dep:
    __slots__ = ("w", "r")

    def __init__(self):
        self.w = None
        self.r = {}


class BDep(Dep):
    __slots__ = ()


class Prog:
    def __init__(self, nc):
        self.nc = nc
        self.q = {e: [] for e in ENGS}
        self.cnt = {e: 0 for e in ENGS}
        self.dcnt = {e: 0 for e in ENGS}
        self.seen = {e: {} for e in ENGS}
        self.sems = {}
        self.ctx = []
        for e in ENGS:
            self._mk(("c", e))
            for i in range(NDMA):
                self._mk(("d", e, i))

    def _mk(self, key):
        cm = self.nc.semaphore("s_" + "_".join(str(k) for k in key))
        self.sems[key] = cm.__enter__()
        self.ctx.append(cm)

    def _need(self, eng, reads, writes):
        needs = {}

        def add(k, v):
            if needs.get(k, 0) < v:
                needs[k] = v
        for d in reads:
            if d.w is not None:
                add(*d.w)
        for d in writes:
            if d.w is not None:
                add(*d.w)
            for k, v in d.r.items():
                add(k, v)
        for k, v in needs.items():
            if k == ("c", "pe") and eng == "pe":
                continue
            if self.seen[eng].get(k, 0) >= v:
                continue
            self.seen[eng][k] = v
            self.q[eng].append(("wait", k, v))

    def _mark(self, tok, reads, writes):
        k, v = tok
        for d in reads:
            if d.r.get(k, 0) < v:
                d.r[k] = v
        for d in writes:
            d.w = tok
            d.r = {}

    def op(self, eng, fn, reads=(), writes=()):
        if any(isinstance(d, BDep) for d in reads):
            writes = list(writes) + [d for d in reads if isinstance(d, BDep)]
            reads = [d for d in reads if not isinstance(d, BDep)]
        self._need(eng, reads, writes)
        self.cnt[eng] += 1
        tok = (("c", eng), self.cnt[eng])
        self.q[eng].append(("op", fn, tok[0], 1))
        self._mark(tok, reads, writes)
        return tok

    def dma(self, eng, out, in_, reads=(), writes=()):
        self._need(eng, reads, writes)
        i = self.dcnt[eng]
        self.dcnt[eng] += 1
        slot, rnd = i % NDMA, i // NDMA
        key = ("d", eng, slot)
        if rnd > 0 and self.seen[eng].get(key, 0) < 16 * rnd:
            self.seen[eng][key] = 16 * rnd
            self.q[eng].append(("wait", key, 16 * rnd))
        tok = (key, 16 * (rnd + 1))
        self.q[eng].append(("op", lambda e: e.dma_start(out=out, in_=in_), key, 16))
        self._mark(tok, reads, writes)
        return tok

    def barrier(self):
        toks = [(("c", e), self.cnt[e]) for e in ENGS if self.cnt[e] > 0]
        for e in ENGS:
            for sl in range(NDMA):
                if self.dcnt[e] > sl:
                    toks.append((("d", e, sl), 16 * ((self.dcnt[e] - sl + NDMA - 1) // NDMA)))
        for e in ENGS:
            for k, v in toks:
                if k == ("c", "pe") and e == "pe":
                    continue
                self.wait_tok(e, (k, v))

    def wait_tok(self, eng, tok):
        k, v = tok
        if self.seen[eng].get(k, 0) < v:
            self.seen[eng][k] = v
            self.q[eng].append(("wait", k, v))

    def emit(self):
        sems = self.sems
        waited = {}
        for e in ENGS:
            for it in self.q[e]:
                if it[0] == "wait" and it[1][0] == "c":
                    waited.setdefault(it[1], set()).add(it[2])
        rank = {k: {v: i + 1 for i, v in enumerate(sorted(vs))} for k, vs in waited.items()}

        def run(name):
            def body(e):
                n = 0
                for it in self.q[name]:
                    if it[0] == "wait":
                        k, v = it[1], it[2]
                        if k[0] == "c":
                            v = rank[k][v]
                        e.wait_ge(sems[k], v)
                    else:
                        ins = it[1](e)
                        if it[2][0] == "c":
                            n += 1
                            if n in rank.get(it[2], ()):
                                ins.then_inc(sems[it[2]], 1)
                        else:
                            ins.then_inc(sems[it[2]], it[3])
            return body
        with self.nc.Block() as block:
            block.tensor(run("pe"))
            block.scalar(run("act"))
            block.vector(run("dve"))
            block.gpsimd(run("pool"))
            block.sync(run("sp"))

    def close(self):
        for cm in reversed(self.ctx):
            cm.__exit__(None, None, None)


def col_layout():
    idx = {}
    n = [0]

    def add(name, k):
        idx[name] = n[0]
        n[0] += k
    add("c", 8)
    add("final_g", 8)
    for l in range(L):
        add(f"norm1_g{l}", 8)
        add(f"norm2_g{l}", 8)
        add(f"ada_b{l}", 48)
        add(f"conv_w{l}", 8)
        add(f"conv_b{l}", 2)
        add(f"rg_b{l}", 2)
        add(f"ig_b{l}", 2)
        add(f"lam{l}", 2)
        add(f"hgrn_lb{l}", 3)
        add(f"mu{l}", 11)
        add(f"w0{l}", 3)
        add(f"a0{l}", 3)
        add(f"k_k{l}", 3)
        add(f"k_a{l}", 3)
        add(f"r_k{l}", 3)
        add(f"lnx_w{l}", 3)
        add(f"lnx_b{l}", 3)
        add(f"beta{l}", 8)
    return idx, n[0]


CI, NCOL = col_layout()
K_ID, K_BO, K_MINC, K_MSTR, K_MTRIL, K_IDP, K_RST = 0, 128, 256, 320, 384, 448, 512
NCONST = 512 + TB


def make_consts():
    cst = np.zeros((128, NCONST), np.float32)
    p = np.arange(128)
    cst[:, K_ID:K_ID + 128] = np.eye(128, dtype=np.float32)
    cst[:, K_BO:K_BO + 128] = (p[:, None] // 64 == p[None, :] // 64).astype(np.float32)
    j = np.arange(64)
    cst[:, K_MINC:K_MINC + 64] = ((p[:, None] % 64) <= j[None, :]).astype(np.float32)
    cst[:, K_MSTR:K_MSTR + 64] = ((p[:, None] % 64) < j[None, :]).astype(np.float32)
    cst[:, K_MTRIL:K_MTRIL + 64] = (j[None, :] < (p[:, None] % 64)).astype(np.float32)
    cst[:, K_IDP:K_IDP + 64] = ((p[:, None] % 64) == j[None, :]).astype(np.float32)
    t = np.arange(TB)
    cst[:, K_RST:K_RST + TB] = (t % CH != 0).astype(np.float32)[None, :]
    return cst


def pack_cols(inp, b):
    cols = np.zeros((128, NCOL), np.float32)

    def put(name, v):
        v = np.asarray(v, np.float32).reshape(-1, 128)
        cols[:, CI[name]:CI[name] + v.shape[0]] = v.T
    put("c", inp["c"][b])
    put("final_g", inp["final_g"])
    for l in range(L):
        put(f"norm1_g{l}", inp["norm1_g"][l])
        put(f"norm2_g{l}", inp["norm2_g"][l])
        put(f"ada_b{l}", inp["ada_b"][l])
        cw = np.asarray(inp["conv_w"][l], np.float32)
        cwp = np.stack([cw[j, cc * 128:(cc + 1) * 128] for cc in range(2) for j in range(4)], 0)
        put(f"conv_w{l}", cwp)
        put(f"conv_b{l}", inp["conv_b"][l])
        put(f"rg_b{l}", inp["rg_b"][l])
        put(f"ig_b{l}", inp["ig_b"][l])
        put(f"lam{l}", inp["lru_lam"][l])
        put(f"hgrn_lb{l}", inp["hgrn_lb"][l])
        put(f"mu{l}", inp["rwkv_mu"][l])
        put(f"w0{l}", inp["rwkv_w0"][l])
        put(f"a0{l}", inp["rwkv_a0"][l])
        put(f"k_k{l}", inp["rwkv_k_k"][l])
        put(f"k_a{l}", inp["rwkv_k_a"][l])
        put(f"r_k{l}", inp["rwkv_r_k"][l])
        put(f"lnx_w{l}", inp["rwkv_lnx_w"][l])
        put(f"lnx_b{l}", inp["rwkv_lnx_b"][l])
        put(f"beta{l}", inp["mix_beta"][l])
    return cols


def blockdiag(w):
    w = np.asarray(w, np.float32)
    out = np.zeros((L, 2, 128, 128), np.float32)
    for l in range(L):
        for g in range(4):
            cc, gg = g // 2, g % 2
            out[l, cc, gg * 64:(gg + 1) * 64, gg * 64:(gg + 1) * 64] = w[l, g]
    return out


def build_program(n_layers=L, taps=(), do_mixer=(True, True, True), do_ffn=True):
    nc = bass.Bass("TRN2", target_bir_lowering=False)
    es = ExitStack()

    def din(name, shape):
        return nc.dram_tensor(name, list(shape), F32, kind="ExternalInput").ap()
    xT_d = din("xT", [8, 128, T])
    cols_d = din("cols", [128, NCOL])
    cst_d = din("cst", [128, NCONST])
    ada_d = din("ada_w", [L, D, 6 * D])
    win_d = din("w_in", [L, D, PW])
    rgbd_d = din("rgbd", [L, 2, 128, 128])
    igbd_d = din("igbd", [L, 2, 128, 128])
    waup_d = din("wa_up", [L, 128, 384])
    gup_d = din("g_up", [L, 128, 384])
    wout_d = din("w_out", [L, D, D])
    wg_d = din("wg", [L, D, DFF])
    wu_d = din("wu", [L, D, DFF])
    wd_d = din("wd", [L, DFF, D])
    outT_d = nc.dram_tensor("outT", [8, 128, T], F32, kind="ExternalOutput").ap()
    tap_d = {}
    for name, shape in taps:
        tap_d[name] = nc.dram_tensor("tap_" + name, list(shape), F32, kind="ExternalOutput").ap()

    P = Prog(nc)

    class Tl:
        def __init__(self, name, shape, dt=F32, nd=1):
            self.t = es.enter_context(nc.sbuf_tensor(name, list(shape), dt))
            self.d = [Dep() for _ in range(nd)]
            self.D = self.d[0]

    EOBJ = {"dve": "vector", "pool": "gpsimd", "act": "scalar"}

    def tt(eng, out, in0, in1, op, R, W):
        P.op(eng, lambda e: e.tensor_tensor(out, in0, in1, op), reads=R, writes=W)

    def ts(eng, out, in0, s1, s2, op0, op1, R, W):
        if s2 is None:
            P.op(eng, lambda e: e.tensor_scalar(out, in0, s1, None, op0), reads=R, writes=W)
        else:
            P.op(eng, lambda e: e.tensor_scalar(out, in0, s1, s2, op0, op1), reads=R, writes=W)

    def stt(eng, out, in0, sc, in1, op0, op1, R, W):
        P.op(eng, lambda e: e.scalar_tensor_tensor(out, in0, sc, in1, op0, op1), reads=R, writes=W)

    def act(out, in_, func, R, W, bias=None, scale=None):
        kw = {}
        if bias is not None:
            kw["bias"] = bias
        if scale is not None:
            kw["scale"] = scale
        P.op("act", lambda e: e.activation(out, in_, func, **kw), reads=R, writes=W)

    def cp(eng, out, in_, R, W):
        if eng == "act":
            P.op("act", lambda e: e.copy(out, in_), reads=R, writes=W)
        else:
            P.op(eng, lambda e: e.tensor_copy(out, in_), reads=R, writes=W)

    def mm(out, lhsT, rhs, start, stop, R, W):
        P.op("pe", lambda e: e.matmul(out, lhsT, rhs, start=start, stop=stop), reads=R, writes=W)

    def memset(eng, ap, val, W):
        P.op(eng, lambda e: e.memset(ap, val), writes=W)

    def scan(out, d0, d1, init, R, W):
        P.op("dve", lambda e: e.tensor_tensor_scan(out, d0, d1, init, ALU.mult, ALU.add), reads=R, writes=W)

    banks = []
    for i in range(8):
        banks.append((es.enter_context(nc.psum_tensor(f"bank{i}", [128, 512], F32)), BDep()))
    bctr = {None: 0, 0: 0, 1: 0}

    def nps(pool=None):
        i = bctr[pool]
        bctr[pool] += 1
        if pool is None:
            return banks[i % 8]
        return banks[pool * 4 + i % 4]

    X = Tl("X", [128, 8, T], F32, nd=8)
    COLS = Tl("COLS", [128, NCOL])
    CST = Tl("CST", [128, NCONST])
    ident = CST.t[:, K_ID:K_ID + 128]
    bones = CST.t[:, K_BO:K_BO + 128]
    m_inc = CST.t[:, K_MINC:K_MINC + 64]
    m_str = CST.t[:, K_MSTR:K_MSTR + 64]
    m_tril = CST.t[:, K_MTRIL:K_MTRIL + 64]
    idp = CST.t[:, K_IDP:K_IDP + 64]
    rst = CST.t[:, K_RST:K_RST + TB]

    def col(name, j=0, n=1):
        return COLS.t[:, CI[name] + j:CI[name] + j + n]

    P.dma("sp", COLS.t[:], cols_d, writes=[COLS.D])
    P.dma("sp", CST.t[:], cst_d, writes=[CST.D])
    for j in range(8):
        P.dma("sp", X.t[:, j, :], xT_d[j], writes=[X.d[j]])

    NSCR = 12
    scr = [Tl(f"scr{i}", [128, TB + 4]) for i in range(NSCR)]
    sctr = [0]

    def tmp():
        s = scr[sctr[0] % NSCR]
        sctr[0] += 1
        return s

    def merge(gens):
        gens = list(gens)
        while gens:
            for g in list(gens):
                try:
                    next(g)
                    yield
                except StopIteration:
                    gens.remove(g)

    def run(g):
        for _ in g:
            pass

    def interleave(gens):
        run(merge(gens))

    CSI = Tl("CSI", [128, 8])
    MOD = Tl("MOD", [128, 48])
    ADA = Tl("ADA", [128, 32])
    DER = Tl("DER", [128, 16])

    act(CSI.t[:], col("c", 0, 8), AF.Silu, [COLS.D], [CSI.D])


    NST = 2
    NWB = 6
    wst = [Tl(f"wst{i}", [128, 8, 128]) for i in range(NST)]
    wbf = [Tl(f"wbf{i}", [128, 8, 128], BF16) for i in range(NWB)]
    wctr = [0]
    sctr2 = [0]

    def wchunk(src_ap, nk=8, cast=True):
        if not cast:
            i = sctr2[0] % NST
            sctr2[0] += 1
            P.dma("sp", wst[i].t[:, 0:nk, :], src_ap, writes=[wst[i].D])
            return wst[i]
        i = wctr[0] % NWB
        wctr[0] += 1
        P.dma("pool", wbf[i].t[:, 0:nk, :], src_ap, writes=[wbf[i].D])
        return wbf[i]

    WOUT = Tl("WOUT", [128, 8, D], BF16)
    HBF = Tl("HBF", [128, 8, TB], BF16)
    YT = Tl("YT", [128, 8, TB], BF16, nd=8)
    YTAs = [Tl(f"YTA{i}", [128, 2, TB], BF16, nd=2) for i in range(2)]
    SLs = [Tl(f"SL{i}", [128, TB], BF16) for i in range(3)]
    RSTD = Tl("RSTD", [128, 512])

    def tap(name, ap, deps):
        if name in tap_d:
            t = P.dma("sp", tap_d[name], ap, reads=deps)
            tap_toks.append(t)
    tap_toks = []

    ONES = Tl("ONES", [128, 128])
    memset("pool", ONES.t[:], 1.0, [ONES.D])
    EPS = Tl("EPS", [128, 8])
    memset("pool", EPS.t[:, 0:1], 1e-6, [EPS.D])
    memset("pool", EPS.t[:, 1:2], 1e-12, [EPS.D])
    memset("pool", EPS.t[:, 2:3], 64e-5, [EPS.D])
    memset("pool", EPS.t[:, 3:4], 1.0, [EPS.D])
    memset("pool", EPS.t[:, 4:5], 0.0, [EPS.D])
    eps6, eps12, epsgn, one_c, zero_c = EPS.t[:, 0:1], EPS.t[:, 1:2], EPS.t[:, 2:3], EPS.t[:, 3:4], EPS.t[:, 4:5]

    def norm_block(t0, ntok, Acol, Bcol, dstbf, dstdep, doff=0, pool=None):
        bk, bd = nps(pool)
        for h0 in range(0, ntok, TB):
            for j in range(8):
                s = tmp()
                act(s.t[:, :TB], X.t[:, j, t0 + h0:t0 + h0 + TB], AF.Square, [X.d[j]], [s.D])
                mm(bk[:, h0:h0 + TB], ONES.t[:], s.t[:, :TB], j == 0, j == 7, [s.D, ONES.D], [bd])
        act(RSTD.t[:, :ntok], bk[:, :ntok], AF.Ln, [bd, EPS.D], [RSTD.D], bias=eps6, scale=1.0 / D)
        act(RSTD.t[:, :ntok], RSTD.t[:, :ntok], AF.Exp, [RSTD.D], [RSTD.D], scale=-0.5)
        for j in range(8):
            for h0 in range(0, ntok, TB):
                s = tmp()
                stt("dve", s.t[:, :TB], X.t[:, j, t0 + h0:t0 + h0 + TB], Acol(j), RSTD.t[:, h0:h0 + TB],
                    ALU.mult, ALU.mult, [X.d[j], RSTD.D, ADA.D], [s.D])
                act(dstbf[:, j, doff + h0:doff + h0 + TB], s.t[:, :TB], AF.Identity, [s.D, MOD.D], [dstdep], bias=Bcol(j))

    class V:
        def __init__(self, ap, nd=1):
            self.t = ap
            self.d = [Dep() for _ in range(nd)]
            self.D = self.d[0]
    NBIG = 9728
    BIG = Tl("BIG", [128, NBIG])
    H2 = V(BIG.t[:, 0:4096].bitcast(BF16).rearrange("p (k t) -> p k t", k=8))
    ACTT = V(BIG.t[:, 4096:9728].bitcast(BF16).rearrange("p (m t) -> p m t", m=11), nd=11)

    scr2 = [V(BIG.t[:, 6060 + i * 260:6060 + (i + 1) * 260]) for i in range(12)]
    SSETS = [scr[0:4] + scr2[0:4], scr[4:8] + scr2[4:8], scr[8:12] + scr2[8:12]]
    PT = [V(BIG.t[:, i * 260:(i + 1) * 260]) for i in range(11)]
    CARRY = Tl("CARRY", [128, 11])
    HST = Tl("HST", [128, 2])
    XAH = Tl("XAH", [128, 2, 3])
    SB = [Tl(f"SB{i}", [128, 64]) for i in range(3)]
    QTs = [Tl(f"QT{i}", [128, TB], BF16) for i in range(3)]
    KTs = [Tl(f"KT{i}", [128, TB], BF16) for i in range(3)]
    QAs = [Tl(f"QA{i}", [128, TB], BF16) for i in range(3)]
    KAs = [Tl(f"KA{i}", [128, TB], BF16) for i in range(3)]
    PVBs = [Tl(f"PVB{i}", [128, TB], BF16) for i in range(3)]
    IDB = Tl("IDB", [128, 128], BF16)
    HBs = [[Tl(f"HB{j}_{i}", [128, 64], BF16) for i in range(5)] for j in range(3)]
    SNs = [Tl(f"SN{i}", [128, 64]) for i in range(3)]
    OBD = [Dep() for _ in range(3)]
    GBs = [Tl(f"GB{i}", [128, 16]) for i in range(3)]
    TOK = [V(BIG.t[:, 4908 + i * 384:4908 + (i + 1) * 384].bitcast(BF16).rearrange("p (c a b) -> p c a b", c=NCH, a=3))
           for i in range(3)]
    ZC = Tl("ZC", [128, 3, 64])
    BH = [Tl(f"BH{i}", [128, TB], BF16) for i in range(3)]
    AH = [Tl(f"AH{i}", [128, TB], BF16) for i in range(3)]
    RH = [Tl(f"RH{i}", [128, TB], BF16) for i in range(3)]
    KH = [Tl(f"KH{i}", [128, TB], BF16) for i in range(3)]
    VB = [Tl(f"VB{i}", [128, TB], BF16) for i in range(3)]
    ZB = Tl("ZB", [128, 3, 64], BF16)
    SGG = V(BIG.t[:, 4396:4652])
    TW = V(BIG.t[:, 4652:4908])
    BON = [V(BIG.t[:, 3628 + i * 256:3628 + (i + 1) * 256]) for i in range(3)]
    YC = V(BIG.t[:, 2860:3628].rearrange("p (a t) -> p a t", a=3))
    GAM = Tl("GAM", [128, 3, NCH])
    MQ = [[Tl(f"MQ{c}_{i}", [128, 3, 64], BF16) for i in range(2)] for c in range(NCH)]
    MP = [[Tl(f"MP{c}_{i}", [128, 3, 64], BF16) for i in range(2)] for c in range(NCH)]
    MT = [Tl(f"MT{i}", [128, 3, 64], BF16) for i in range(NCH)]
    MAK = [Tl(f"MAK{i}", [128, 3, 64], BF16) for i in range(NCH)]
    MRB = [Tl(f"MRB{i}", [128, 3, 64], BF16) for i in range(NCH)]
    MRK = [Tl(f"MRK{i}", [128, 3, 64], BF16) for i in range(NCH)]
    XS = Tl("XS", [128, 3, 64], BF16); US = Tl("US", [128, 3, 64], BF16); ZTMP = Tl("ZTMP", [128, 3, 64])
    RGBD = Tl("RGBD", [128, 2, 128]); IGBD = Tl("IGBD", [128, 2, 128])
    WAUP = Tl("WAUP", [128, 384]); GUP = Tl("GUP", [128, 384])

    mqctr = [0]
    cp("dve", IDB.t[:], ident, [CST.D], [IDB.D])
    identb = IDB.t

    def bc3(ap2):
        return ap2.unsqueeze(1).to_broadcast([128, 3, 64])

    for l in range(n_layers):
        if l == 0:
            memset("dve", DER.t[:, 2:5], 0.0, [DER.D])
        else:
            tt("dve", DER.t[:, 2:5], col("hgrn_lb1", 0, 3), col("hgrn_lb0", 0, 3), ALU.subtract, [COLS.D], [DER.D])
            act(DER.t[:, 2:5], DER.t[:, 2:5], AF.Sigmoid, [DER.D], [DER.D])
        ts("dve", DER.t[:, 5:8], DER.t[:, 2:5], -1.0, 1.0, ALU.mult, ALU.add, [DER.D], [DER.D])
        act(DER.t[:, 0:2], col(f"lam{l}", 0, 2), AF.Exp, [COLS.D], [DER.D], scale=-1.0)
        act(DER.t[:, 0:2], DER.t[:, 0:2], AF.Ln, [DER.D, EPS.D], [DER.D], bias=one_c)
        ts("dve", DER.t[:, 0:2], DER.t[:, 0:2], -8.0, None, ALU.mult, None, [DER.D], [DER.D])
        lbc = lambda hp: DER.t[:, 2 + hp:3 + hp]
        omlc = lambda hp: DER.t[:, 5 + hp:6 + hp]
        nsp8 = lambda cc: DER.t[:, cc:cc + 1]

        P.dma("sp", RGBD.t[:], rgbd_d[l].rearrange("c p n -> p c n"), writes=[RGBD.D])
        P.dma("sp", IGBD.t[:], igbd_d[l].rearrange("c p n -> p c n"), writes=[IGBD.D])
        P.dma("sp", WAUP.t[:], waup_d[l], writes=[WAUP.D])
        P.dma("sp", GUP.t[:], gup_d[l], writes=[GUP.D])

        adv = ada_d[l].rearrange("(k p) m -> p k m", p=128)
        mbk, mbd = nps()
        for j in range(48):
            st = wchunk(adv[:, :, j * 128:(j + 1) * 128], cast=False)
            for k in range(8):
                mm(mbk[:, j:j + 1], st.t[:, k, :], CSI.t[:, k:k + 1], k == 0, k == 7, [st.D, CSI.D], [mbd])
        tt("dve", MOD.t[:], mbk[:, 0:48], col(f"ada_b{l}", 0, 48), ALU.add, [mbd, COLS.D], [MOD.D])
        stt("dve", ADA.t[:, 0:8], MOD.t[:, 8:16], 1.0, col(f"norm1_g{l}", 0, 8), ALU.add, ALU.mult, [MOD.D, COLS.D], [ADA.D])
        stt("dve", ADA.t[:, 8:16], MOD.t[:, 32:40], 1.0, col(f"norm2_g{l}", 0, 8), ALU.add, ALU.mult, [MOD.D, COLS.D], [ADA.D])
        tap(f"mod{l}", MOD.t[:], [MOD.D])

        wov = wout_d[l].rearrange("(k p) n -> p k n", p=128)
        for f in range(8):
            wb = wchunk(wov[:, :, f * 128:(f + 1) * 128])
            cp("act", WOUT.t[:, :, f * 128:(f + 1) * 128], wb.t[:], [wb.D], [WOUT.D])

        wiv = win_d[l].rearrange("(k p) n -> p k n", p=128)

        def project(chunk, dst_ap, dst_deps, evac="act", pool=None):
            wb = wchunk(wiv[:, :, chunk * 128:(chunk + 1) * 128])
            bk, bd = nps(pool)
            for k in range(8):
                mm(bk[:, :TB], wb.t[:, k, :], HBF.t[:, k, :], k == 0, k == 7, [wb.D, HBF.D], [bd])
            cp(evac, dst_ap, bk[:, :TB], [bd], dst_deps)

        memset("dve", HST.t[:], 0.0, [HST.D])
        memset("dve", CARRY.t[:], 0.0, [CARRY.D])
        for hp in range(3):
            memset("dve", SB[hp].t[:], 0.0, [SB[hp].D])
        memset("dve", ZC.t[:], 0.0, [ZC.D])
        memset("dve", ZB.t[:], 0.0, [ZB.D])
        memset("dve", XAH.t[:], 0.0, [XAH.D])

        PGs = [PT[6], PT[7], PT[8]]
        cur = [None] * NCH
        def gen_normA(tb):
            PB = 1
            t0 = tb * TB
            YTA = YTAs[tb % 2]
            norm_block(t0, TB, lambda j: ADA.t[:, j:j + 1], lambda j: MOD.t[:, j:j + 1], HBF.t, HBF.D, pool=PB)
            yield
            def _gen_mixa(cc):
                S = SSETS[cc]
                XA, YA = PT[cc], PT[2 + cc]
                yield
                project(cc, XA.t[:, 3:3 + TB], [XA.D], pool=PB)
                yield
                cp("act", XA.t[:, 0:3], XAH.t[:, cc, :], [XAH.D], [XA.D])
                yield
                cp("act", XAH.t[:, cc, :], XA.t[:, TB:TB + 3], [XA.D], [XAH.D])
                yield
                project(2 + cc, YA.t[:, :TB], [YA.D], evac="dve", pool=PB)
                yield
                u = S[0]
                yield
                cw = lambda j: col(f"conv_w{l}", cc * 4 + j)
                yield
                ts("dve", u.t[:, :TB], XA.t[:, 0:TB], cw(0), col(f"conv_b{l}", cc), ALU.mult, ALU.add, [XA.D, COLS.D], [u.D])
                yield
                for j in range(1, 4):
                    stt("dve", u.t[:, :TB], XA.t[:, j:j + TB], cw(j), u.t[:, :TB], ALU.mult, ALU.add, [XA.D, COLS.D, u.D], [u.D])
                rb, rbd = nps(PB)
                yield
                mm(rb[:, :TB], RGBD.t[:, cc, :], u.t[:, :TB], True, True, [RGBD.D, u.D], [rbd])
                yield
                ib, ibd = nps(PB)
                yield
                mm(ib[:, :TB], IGBD.t[:, cc, :], u.t[:, :TB], True, True, [IGBD.D, u.D], [ibd])
                yield
                r = S[1]; ig = S[2]
                yield
                act(r.t[:, :TB], rb[:, :TB], AF.Sigmoid, [rbd, COLS.D], [r.D], bias=col(f"rg_b{l}", cc))
                yield
                act(ig.t[:, :TB], ib[:, :TB], AF.Sigmoid, [ibd, COLS.D], [ig.D], bias=col(f"ig_b{l}", cc))
                yield
                a = S[3]
                yield
                act(a.t[:, :TB], r.t[:, :TB], AF.Exp, [r.D, DER.D], [a.D], scale=nsp8(cc))
                yield
                m = S[4]
                yield
                act(m.t[:, :TB], a.t[:, :TB], AF.Square, [a.D], [m.D])
                yield
                act(m.t[:, :TB], m.t[:, :TB], AF.Sqrt, [m.D, EPS.D], [m.D], bias=one_c, scale=-1.0)
                yield
                if tb == 0:
                    memset("dve", m.t[:, 0:1], 1.0, [m.D])
                tt("dve", m.t[:, :TB], m.t[:, :TB], ig.t[:, :TB], ALU.mult, [m.D, ig.D], [m.D])
                yield
                tt("dve", m.t[:, :TB], m.t[:, :TB], u.t[:, :TB], ALU.mult, [m.D, u.D], [m.D])
                yield
                h = S[5]
                yield
                scan(h.t[:, :TB], a.t[:, :TB], m.t[:, :TB], HST.t[:, cc:cc + 1], [a.D, m.D, HST.D], [h.D])
                yield
                cp("dve", HST.t[:, cc:cc + 1], h.t[:, TB - 1:TB], [h.D], [HST.D])
                yield
                g1 = S[6]
                yield
                act(g1.t[:, :TB], YA.t[:, :TB], AF.Square, [YA.D], [g1.D])
                yield
                ts("dve", g1.t[:, :TB], g1.t[:, :TB], 0.044715, 1.0, ALU.mult, ALU.add, [g1.D], [g1.D])
                yield
                tt("dve", g1.t[:, :TB], g1.t[:, :TB], YA.t[:, :TB], ALU.mult, [g1.D, YA.D], [g1.D])
                yield
                act(g1.t[:, :TB], g1.t[:, :TB], AF.Sigmoid, [g1.D], [g1.D], scale=1.5957691216057308)
                yield
                tt("dve", g1.t[:, :TB], g1.t[:, :TB], YA.t[:, :TB], ALU.mult, [g1.D, YA.D], [g1.D])
                yield
                tt("dve", h.t[:, :TB], h.t[:, :TB], g1.t[:, :TB], ALU.mult, [h.D, g1.D], [h.D])
                yield
                sq = S[7]
                yield
                act(sq.t[:, :TB], h.t[:, :TB], AF.Square, [h.D], [sq.D])
                yield
                sb_, sbd = nps(PB)
                yield
                mm(sb_[:, :TB], bones, sq.t[:, :TB], True, True, [CST.D, sq.D], [sbd])
                yield
                act(sq.t[:, :TB], sb_[:, :TB], AF.Ln, [sbd, EPS.D], [sq.D], bias=eps6, scale=1.0 / 64)
                yield
                act(sq.t[:, :TB], sq.t[:, :TB], AF.Exp, [sq.D], [sq.D], scale=-0.5)
                yield
                stt("dve", YTA.t[:, cc, :], h.t[:, :TB], col(f"beta{l}", cc), sq.t[:, :TB], ALU.mult, ALU.mult,
                    [h.D, sq.D, COLS.D], [YTA.d[cc]])
                yield
            yield from merge([_gen_mixa(_i) for _i in range(2)])
        def gen_Bprep(tb):
            def _gen_bprep(hp):
                S = SSETS[hp]
                Pq, Pf, Pg = PT[hp], PT[3 + hp], PGs[hp]
                yield
                qa, ka, QT, KT, PVB, GB = QAs[hp], KAs[hp], QTs[hp], KTs[hp], PVBs[hp], GBs[hp]
                yield
                project(4 + hp, Pq.t[:, :TB], [Pq.D])
                yield
                project(7 + hp, Pf.t[:, :TB], [Pf.D], evac="dve")
                yield
                project(10 + hp, PVB.t[:], [PVB.D])
                yield
                project(13 + hp, Pg.t[:, :TB], [Pg.D], evac="dve")
                yield
                act(SLs[hp].t[:], Pg.t[:, :TB], AF.Silu, [Pg.D], [SLs[hp].D])
                yield
                s1 = S[0]; kg = S[1]; bc = S[2]; dd = S[3]; e1 = S[4]
                yield
                act(s1.t[:, :TB], Pf.t[:, :TB], AF.Sigmoid, [Pf.D], [s1.D])
                yield
                act(s1.t[:, :TB], s1.t[:, :TB], AF.Identity, [s1.D, DER.D], [s1.D], bias=lbc(hp), scale=omlc(hp))
                yield
                act(s1.t[:, :TB], s1.t[:, :TB], AF.Ln, [s1.D], [s1.D])
                yield
                act(kg.t[:, :TB], Pf.t[:, :TB], AF.Sigmoid, [Pf.D], [kg.D], scale=-1.0)
                yield
                ts("dve", kg.t[:, :TB], kg.t[:, :TB], omlc(hp), None, ALU.mult, None, [kg.D, DER.D], [kg.D])
                yield
                scan(bc.t[:, :TB], rst, s1.t[:, :TB], zero_c, [CST.D, s1.D, EPS.D], [bc.D])
                yield
                bc3v = bc.t[:, :TB].rearrange("p (c s) -> p c s", s=CH)
                yield
                bmid = bc3v[:, :, 31:32]
                yield
                blast = bc3v[:, :, 63:64]
                yield
                act(e1.t[:, :TB], bc.t[:, :TB], AF.Exp, [bc.D], [e1.D])
                yield
                stt("dve", qa.t[:], Pq.t[:, :TB], 0.125, e1.t[:, :TB], ALU.mult, ALU.mult, [Pq.D, e1.D], [qa.D])
                yield
                ts("dve", e1.t[:, :TB], bc.t[:, :TB], -1.0, 60.0, ALU.mult, ALU.min, [bc.D], [e1.D])
                yield
                act(e1.t[:, :TB], e1.t[:, :TB], AF.Exp, [e1.D], [e1.D])
                yield
                tt("dve", ka.t[:], kg.t[:, :TB], e1.t[:, :TB], ALU.mult, [kg.D, e1.D], [ka.D])
                yield
                tt("dve", dd.t[:, :TB].rearrange("p (c s) -> p c s", s=CH), bc3v, bmid.to_broadcast([128, NCH, CH]),
                   ALU.subtract, [bc.D], [dd.D])
                act(e1.t[:, :TB], dd.t[:, :TB], AF.Exp, [dd.D], [e1.D])
                yield
                stt("dve", QT.t[:], Pq.t[:, :TB], 0.125, e1.t[:, :TB], ALU.mult, ALU.mult, [Pq.D, e1.D], [QT.D])
                yield
                act(e1.t[:, :TB], dd.t[:, :TB], AF.Exp, [dd.D], [e1.D], scale=-1.0)
                yield
                tt("dve", KT.t[:], kg.t[:, :TB], e1.t[:, :TB], ALU.mult, [kg.D, e1.D], [KT.D])
                yield
                gbv = GB.t[:, 0:12].rearrange("p (a c) -> p a c", c=NCH)
                yield
                act(gbv[:, 0, :].unsqueeze(2), bmid, AF.Exp, [bc.D], [GB.D])
                yield
                act(gbv[:, 1, :].unsqueeze(2), blast, AF.Exp, [bc.D], [GB.D])
                yield
                tt("dve", gbv[:, 2, :].unsqueeze(2), blast, bmid, ALU.subtract, [bc.D], [GB.D])
                yield
                act(gbv[:, 2, :], gbv[:, 2, :], AF.Exp, [GB.D], [GB.D])
                yield
                yield
            yield from merge([_gen_bprep(_i) for _i in range(3)])
        def gen_Bloop(tb):
            PB = 0
            for c in range(NCH):
                cs_ = slice(c * CH, (c + 1) * CH)
                ca_ = slice(c * CH, c * CH + 32)
                cb_ = slice(c * CH + 32, (c + 1) * CH)
                abs_ = []
                for hp in range(3):
                    qa, ka, QT, KT, PVB = QAs[hp], KAs[hp], QTs[hp], KTs[hp], PVBs[hp]
                    ab, abd = nps(PB)
                    abs_.append((ab, abd))
                    for hh in range(2):
                        ps_ = slice(hh * 64, hh * 64 + 64)
                        mm(ab[ps_, 0:32], ka.t[ps_, cs_], qa.t[ps_, ca_], True, True, [ka.D, qa.D], [abd])
                        yield
                        mm(ab[ps_, 32:64], KT.t[ps_, cs_], QT.t[ps_, cb_], True, True, [KT.D, QT.D], [abd])
                        yield
                        mm(ab[ps_, 64:128], PVB.t[ps_, cs_], identb[ps_, ps_], True, True, [PVB.D, IDB.D], [abd])
                        yield
                        mm(ab[ps_, 128:192], KT.t[ps_, cs_], identb[ps_, ps_], True, True, [KT.D, IDB.D], [abd])
                        yield
                for hp in range(3):
                    ATS, VTK, KTK, STL, SBb = HBs[hp]
                    ab, abd = abs_[hp]
                    tt("dve", ATS.t[:], ab[:, 0:64], m_inc, ALU.mult, [abd, CST.D], [ATS.D])
                    yield
                    cp("act", VTK.t[:], ab[:, 64:128], [abd], [VTK.D])
                    yield
                    cp("act", KTK.t[:], ab[:, 128:192], [abd], [KTK.D])
                    yield
                    ts("dve", STL.t[:], SB[hp].t[:], GBs[hp].t[:, c:c + 1], None, ALU.mult, None, [SB[hp].D, GBs[hp].D], [STL.D])
                    yield
                    cp("act", SBb.t[:], SB[hp].t[:], [SB[hp].D], [SBb.D])
                    yield
                obs_ = []
                for hp in range(3):
                    ATS, VTK, KTK, STL, SBb = HBs[hp]
                    qa, QT = QAs[hp], QTs[hp]
                    ob, obd = nps(PB)
                    obs_.append((ob, obd))
                    for hh in range(2):
                        ps_ = slice(hh * 64, hh * 64 + 64)
                        mm(ob[ps_, 0:64], VTK.t[ps_, :], ATS.t[ps_, :], True, False, [VTK.D, ATS.D], [obd])
                        yield
                        mm(ob[ps_, 0:32], SBb.t[ps_, :], qa.t[ps_, ca_], False, False, [SBb.D, qa.D], [obd])
                        yield
                        mm(ob[ps_, 32:64], STL.t[ps_, :], QT.t[ps_, cb_], False, True, [STL.D, QT.D], [obd])
                        yield
                        mm(ob[ps_, 64:128], KTK.t[ps_, :], VTK.t[ps_, :], True, True, [KTK.D, VTK.D], [obd])
                        yield
                for hp in range(3):
                    ob, obd = obs_[hp]
                    GB = GBs[hp]
                    cp("act", YC.t[:, hp, cs_], ob[:, 0:64], [obd], [OBD[hp]])
                    yield
                    ts("dve", SNs[hp].t[:], ob[:, 64:128], GB.t[:, 8 + c:9 + c], None, ALU.mult, None, [obd, GB.D], [SNs[hp].D])
                    yield
                    stt("dve", SB[hp].t[:], SB[hp].t[:], GB.t[:, 4 + c:5 + c], SNs[hp].t[:], ALU.mult, ALU.add,
                        [SB[hp].D, GB.D, SNs[hp].D], [SB[hp].D])
        def gen_Bepi(tb):
            def _gen_bepi(hp):
                S = SSETS[hp]
                Pg = PGs[hp]
                yield
                OBt = YC.t[:, hp, :]
                yield
                sq = S[0]
                yield
                act(sq.t[:, :TB], OBt, AF.Square, [OBD[hp]], [sq.D])
                yield
                sb_, sbd = nps()
                yield
                mm(sb_[:, :TB], bones, sq.t[:, :TB], True, True, [CST.D, sq.D], [sbd])
                yield
                act(sq.t[:, :TB], sb_[:, :TB], AF.Ln, [sbd, EPS.D], [sq.D], bias=eps6, scale=1.0 / 64)
                yield
                act(sq.t[:, :TB], sq.t[:, :TB], AF.Exp, [sq.D], [sq.D], scale=-0.5)
                yield
                sl = S[1]
                yield
                yield
                tt("dve", sq.t[:, :TB], sq.t[:, :TB], OBt, ALU.mult, [sq.D, OBD[hp]], [sq.D])
                yield
                stt("dve", YT.t[:, 2 + hp, :], sq.t[:, :TB], col(f"beta{l}", 2 + hp), SLs[hp].t[:], ALU.mult, ALU.mult,
                    [sq.D, SLs[hp].D, COLS.D], [YT.d[2 + hp]])

                yield
            yield from merge([_gen_bepi(_i) for _i in range(3)])
        def gen_Cfront(tb):
            PB = 1
            for i in range(11):
                Xt = PT[i]
                project(16 + i, Xt.t[:, 1:1 + TB], [Xt.D], evac=("act" if i % 2 == 0 else "dve"), pool=PB)
                yield
                cp("act", Xt.t[:, 0:1], CARRY.t[:, i:i + 1], [CARRY.D], [Xt.D])
                yield
                cp("act", CARRY.t[:, i:i + 1], Xt.t[:, TB:TB + 1], [Xt.D], [CARRY.D])
                yield
                d_ = tmp()
                tt("dve", d_.t[:, :TB], Xt.t[:, 0:TB], Xt.t[:, 1:1 + TB], ALU.subtract, [Xt.D], [d_.D])
                yield
                stt("dve", Xt.t[:, 1:1 + TB], d_.t[:, :TB], col(f"mu{l}", i), Xt.t[:, 1:1 + TB], ALU.mult, ALU.add,
                    [d_.D, Xt.D, COLS.D], [Xt.D])
            sR = [PT[i] for i in range(0, 3)]
            sK = [PT[i] for i in range(3, 6)]
            sV = [PT[i] for i in range(6, 9)]
            sWA, sG = PT[9], PT[10]
            V1 = lambda t_: t_.t[:, 1:1 + TB]
            tw = TW
            act(tw.t[0:64, :TB], sWA.t[0:64, 1:1 + TB], AF.Tanh, [sWA.D], [tw.D])
            yield
            act(SGG.t[:], V1(sG), AF.Sigmoid, [sG.D], [SGG.D])
            yield
            def _gen_cprep(hp):
                S = SSETS[hp]
                R_, K_, V_ = sR[hp], sK[hp], sV[hp]
                yield
                kk = S[0]; sq = S[1]; k2 = S[2]; e_ = S[3]; cum = S[4]; ex = S[5]; lw = S[6]; aa = S[7]
                yield
                hc = slice(hp * 128, (hp + 1) * 128)
                yield
                zb, zbd = nps(PB)
                yield
                mm(zb[:, :TB], WAUP.t[0:64, hc], tw.t[0:64, :TB], True, True, [WAUP.D, tw.D], [zbd])
                yield
                act(lw.t[:, :TB], zb[:, :TB], AF.Sigmoid, [zbd, COLS.D], [lw.D], bias=col(f"w0{l}", hp))
                yield
                ts("dve", lw.t[:, :TB], lw.t[:, :TB], -0.6065306597126334, None, ALU.mult, None, [lw.D], [lw.D])
                yield
                ab_, abd_ = nps(PB)
                yield
                mm(ab_[:, :TB], WAUP.t[64:128, hc], sWA.t[64:128, 1:1 + TB], True, True, [WAUP.D, sWA.D], [abd_])
                yield
                act(aa.t[:, :TB], ab_[:, :TB], AF.Sigmoid, [abd_, COLS.D], [aa.D], bias=col(f"a0{l}", hp))
                yield
                ts("dve", kk.t[:, :TB], V1(K_), col(f"k_k{l}", hp), None, ALU.mult, None, [K_.D, COLS.D], [kk.D])
                yield
                act(sq.t[:, :TB], kk.t[:, :TB], AF.Square, [kk.D], [sq.D])
                yield
                nb_, nbd_ = nps(PB)
                yield
                mm(nb_[:, :TB], bones, sq.t[:, :TB], True, True, [CST.D, sq.D], [nbd_])
                yield
                act(sq.t[:, :TB], nb_[:, :TB], AF.Ln, [nbd_, EPS.D], [sq.D], bias=eps12)
                yield
                act(sq.t[:, :TB], sq.t[:, :TB], AF.Exp, [sq.D], [sq.D], scale=-0.5)
                yield
                tt("dve", kk.t[:, :TB], kk.t[:, :TB], sq.t[:, :TB], ALU.mult, [kk.D, sq.D], [kk.D])
                yield
                ts("dve", k2.t[:, :TB], aa.t[:, :TB], -1.0, col(f"k_a{l}", hp), ALU.add, ALU.mult, [aa.D, COLS.D], [k2.D])
                yield
                stt("dve", k2.t[:, :TB], k2.t[:, :TB], 1.0, V1(K_), ALU.add, ALU.mult, [k2.D, K_.D], [k2.D])
                yield
                tt("dve", e_.t[:, :TB], V1(R_), k2.t[:, :TB], ALU.mult, [R_.D, k2.D], [e_.D])
                yield
                ts("dve", e_.t[:, :TB], e_.t[:, :TB], col(f"r_k{l}", hp), None, ALU.mult, None, [e_.D, COLS.D], [e_.D])
                yield
                bb_, bbd_ = nps(PB)
                yield
                mm(bb_[:, :TB], bones, e_.t[:, :TB], True, True, [CST.D, e_.D], [bbd_])
                yield
                tt("dve", BON[hp].t[:], bb_[:, :TB], V1(V_), ALU.mult, [bbd_, V_.D], [BON[hp].D])
                yield
                scan(cum.t[:, :TB], rst, lw.t[:, :TB], zero_c, [CST.D, lw.D, EPS.D], [cum.D])
                yield
                act(ex.t[:, :TB], cum.t[:, :TB], AF.Exp, [cum.D], [ex.D])
                yield
                cp("act", GAM.t[:, hp, :].unsqueeze(2), ex.t[:, :TB].rearrange("p (c s) -> p c s", s=CH)[:, :, 63:64], [ex.D], [GAM.D])
                yield
                tt("dve", RH[hp].t[:], V1(R_), ex.t[:, :TB], ALU.mult, [R_.D, ex.D], [RH[hp].D])
                yield
                cp("act", VB[hp].t[:], V1(V_), [V_.D], [VB[hp].D])
                yield
                act(ex.t[:, :TB], cum.t[:, :TB], AF.Exp, [cum.D], [ex.D], scale=-1.0)
                yield
                tt("dve", KH[hp].t[:], k2.t[:, :TB], ex.t[:, :TB], ALU.mult, [k2.D, ex.D], [KH[hp].D])
                yield
                tt("dve", e_.t[:, :TB], kk.t[:, :TB], aa.t[:, :TB], ALU.mult, [kk.D, aa.D], [e_.D])
                yield
                tt("dve", BH[hp].t[:], e_.t[:, :TB], ex.t[:, :TB], ALU.mult, [e_.D, ex.D], [BH[hp].D])
                yield
                tt("dve", cum.t[:, :TB], cum.t[:, :TB], lw.t[:, :TB], ALU.subtract, [cum.D, lw.D], [cum.D])
                yield
                act(ex.t[:, :TB], cum.t[:, :TB], AF.Exp, [cum.D], [ex.D])
                yield
                stt("dve", AH[hp].t[:], kk.t[:, :TB], -1.0, ex.t[:, :TB], ALU.mult, ALU.mult, [kk.D, ex.D], [AH[hp].D])
                yield
                yield
            yield from merge([_gen_cprep(_i) for _i in range(3)])
        def gen_Cmid(tb):
            for c in range(NCH):
                cs1 = slice(1 + c * CH, 1 + (c + 1) * CH)
                cs_ = slice(c * CH, (c + 1) * CH)
                srcs = [(VB, False), (BH, False), (KH, False)]
                for qi, (src, halo) in enumerate(srcs):
                    tb_, tbd_ = nps()
                    for hp in range(3):
                        for hh in range(2):
                            ps_ = slice(hh * 64, hh * 64 + 64)
                            sap = src[hp].t[ps_, cs1] if halo else src[hp].t[ps_, cs_]
                            mm(tb_[ps_, hp * 64:(hp + 1) * 64], sap, identb[ps_, ps_], True, True, [src[hp].D, IDB.D], [tbd_])
                            yield
                    cp("act" if qi != 1 else "dve", TOK[qi].t[:, c, :, :], tb_[:, 0:192].rearrange("p (a b) -> p a b", b=64),
                       [tbd_], [TOK[qi].D])
            for c in range(NCH):
                cs_ = slice(c * CH, (c + 1) * CH)
                q0 = MQ[c][0]; p0 = MP[c][0]
                specs = [
                    (q0, lambda hp, ps_: BH[hp].t[ps_, cs_], lambda hp, ps_: AH[hp].t[ps_, cs_], m_str, lambda hp: [BH[hp].D, AH[hp].D]),
                    (p0, lambda hp, ps_: AH[hp].t[ps_, cs_], lambda hp, ps_: BH[hp].t[ps_, cs_], m_tril, lambda hp: [BH[hp].D, AH[hp].D]),
                    (MAK[c], lambda hp, ps_: KH[hp].t[ps_, cs_], lambda hp, ps_: AH[hp].t[ps_, cs_], m_str, lambda hp: [KH[hp].D, AH[hp].D]),
                    (MRB[c], lambda hp, ps_: BH[hp].t[ps_, cs_], lambda hp, ps_: RH[hp].t[ps_, cs_], m_inc, lambda hp: [BH[hp].D, RH[hp].D]),
                    (MRK[c], lambda hp, ps_: KH[hp].t[ps_, cs_], lambda hp, ps_: RH[hp].t[ps_, cs_], m_inc, lambda hp: [KH[hp].D, RH[hp].D]),
                ]
                for si, (dst, lf, rf, msk, dps) in enumerate(specs):
                    b_, bd_ = nps()
                    for hp in range(3):
                        for hh in range(2):
                            ps_ = slice(hh * 64, hh * 64 + 64)
                            mm(b_[ps_, hp * 64:(hp + 1) * 64], lf(hp, ps_), rf(hp, ps_), True, True, dps(hp), [bd_])
                            yield
                    tt("dve", dst.t[:], b_[:, 0:192].rearrange("p (a b) -> p a b", b=64), bc3(msk), ALU.mult, [bd_, CST.D], [dst.D])
                    yield
                tt("dve", MT[c].t[:], q0.t[:], bc3(idp), ALU.add, [q0.D, CST.D], [MT[c].D])
                yield
                cur[c] = (q0, p0)
        def gen_Cds(tb):
            PB = 0
            for lev in range(1, 6):
                pbs = []
                for c in range(NCH):
                    qc, pc = cur[c]
                    pb_, pbd_ = nps(PB)
                    pbs.append((pb_, pbd_))
                    for hp in range(3):
                        for hh in range(2):
                            ps_ = slice(hh * 64, hh * 64 + 64)
                            mm(pb_[ps_, hp * 64:(hp + 1) * 64], qc.t[ps_, hp, :], pc.t[ps_, hp, :], True, True, [qc.D, pc.D], [pbd_])
                            yield
                            if lev < 5:
                                mm(pb_[ps_, 192 + hp * 64:192 + (hp + 1) * 64], pc.t[ps_, hp, :], qc.t[ps_, hp, :], True, True,
                                   [qc.D, pc.D], [pbd_])
                for c in range(NCH):
                    pb_, pbd_ = pbs[c]
                    qn = MQ[c][lev % 2]; pn = MP[c][lev % 2]
                    cp("act", pn.t[:], pb_[:, 0:192].rearrange("p (a b) -> p a b", b=64), [pbd_], [pn.D])
                    yield
                    if lev < 5:
                        cp("dve", qn.t[:], pb_[:, 192:384].rearrange("p (a b) -> p a b", b=64), [pbd_], [qn.D])
                        yield
                    cur[c] = (qn, pn)
                tbs = []
                for c in range(NCH):
                    qn, pn = cur[c]
                    tb2, tbd2 = nps(PB)
                    tbs.append((tb2, tbd2))
                    for hp in range(3):
                        for hh in range(2):
                            ps_ = slice(hh * 64, hh * 64 + 64)
                            mm(tb2[ps_, hp * 64:(hp + 1) * 64], pn.t[ps_, hp, :], MT[c].t[ps_, hp, :], True, True, [pn.D, MT[c].D], [tbd2])
                            yield
                for c in range(NCH):
                    tb2, tbd2 = tbs[c]
                    tt("dve", MT[c].t[:], MT[c].t[:], tb2[:, 0:192].rearrange("p (a b) -> p a b", b=64), ALU.add, [MT[c].D, tbd2], [MT[c].D])
                    yield
            for c in range(NCH):
                cs1 = slice(1 + c * CH, 1 + (c + 1) * CH)
                cs_ = slice(c * CH, (c + 1) * CH)
                xb, xbd = nps(PB)
                for hp in range(3):
                    for hh in range(2):
                        ps_ = slice(hh * 64, hh * 64 + 64)
                        o_ = xb[ps_, hp * 64:(hp + 1) * 64]
                        mm(o_, AH[hp].t[ps_, cs_], ZB.t[ps_, hp, :], True, False, [AH[hp].D, ZB.D], [xbd])
                        yield
                        mm(o_, MAK[c].t[ps_, hp, :], TOK[0].t[ps_, c, hp, :], False, True, [MAK[c].D, TOK[0].D], [xbd])
                        yield
                cp("act", XS.t[:], xb[:, 0:192].rearrange("p (a b) -> p a b", b=64), [xbd], [XS.D])
                yield
                ub, ubd = nps(PB)
                for hp in range(3):
                    for hh in range(2):
                        ps_ = slice(hh * 64, hh * 64 + 64)
                        mm(ub[ps_, hp * 64:(hp + 1) * 64], MT[c].t[ps_, hp, :], XS.t[ps_, hp, :], True, True, [MT[c].D, XS.D], [ubd])
                        yield
                cp("dve", US.t[:], ub[:, 0:192].rearrange("p (a b) -> p a b", b=64), [ubd], [US.D])
                yield
                yb, ybd = nps(PB)
                for hp in range(3):
                    for hh in range(2):
                        ps_ = slice(hh * 64, hh * 64 + 64)
                        o_ = yb[ps_, hp * 64:(hp + 1) * 64]
                        mm(o_, ZB.t[ps_, hp, :], RH[hp].t[ps_, cs_], True, False, [ZB.D, RH[hp].D], [ybd])
                        yield
                        mm(o_, US.t[ps_, hp, :], MRB[c].t[ps_, hp, :], False, False, [US.D, MRB[c].D], [ybd])
                        yield
                        mm(o_, TOK[0].t[ps_, c, hp, :], MRK[c].t[ps_, hp, :], False, True, [TOK[0].D, MRK[c].D], [ybd])
                        yield
                        o2 = yb[ps_, 192 + hp * 64:192 + (hp + 1) * 64]
                        mm(o2, TOK[1].t[ps_, c, hp, :], US.t[ps_, hp, :], True, False, [TOK[1].D, US.D], [ybd])
                        yield
                        mm(o2, TOK[2].t[ps_, c, hp, :], TOK[0].t[ps_, c, hp, :], False, True, [TOK[2].D, TOK[0].D], [ybd])
                        yield
                cp("act", YC.t[:, :, cs_], yb[:, 0:192].rearrange("p (a b) -> p a b", b=64), [ybd], [YC.D] + OBD)
                yield
                tt("dve", ZTMP.t[:], yb[:, 192:384].rearrange("p (a b) -> p a b", b=64), ZC.t[:], ALU.add, [ybd, ZC.D], [ZTMP.D])
                yield
                tt("dve", ZB.t[:], ZTMP.t[:], GAM.t[:, :, c:c + 1].to_broadcast([128, 3, 64]), ALU.mult, [ZTMP.D, GAM.D], [ZB.D])
                yield
                tt("dve", ZC.t[:], ZTMP.t[:], GAM.t[:, :, c:c + 1].to_broadcast([128, 3, 64]), ALU.mult, [ZTMP.D, GAM.D], [ZC.D])
                yield
        def gen_Cgn(tb):
            def _gen_cgn(hp):
                S = SSETS[hp]
                mb_, mbd_ = nps()
                yield
                mm(mb_[:, :TB], bones, YC.t[:, hp, :], True, True, [CST.D, YC.D, OBD[hp]], [mbd_])
                yield
                yc = S[0]; sq = S[1]
                yield
                stt("dve", yc.t[:, :TB], mb_[:, :TB], -1.0 / 64, YC.t[:, hp, :], ALU.mult, ALU.add, [mbd_, YC.D, OBD[hp]], [yc.D])
                yield
                act(sq.t[:, :TB], yc.t[:, :TB], AF.Square, [yc.D], [sq.D])
                yield
                vb_, vbd_ = nps()
                yield
                mm(vb_[:, :TB], bones, sq.t[:, :TB], True, True, [CST.D, sq.D], [vbd_])
                yield
                act(sq.t[:, :TB], vb_[:, :TB], AF.Ln, [vbd_, EPS.D], [sq.D], bias=epsgn, scale=1.0 / 64)
                yield
                act(sq.t[:, :TB], sq.t[:, :TB], AF.Exp, [sq.D], [sq.D], scale=-0.5)
                yield
                tt("dve", yc.t[:, :TB], yc.t[:, :TB], sq.t[:, :TB], ALU.mult, [yc.D, sq.D], [yc.D])
                yield
                act(yc.t[:, :TB], yc.t[:, :TB], AF.Identity, [yc.D, COLS.D], [yc.D], bias=col(f"lnx_b{l}", hp), scale=col(f"lnx_w{l}", hp))
                yield
                tt("dve", yc.t[:, :TB], yc.t[:, :TB], BON[hp].t[:], ALU.add, [yc.D, BON[hp].D], [yc.D])
                yield
                gb_, gbd_ = nps()
                yield
                mm(gb_[:, :TB], GUP.t[:, hp * 128:(hp + 1) * 128], SGG.t[:], True, True, [GUP.D, SGG.D], [gbd_])
                yield
                stt("dve", YT.t[:, 5 + hp, :], yc.t[:, :TB], col(f"beta{l}", 5 + hp), gb_[:, :TB], ALU.mult, ALU.mult,
                    [yc.D, gbd_, COLS.D], [YT.d[5 + hp]])
                yield
            yield from merge([_gen_cgn(_i) for _i in range(3)])
        def do_wout(tb):
            t0 = tb * TB
            YTA = YTAs[tb % 2]
            if tb == 0 and f"y{l}" in tap_d:
                ytmp = tmp()
                for j in range(8):
                    ysrc, ydep = (YTA.t[:, j, :], YTA.d[j]) if j < 2 else (YT.t[:, j, :], YT.d[j])
                    cp("dve", ytmp.t[:, :TB], ysrc, [ydep], [ytmp.D])
                    t = P.dma("sp", tap_d[f"y{l}"][j], ytmp.t[:, :TB], reads=[ytmp.D])
                    tap_toks.append(t)

            for f in range(8):
                bk, bd = nps()
                for k in range(8):
                    ysrc, ydep = (YTA.t[:, k, :], YTA.d[k]) if k < 2 else (YT.t[:, k, :], YT.d[k])
                    mm(bk[:, :TB], WOUT.t[:, k, f * 128:(f + 1) * 128], ysrc, k == 0, k == 7, [WOUT.D, ydep], [bd])
                stt("dve", X.t[:, f, t0:t0 + TB], bk[:, :TB], MOD.t[:, 16 + f:17 + f], X.t[:, f, t0:t0 + TB], ALU.mult, ALU.add,
                    [bd, MOD.D, X.d[f]], [X.d[f]])

        for tb in range(NTB):
            if tb == 0:
                run(gen_normA(0))
            run(gen_Bprep(tb))
            run(merge([gen_Bloop(tb), gen_Cfront(tb)]))
            run(gen_Bepi(tb))
            run(gen_Cmid(tb))
            run(merge([gen_Cds(tb)] + ([gen_normA(tb + 1)] if tb + 1 < NTB else [])))
            run(gen_Cgn(tb))
            do_wout(tb)
        if f"x1_{l}" in tap_d:
            for j in range(8):
                for q in range(0, T, 512):
                    tap_toks.append(P.dma("sp", tap_d[f"x1_{l}"][j, :, q:q + 512], X.t[:, j, q:q + 512], reads=[X.d[j]]))

        if do_ffn:
            wgv = wg_d[l].rearrange("(k p) n -> p k n", p=128)
            wuv = wu_d[l].rearrange("(k p) n -> p k n", p=128)
            wdv = wd_d[l].rearrange("(m p) n -> p m n", p=128)
            P.barrier()
            for fb in range(NTBF):
                t0 = fb * TBF
                for th in range(TBF // 512):
                    norm_block(t0 + th * 512, 512, lambda j: ADA.t[:, 8 + j:9 + j], lambda j: MOD.t[:, 24 + j:25 + j], H2.t, H2.D, doff=th * 512)
                for half in range(2):
                    for mi in range(11):
                        m = half * 11 + mi
                        wgb = wchunk(wgv[:, :, m * 128:(m + 1) * 128])
                        wub = wchunk(wuv[:, :, m * 128:(m + 1) * 128])
                        for th in range(TBF // 512):
                            tsl = slice(th * 512, (th + 1) * 512)
                            gk, gd = nps()
                            for k in range(8):
                                mm(gk[:, :], wgb.t[:, k, :], H2.t[:, k, tsl], k == 0, k == 7, [wgb.D, H2.D], [gd])
                            uk, ud = nps()
                            for k in range(8):
                                mm(uk[:, :], wub.t[:, k, :], H2.t[:, k, tsl], k == 0, k == 7, [wub.D, H2.D], [ud])
                            for h0 in range(0, 512, TB):
                                sg = tmp()
                                act(sg.t[:, :TB], gk[:, h0:h0 + TB], AF.Silu, [gd], [sg.D])
                                tt("dve", ACTT.t[:, mi, th * 512 + h0:th * 512 + h0 + TB], sg.t[:, :TB], uk[:, h0:h0 + TB], ALU.mult,
                                   [sg.D, ud], [ACTT.d[mi]])
                    for f in range(8):
                        w1 = wchunk(wdv[:, half * 11:half * 11 + 8, f * 128:(f + 1) * 128])
                        w2 = wchunk(wdv[:, half * 11 + 8:half * 11 + 11, f * 128:(f + 1) * 128], nk=3)
                        for th in range(TBF // 512):
                            tsl = slice(th * 512, (th + 1) * 512)
                            bk, bd = nps()
                            for mi in range(11):
                                wt = w1.t[:, mi, :] if mi < 8 else w2.t[:, mi - 8, :]
                                wd_ = w1.D if mi < 8 else w2.D
                                mm(bk[:, :], wt, ACTT.t[:, mi, tsl], mi == 0, mi == 10, [wd_, ACTT.d[mi]], [bd])
                            xs_ = slice(t0 + th * 512, t0 + (th + 1) * 512)
                            stt("dve", X.t[:, f, xs_], bk[:, :], MOD.t[:, 40 + f:41 + f], X.t[:, f, xs_], ALU.mult, ALU.add,
                                [bd, MOD.D, X.d[f]], [X.d[f]])
            P.barrier()
        if f"x2_{l}" in tap_d:
            for j in range(8):
                for q in range(0, T, 512):
                    tap_toks.append(P.dma("sp", tap_d[f"x2_{l}"][j, :, q:q + 512], X.t[:, j, q:q + 512], reads=[X.d[j]]))

    out_toks = []
    for fb in range(T // 512):
        t0 = fb * 512
        bk, bd = nps()
        for h0 in range(0, 512, TB):
            for j in range(8):
                s = tmp()
                act(s.t[:, :TB], X.t[:, j, t0 + h0:t0 + h0 + TB], AF.Square, [X.d[j]], [s.D])
                mm(bk[:, h0:h0 + TB], ONES.t[:], s.t[:, :TB], j == 0, j == 7, [s.D, ONES.D], [bd])
        act(RSTD.t[:, :512], bk[:, :512], AF.Ln, [bd, EPS.D], [RSTD.D], bias=eps6, scale=1.0 / D)
        act(RSTD.t[:, :512], RSTD.t[:, :512], AF.Exp, [RSTD.D], [RSTD.D], scale=-0.5)
        for j in range(8):
            for h0 in range(0, 512, TB):
                o = tmp()
                stt("dve", o.t[:, :TB], X.t[:, j, t0 + h0:t0 + h0 + TB], col("final_g", j), RSTD.t[:, h0:h0 + TB], ALU.mult, ALU.mult,
                    [X.d[j], RSTD.D, COLS.D], [o.D])
                out_toks.append(P.dma("sp", outT_d[j, :, t0 + h0:t0 + h0 + TB], o.t[:, :TB], reads=[o.D]))
    for t in out_toks + tap_toks:
        P.wait_tok("sp", t)
    build_program.last_counts = (dict(P.cnt), dict(P.dcnt))
    P.emit()
    P.close()
    es.close()
    return nc


def make_in_maps(inp):
    f = lambda a: np.ascontiguousarray(np.asarray(a, np.float32))
    cst = make_consts()
    rgbd = blockdiag(inp["rg_w"])
    igbd = blockdiag(inp["ig_w"])
    wa_up = f(np.concatenate([np.asarray(inp["rwkv_w_up"]), np.asarray(inp["rwkv_a_up"])], axis=1))
    shared = {
        "cst": cst, "ada_w": f(inp["ada_w"]), "w_in": f(inp["w_in"]), "rgbd": rgbd, "igbd": igbd,
        "wa_up": wa_up, "g_up": f(inp["rwkv_g_up"]), "w_out": f(inp["w_out"]),
        "wg": f(inp["ffn_w_gate"]), "wu": f(inp["ffn_w_up"]), "wd": f(inp["ffn_w_down"]),
    }
    x = np.asarray(inp["x"], np.float32)
    maps = []
    for b in range(NB):
        m = dict(shared)
        m["xT"] = np.ascontiguousarray(x[b].T.reshape(8, 128, T))
        m["cols"] = pack_cols(inp, b)
        maps.append(m)
    return maps


def kernel(**inputs):
    nc = build_program()
    maps = make_in_maps(inputs)
    res = run_bass_kernel_spmd(nc, maps, core_ids=list(range(NB)))
    out = np.empty((NB, T, D), np.float32)
    for b in range(NB):
        out[b] = res.results[b]["outT"].reshape(D, T).T
    return out
```

```python
import math
STOPB = 99
from contextlib import ExitStack
import numpy as np
import concourse.bass as bass
import concourse.mybir as mybir
from concourse.bass_utils import run_bass_kernel_spmd

F32 = mybir.dt.float32
BF16 = mybir.dt.bfloat16
ALU = mybir.AluOpType
AF = mybir.ActivationFunctionType

L = 2
D = 1024
T = 2048
NB = 8
PW = 3456
DFF = 2816
TB = 256
NTB = T // TB
CH = 64
NCH = TB // CH
TBF = 1024
NTBF = T // TBF
ENGS = ("pe", "act", "dve", "pool", "sp")
NDMA = 8


class Dep:
    __slots__ = ("w", "r")

    def __init__(self):
        self.w = None
        self.r = {}


class BDep(Dep):
    __slots__ = ()


class Prog:
    def __init__(self, nc):
        self.nc = nc
        self.q = {e: [] for e in ENGS}
        self.cnt = {e: 0 for e in ENGS}
        self.dcnt = {e: 0 for e in ENGS}
        self.seen = {e: {} for e in ENGS}
        self.sems = {}
        self.ctx = []
        for e in ENGS:
            self._mk(("c", e))
            for i in range(NDMA):
                self._mk(("d", e, i))

    def _mk(self, key):
        cm = self.nc.semaphore("s_" + "_".join(str(k) for k in key))
        self.sems[key] = cm.__enter__()
        self.ctx.append(cm)

    def _need(self, eng, reads, writes):
        needs = {}

        def add(k, v):
            if needs.get(k, 0) < v:
                needs[k] = v
        for d in reads:
            if d.w is not None:
                add(*d.w)
        for d in writes:
            if d.w is not None:
                add(*d.w)
            for k, v in d.r.items():
                add(k, v)
        for k, v in needs.items():
            if k == ("c", "pe") and eng == "pe":
                continue
            if self.seen[eng].get(k, 0) >= v:
                continue
            self.seen[eng][k] = v
            self.q[eng].append(("wait", k, v))

    def _mark(self, tok, reads, writes):
        k, v = tok
        for d in reads:
            if d.r.get(k, 0) < v:
                d.r[k] = v
        for d in writes:
            d.w = tok
            d.r = {}

    def op(self, eng, fn, reads=(), writes=()):
        if any(isinstance(d, BDep) for d in reads):
            writes = list(writes) + [d for d in reads if isinstance(d, BDep)]
            reads = [d for d in reads if not isinstance(d, BDep)]
        self._need(eng, reads, writes)
        self.cnt[eng] += 1
        tok = (("c", eng), self.cnt[eng])
        self.q[eng].append(("op", fn, tok[0], 1))
        self._mark(tok, reads, writes)
        return tok

    def dma(self, eng, out, in_, reads=(), writes=()):
        self._need(eng, reads, writes)
        i = self.dcnt[eng]
        self.dcnt[eng] += 1
        slot, rnd = i % NDMA, i // NDMA
        key = ("d", eng, slot)
        if rnd > 0 and self.seen[eng].get(key, 0) < 16 * rnd:
            self.seen[eng][key] = 16 * rnd
            self.q[eng].append(("wait", key, 16 * rnd))
        tok = (key, 16 * (rnd + 1))
        self.q[eng].append(("op", lambda e: e.dma_start(out=out, in_=in_), key, 16))
        self._mark(tok, reads, writes)
        return tok

    def barrier(self):
        toks = [(("c", e), self.cnt[e]) for e in ENGS if self.cnt[e] > 0]
        for e in ENGS:
            for sl in range(NDMA):
                if self.dcnt[e] > sl:
                    toks.append((("d", e, sl), 16 * ((self.dcnt[e] - sl + NDMA - 1) // NDMA)))
        for e in ENGS:
            for k, v in toks:
                if k == ("c", "pe") and e == "pe":
                    continue
                self.wait_tok(e, (k, v))

    def wait_tok(self, eng, tok):
        k, v = tok
        if self.seen[eng].get(k, 0) < v:
            self.seen[eng][k] = v
            self.q[eng].append(("wait", k, v))

    def emit(self):
        sems = self.sems
        waited = {}
        for e in ENGS:
            for it in self.q[e]:
                if it[0] == "wait" and it[1][0] == "c":
                    waited.setdefault(it[1], set()).add(it[2])
        rank = {k: {v: i + 1 for i, v in enumerate(sorted(vs))} for k, vs in waited.items()}

        def run(name):
            def body(e):
                n = 0
                for it in self.q[name]:
                    if it[0] == "wait":
                        k, v = it[1], it[2]
                        if k[0] == "c":
                            v = rank[k][v]
                        e.wait_ge(sems[k], v)
                    else:
                        ins = it[1](e)
                        if it[2][0] == "c":
                            n += 1
                            if n in rank.get(it[2], ()):
                                ins.then_inc(sems[it[2]], 1)
                        else:
                            ins.then_inc(sems[it[2]], it[3])
            return body
        with self.nc.Block() as block:
            block.tensor(run("pe"))
            block.scalar(run("act"))
            block.vector(run("dve"))
            block.gpsimd(run("pool"))
            block.sync(run("sp"))

    def close(self):
        for cm in reversed(self.ctx):
            cm.__exit__(None, None, None)


def col_layout():
    idx = {}
    n = [0]

    def add(name, k):
        idx[name] = n[0]
        n[0] += k
    add("c", 8)
    add("final_g", 8)
    for l in range(L):
        add(f"norm1_g{l}", 8)
        add(f"norm2_g{l}", 8)
        add(f"ada_b{l}", 48)
        add(f"conv_w{l}", 8)
        add(f"conv_b{l}", 2)
        add(f"rg_b{l}", 2)
        add(f"ig_b{l}", 2)
        add(f"lam{l}", 2)
        add(f"hgrn_lb{l}", 3)
        add(f"mu{l}", 11)
        add(f"w0{l}", 3)
        add(f"a0{l}", 3)
        add(f"k_k{l}", 3)
        add(f"k_a{l}", 3)
        add(f"r_k{l}", 3)
        add(f"lnx_w{l}", 3)
        add(f"lnx_b{l}", 3)
        add(f"beta{l}", 8)
    return idx, n[0]


CI, NCOL = col_layout()
K_ID, K_BO, K_MINC, K_MSTR, K_MTRIL, K_IDP, K_RST = 0, 128, 256, 320, 384, 448, 512
NCONST = 512 + TB


def make_consts():
    cst = np.zeros((128, NCONST), np.float32)
    p = np.arange(128)
    cst[:, K_ID:K_ID + 128] = np.eye(128, dtype=np.float32)
    cst[:, K_BO:K_BO + 128] = (p[:, None] // 64 == p[None, :] // 64).astype(np.float32)
    j = np.arange(64)
    cst[:, K_MINC:K_MINC + 64] = ((p[:, None] % 64) <= j[None, :]).astype(np.float32)
    cst[:, K_MSTR:K_MSTR + 64] = ((p[:, None] % 64) < j[None, :]).astype(np.float32)
    cst[:, K_MTRIL:K_MTRIL + 64] = (j[None, :] < (p[:, None] % 64)).astype(np.float32)
    cst[:, K_IDP:K_IDP + 64] = ((p[:, None] % 64) == j[None, :]).astype(np.float32)
    t = np.arange(TB)
    cst[:, K_RST:K_RST + TB] = (t % CH != 0).astype(np.float32)[None, :]
    return cst


def pack_cols(inp, b):
    cols = np.zeros((128, NCOL), np.float32)

    def put(name, v):
        v = np.asarray(v, np.float32).reshape(-1, 128)
        cols[:, CI[name]:CI[name] + v.shape[0]] = v.T
    put("c", inp["c"][b])
    put("final_g", inp["final_g"])
    for l in range(L):
        put(f"norm1_g{l}", inp["norm1_g"][l])
        put(f"norm2_g{l}", inp["norm2_g"][l])
        put(f"ada_b{l}", inp["ada_b"][l])
        cw = np.asarray(inp["conv_w"][l], np.float32)
        cwp = np.stack([cw[j, cc * 128:(cc + 1) * 128] for cc in range(2) for j in range(4)], 0)
        put(f"conv_w{l}", cwp)
        put(f"conv_b{l}", inp["conv_b"][l])
        put(f"rg_b{l}", inp["rg_b"][l])
        put(f"ig_b{l}", inp["ig_b"][l])
        put(f"lam{l}", inp["lru_lam"][l])
        put(f"hgrn_lb{l}", inp["hgrn_lb"][l])
        put(f"mu{l}", inp["rwkv_mu"][l])
        put(f"w0{l}", inp["rwkv_w0"][l])
        put(f"a0{l}", inp["rwkv_a0"][l])
        put(f"k_k{l}", inp["rwkv_k_k"][l])
        put(f"k_a{l}", inp["rwkv_k_a"][l])
        put(f"r_k{l}", inp["rwkv_r_k"][l])
        put(f"lnx_w{l}", inp["rwkv_lnx_w"][l])
        put(f"lnx_b{l}", inp["rwkv_lnx_b"][l])
        put(f"beta{l}", inp["mix_beta"][l])
    return cols


def blockdiag(w):
    w = np.asarray(w, np.float32)
    out = np.zeros((L, 2, 128, 128), np.float32)
    for l in range(L):
        for g in range(4):
            cc, gg = g // 2, g % 2
            out[l, cc, gg * 64:(gg + 1) * 64, gg * 64:(gg + 1) * 64] = w[l, g]
    return out


def build_program(n_layers=L, taps=(), do_mixer=(True, True, True), do_ffn=True):
    nc = bass.Bass("TRN2", target_bir_lowering=False)
    es = ExitStack()

    def din(name, shape):
        return nc.dram_tensor(name, list(shape), F32, kind="ExternalInput").ap()
    xT_d = din("xT", [8, 128, T])
    cols_d = din("cols", [128, NCOL])
    cst_d = din("cst", [128, NCONST])
    ada_d = din("ada_w", [L, D, 6 * D])
    win_d = din("w_in", [L, D, PW])
    rgbd_d = din("rgbd", [L, 2, 128, 128])
    igbd_d = din("igbd", [L, 2, 128, 128])
    waup_d = din("wa_up", [L, 128, 384])
    gup_d = din("g_up", [L, 128, 384])
    wout_d = din("w_out", [L, D, D])
    wg_d = din("wg", [L, D, DFF])
    wu_d = din("wu", [L, D, DFF])
    wd_d = din("wd", [L, DFF, D])
    outT_d = nc.dram_tensor("outT", [8, 128, T], F32, kind="ExternalOutput").ap()
    tap_d = {}
    for name, shape in taps:
        tap_d[name] = nc.dram_tensor("tap_" + name, list(shape), F32, kind="ExternalOutput").ap()

    P = Prog(nc)

    class Tl:
        def __init__(self, name, shape, dt=F32, nd=1):
            self.t = es.enter_context(nc.sbuf_tensor(name, list(shape), dt))
            self.d = [Dep() for _ in range(nd)]
            self.D = self.d[0]

    EOBJ = {"dve": "vector", "pool": "gpsimd", "act": "scalar"}

    def tt(eng, out, in0, in1, op, R, W):
        P.op(eng, lambda e: e.tensor_tensor(out, in0, in1, op), reads=R, writes=W)

    def ts(eng, out, in0, s1, s2, op0, op1, R, W):
        if s2 is None:
            P.op(eng, lambda e: e.tensor_scalar(out, in0, s1, None, op0), reads=R, writes=W)
        else:
            P.op(eng, lambda e: e.tensor_scalar(out, in0, s1, s2, op0, op1), reads=R, writes=W)

    def stt(eng, out, in0, sc, in1, op0, op1, R, W):
        P.op(eng, lambda e: e.scalar_tensor_tensor(out, in0, sc, in1, op0, op1), reads=R, writes=W)

    def act(out, in_, func, R, W, bias=None, scale=None):
        kw = {}
        if bias is not None:
            kw["bias"] = bias
        if scale is not None:
            kw["scale"] = scale
        P.op("act", lambda e: e.activation(out, in_, func, **kw), reads=R, writes=W)

    def cp(eng, out, in_, R, W):
        if eng == "act":
            P.op("act", lambda e: e.copy(out, in_), reads=R, writes=W)
        else:
            P.op(eng, lambda e: e.tensor_copy(out, in_), reads=R, writes=W)

    def mm(out, lhsT, rhs, start, stop, R, W):
        P.op("pe", lambda e: e.matmul(out, lhsT, rhs, start=start, stop=stop), reads=R, writes=W)

    def memset(eng, ap, val, W):
        P.op(eng, lambda e: e.memset(ap, val), writes=W)

    def scan(out, d0, d1, init, R, W):
        P.op("dve", lambda e: e.tensor_tensor_scan(out, d0, d1, init, ALU.mult, ALU.add), reads=R, writes=W)

    banks = []
    for i in range(8):
        banks.append((es.enter_context(nc.psum_tensor(f"bank{i}", [128, 512], F32)), BDep()))
    bctr = [0]

    def nps():
        b = banks[bctr[0] % 8]
        bctr[0] += 1
        return b

    X = Tl("X", [128, 8, T], F32, nd=8)
    COLS = Tl("COLS", [128, NCOL])
    CST = Tl("CST", [128, NCONST])
    ident = CST.t[:, K_ID:K_ID + 128]
    bones = CST.t[:, K_BO:K_BO + 128]
    m_inc = CST.t[:, K_MINC:K_MINC + 64]
    m_str = CST.t[:, K_MSTR:K_MSTR + 64]
    m_tril = CST.t[:, K_MTRIL:K_MTRIL + 64]
    idp = CST.t[:, K_IDP:K_IDP + 64]
    rst = CST.t[:, K_RST:K_RST + TB]

    def col(name, j=0, n=1):
        return COLS.t[:, CI[name] + j:CI[name] + j + n]

    P.dma("sp", COLS.t[:], cols_d, writes=[COLS.D])
    P.dma("sp", CST.t[:], cst_d, writes=[CST.D])
    for j in range(8):
        P.dma("sp", X.t[:, j, :], xT_d[j], writes=[X.d[j]])

    NSCR = 12
    scr = [Tl(f"scr{i}", [128, TB + 4]) for i in range(NSCR)]
    sctr = [0]

    def tmp():
        s = scr[sctr[0] % NSCR]
        sctr[0] += 1
        return s

    def interleave(gens):
        gens = list(gens)
        while gens:
            for g in list(gens):
                try:
                    next(g)
                except StopIteration:
                    gens.remove(g)

    CSI = Tl("CSI", [128, 8])
    MOD = Tl("MOD", [128, 48])
    ADA = Tl("ADA", [128, 32])
    DER = Tl("DER", [128, 16])

    act(CSI.t[:], col("c", 0, 8), AF.Silu, [COLS.D], [CSI.D])


    NST = 2
    NWB = 7
    wst = [Tl(f"wst{i}", [128, 8, 128]) for i in range(NST)]
    wbf = [Tl(f"wbf{i}", [128, 8, 128], BF16) for i in range(NWB)]
    wctr = [0]
    sctr2 = [0]

    def wchunk(src_ap, nk=8, cast=True):
        if not cast:
            i = sctr2[0] % NST
            sctr2[0] += 1
            P.dma("sp", wst[i].t[:, 0:nk, :], src_ap, writes=[wst[i].D])
            return wst[i]
        i = wctr[0] % NWB
        wctr[0] += 1
        P.dma("pool", wbf[i].t[:, 0:nk, :], src_ap, writes=[wbf[i].D])
        return wbf[i]

    WOUT = Tl("WOUT", [128, 8, D], BF16)
    HBF = Tl("HBF", [128, 8, TB], BF16)
    YT = Tl("YT", [128, 8, TB], BF16, nd=8)
    RSTD = Tl("RSTD", [128, 512])

    def tap(name, ap, deps):
        if name in tap_d:
            t = P.dma("sp", tap_d[name], ap, reads=deps)
            tap_toks.append(t)
    tap_toks = []

    ONES = Tl("ONES", [128, 128])
    memset("pool", ONES.t[:], 1.0, [ONES.D])
    EPS = Tl("EPS", [128, 8])
    memset("pool", EPS.t[:, 0:1], 1e-6, [EPS.D])
    memset("pool", EPS.t[:, 1:2], 1e-12, [EPS.D])
    memset("pool", EPS.t[:, 2:3], 64e-5, [EPS.D])
    memset("pool", EPS.t[:, 3:4], 1.0, [EPS.D])
    memset("pool", EPS.t[:, 4:5], 0.0, [EPS.D])
    eps6, eps12, epsgn, one_c, zero_c = EPS.t[:, 0:1], EPS.t[:, 1:2], EPS.t[:, 2:3], EPS.t[:, 3:4], EPS.t[:, 4:5]

    def norm_block(t0, ntok, Acol, Bcol, dstbf, dstdep, doff=0):
        bk, bd = nps()
        for h0 in range(0, ntok, TB):
            for j in range(8):
                s = tmp()
                act(s.t[:, :TB], X.t[:, j, t0 + h0:t0 + h0 + TB], AF.Square, [X.d[j]], [s.D])
                mm(bk[:, h0:h0 + TB], ONES.t[:], s.t[:, :TB], j == 0, j == 7, [s.D, ONES.D], [bd])
        act(RSTD.t[:, :ntok], bk[:, :ntok], AF.Ln, [bd, EPS.D], [RSTD.D], bias=eps6, scale=1.0 / D)
        act(RSTD.t[:, :ntok], RSTD.t[:, :ntok], AF.Exp, [RSTD.D], [RSTD.D], scale=-0.5)
        for j in range(8):
            for h0 in range(0, ntok, TB):
                s = tmp()
                stt("dve", s.t[:, :TB], X.t[:, j, t0 + h0:t0 + h0 + TB], Acol(j), RSTD.t[:, h0:h0 + TB],
                    ALU.mult, ALU.mult, [X.d[j], RSTD.D, ADA.D], [s.D])
                act(dstbf[:, j, doff + h0:doff + h0 + TB], s.t[:, :TB], AF.Identity, [s.D, MOD.D], [dstdep], bias=Bcol(j))

    class V:
        def __init__(self, ap, nd=1):
            self.t = ap
            self.d = [Dep() for _ in range(nd)]
            self.D = self.d[0]
    NBIG = 9728
    BIG = Tl("BIG", [128, NBIG])
    H2 = V(BIG.t[:, 0:4096].bitcast(BF16).rearrange("p (k t) -> p k t", k=8))
    ACTT = V(BIG.t[:, 4096:9728].bitcast(BF16).rearrange("p (m t) -> p m t", m=11), nd=11)

    scr2 = [V(BIG.t[:, 6060 + i * 260:6060 + (i + 1) * 260]) for i in range(12)]
    SSETS = [scr[0:4] + scr2[0:4], scr[4:8] + scr2[4:8], scr[8:12] + scr2[8:12]]
    PT = [V(BIG.t[:, i * 260:(i + 1) * 260]) for i in range(11)]
    CARRY = Tl("CARRY", [128, 11])
    HST = Tl("HST", [128, 2])
    XAH = Tl("XAH", [128, 2, 3])
    SB = [Tl(f"SB{i}", [128, 64]) for i in range(3)]
    QTs = [Tl(f"QT{i}", [128, TB], BF16) for i in range(3)]
    KTs = [Tl(f"KT{i}", [128, TB], BF16) for i in range(3)]
    QAs = [Tl(f"QA{i}", [128, TB], BF16) for i in range(3)]
    KAs = [Tl(f"KA{i}", [128, TB], BF16) for i in range(3)]
    PVBs = [Tl(f"PVB{i}", [128, TB], BF16) for i in range(3)]
    IDB = Tl("IDB", [128, 128], BF16)
    HBs = [[Tl(f"HB{j}_{i}", [128, 64], BF16) for i in range(5)] for j in range(3)]
    SNs = [Tl(f"SN{i}", [128, 64]) for i in range(3)]
    OBD = [Dep() for _ in range(3)]
    GBs = [Tl(f"GB{i}", [128, 16]) for i in range(3)]
    TOK = [V(BIG.t[:, 4908 + i * 384:4908 + (i + 1) * 384].bitcast(BF16).rearrange("p (c a b) -> p c a b", c=NCH, a=3))
           for i in range(3)]
    ZC = Tl("ZC", [128, 3, 64])
    BH = [Tl(f"BH{i}", [128, TB], BF16) for i in range(3)]
    AH = [Tl(f"AH{i}", [128, TB], BF16) for i in range(3)]
    RH = [Tl(f"RH{i}", [128, TB], BF16) for i in range(3)]
    KH = [Tl(f"KH{i}", [128, TB], BF16) for i in range(3)]
    VB = [Tl(f"VB{i}", [128, TB], BF16) for i in range(3)]
    ZB = Tl("ZB", [128, 3, 64], BF16)
    SGG = V(BIG.t[:, 4396:4652])
    TW = V(BIG.t[:, 4652:4908])
    BON = [V(BIG.t[:, 3628 + i * 256:3628 + (i + 1) * 256]) for i in range(3)]
    YC = V(BIG.t[:, 2860:3628].rearrange("p (a t) -> p a t", a=3))
    GAM = Tl("GAM", [128, 3, NCH])
    MQ = [[Tl(f"MQ{c}_{i}", [128, 3, 64], BF16) for i in range(2)] for c in range(NCH)]
    MP = [[Tl(f"MP{c}_{i}", [128, 3, 64], BF16) for i in range(2)] for c in range(NCH)]
    MT = [Tl(f"MT{i}", [128, 3, 64], BF16) for i in range(NCH)]
    MAK = [Tl(f"MAK{i}", [128, 3, 64], BF16) for i in range(NCH)]
    MRB = [Tl(f"MRB{i}", [128, 3, 64], BF16) for i in range(NCH)]
    MRK = [Tl(f"MRK{i}", [128, 3, 64], BF16) for i in range(NCH)]
    XS = Tl("XS", [128, 3, 64], BF16); US = Tl("US", [128, 3, 64], BF16); ZTMP = Tl("ZTMP", [128, 3, 64])
    RGBD = Tl("RGBD", [128, 2, 128]); IGBD = Tl("IGBD", [128, 2, 128])
    WAUP = Tl("WAUP", [128, 384]); GUP = Tl("GUP", [128, 384])

    mqctr = [0]
    cp("dve", IDB.t[:], ident, [CST.D], [IDB.D])
    identb = IDB.t

    def bc3(ap2):
        return ap2.unsqueeze(1).to_broadcast([128, 3, 64])

    for l in range(n_layers):
        if l == 0:
            memset("dve", DER.t[:, 2:5], 0.0, [DER.D])
        else:
            tt("dve", DER.t[:, 2:5], col("hgrn_lb1", 0, 3), col("hgrn_lb0", 0, 3), ALU.subtract, [COLS.D], [DER.D])
            act(DER.t[:, 2:5], DER.t[:, 2:5], AF.Sigmoid, [DER.D], [DER.D])
        ts("dve", DER.t[:, 5:8], DER.t[:, 2:5], -1.0, 1.0, ALU.mult, ALU.add, [DER.D], [DER.D])
        act(DER.t[:, 0:2], col(f"lam{l}", 0, 2), AF.Exp, [COLS.D], [DER.D], scale=-1.0)
        act(DER.t[:, 0:2], DER.t[:, 0:2], AF.Ln, [DER.D, EPS.D], [DER.D], bias=one_c)
        ts("dve", DER.t[:, 0:2], DER.t[:, 0:2], -8.0, None, ALU.mult, None, [DER.D], [DER.D])
        lbc = lambda hp: DER.t[:, 2 + hp:3 + hp]
        omlc = lambda hp: DER.t[:, 5 + hp:6 + hp]
        nsp8 = lambda cc: DER.t[:, cc:cc + 1]

        P.dma("sp", RGBD.t[:], rgbd_d[l].rearrange("c p n -> p c n"), writes=[RGBD.D])
        P.dma("sp", IGBD.t[:], igbd_d[l].rearrange("c p n -> p c n"), writes=[IGBD.D])
        P.dma("sp", WAUP.t[:], waup_d[l], writes=[WAUP.D])
        P.dma("sp", GUP.t[:], gup_d[l], writes=[GUP.D])

        adv = ada_d[l].rearrange("(k p) m -> p k m", p=128)
        mbk, mbd = nps()
        for j in range(48):
            st = wchunk(adv[:, :, j * 128:(j + 1) * 128], cast=False)
            for k in range(8):
                mm(mbk[:, j:j + 1], st.t[:, k, :], CSI.t[:, k:k + 1], k == 0, k == 7, [st.D, CSI.D], [mbd])
        tt("dve", MOD.t[:], mbk[:, 0:48], col(f"ada_b{l}", 0, 48), ALU.add, [mbd, COLS.D], [MOD.D])
        stt("dve", ADA.t[:, 0:8], MOD.t[:, 8:16], 1.0, col(f"norm1_g{l}", 0, 8), ALU.add, ALU.mult, [MOD.D, COLS.D], [ADA.D])
        stt("dve", ADA.t[:, 8:16], MOD.t[:, 32:40], 1.0, col(f"norm2_g{l}", 0, 8), ALU.add, ALU.mult, [MOD.D, COLS.D], [ADA.D])
        tap(f"mod{l}", MOD.t[:], [MOD.D])

        wov = wout_d[l].rearrange("(k p) n -> p k n", p=128)
        for f in range(8):
            wb = wchunk(wov[:, :, f * 128:(f + 1) * 128])
            cp("act", WOUT.t[:, :, f * 128:(f + 1) * 128], wb.t[:], [wb.D], [WOUT.D])

        wiv = win_d[l].rearrange("(k p) n -> p k n", p=128)

        def project(chunk, dst_ap, dst_deps, evac="act"):
            wb = wchunk(wiv[:, :, chunk * 128:(chunk + 1) * 128])
            bk, bd = nps()
            for k in range(8):
                mm(bk[:, :TB], wb.t[:, k, :], HBF.t[:, k, :], k == 0, k == 7, [wb.D, HBF.D], [bd])
            cp(evac, dst_ap, bk[:, :TB], [bd], dst_deps)

        memset("dve", HST.t[:], 0.0, [HST.D])
        memset("dve", CARRY.t[:], 0.0, [CARRY.D])
        for hp in range(3):
            memset("dve", SB[hp].t[:], 0.0, [SB[hp].D])
        memset("dve", ZC.t[:], 0.0, [ZC.D])
        memset("dve", ZB.t[:], 0.0, [ZB.D])
        memset("dve", XAH.t[:], 0.0, [XAH.D])

        for tb in range(NTB):
            t0 = tb * TB
            norm_block(t0, TB, lambda j: ADA.t[:, j:j + 1], lambda j: MOD.t[:, j:j + 1], HBF.t, HBF.D)

            if do_mixer[0]:
                def _gen_mixa(cc):
                    S = SSETS[cc]
                    XA, YA = PT[cc], PT[2 + cc]
                    yield
                    project(cc, XA.t[:, 3:3 + TB], [XA.D])
                    yield
                    cp("act", XA.t[:, 0:3], XAH.t[:, cc, :], [XAH.D], [XA.D])
                    yield
                    cp("act", XAH.t[:, cc, :], XA.t[:, TB:TB + 3], [XA.D], [XAH.D])
                    yield
                    project(2 + cc, YA.t[:, :TB], [YA.D], evac="dve")
                    yield
                    u = S[0]
                    yield
                    cw = lambda j: col(f"conv_w{l}", cc * 4 + j)
                    yield
                    ts("dve", u.t[:, :TB], XA.t[:, 0:TB], cw(0), col(f"conv_b{l}", cc), ALU.mult, ALU.add, [XA.D, COLS.D], [u.D])
                    yield
                    for j in range(1, 4):
                        stt("dve", u.t[:, :TB], XA.t[:, j:j + TB], cw(j), u.t[:, :TB], ALU.mult, ALU.add, [XA.D, COLS.D, u.D], [u.D])
                    rb, rbd = nps()
                    yield
                    mm(rb[:, :TB], RGBD.t[:, cc, :], u.t[:, :TB], True, True, [RGBD.D, u.D], [rbd])
                    yield
                    ib, ibd = nps()
                    yield
                    mm(ib[:, :TB], IGBD.t[:, cc, :], u.t[:, :TB], True, True, [IGBD.D, u.D], [ibd])
                    yield
                    r = S[1]; ig = S[2]
                    yield
                    act(r.t[:, :TB], rb[:, :TB], AF.Sigmoid, [rbd, COLS.D], [r.D], bias=col(f"rg_b{l}", cc))
                    yield
                    act(ig.t[:, :TB], ib[:, :TB], AF.Sigmoid, [ibd, COLS.D], [ig.D], bias=col(f"ig_b{l}", cc))
                    yield
                    a = S[3]
                    yield
                    act(a.t[:, :TB], r.t[:, :TB], AF.Exp, [r.D, DER.D], [a.D], scale=nsp8(cc))
                    yield
                    m = S[4]
                    yield
                    act(m.t[:, :TB], a.t[:, :TB], AF.Square, [a.D], [m.D])
                    yield
                    act(m.t[:, :TB], m.t[:, :TB], AF.Sqrt, [m.D, EPS.D], [m.D], bias=one_c, scale=-1.0)
                    yield
                    if tb == 0:
                        memset("dve", m.t[:, 0:1], 1.0, [m.D])
                    tt("dve", m.t[:, :TB], m.t[:, :TB], ig.t[:, :TB], ALU.mult, [m.D, ig.D], [m.D])
                    yield
                    tt("dve", m.t[:, :TB], m.t[:, :TB], u.t[:, :TB], ALU.mult, [m.D, u.D], [m.D])
                    yield
                    h = S[5]
                    yield
                    scan(h.t[:, :TB], a.t[:, :TB], m.t[:, :TB], HST.t[:, cc:cc + 1], [a.D, m.D, HST.D], [h.D])
                    yield
                    cp("dve", HST.t[:, cc:cc + 1], h.t[:, TB - 1:TB], [h.D], [HST.D])
                    yield
                    g1 = S[6]
                    yield
                    act(g1.t[:, :TB], YA.t[:, :TB], AF.Square, [YA.D], [g1.D])
                    yield
                    ts("dve", g1.t[:, :TB], g1.t[:, :TB], 0.044715, 1.0, ALU.mult, ALU.add, [g1.D], [g1.D])
                    yield
                    tt("dve", g1.t[:, :TB], g1.t[:, :TB], YA.t[:, :TB], ALU.mult, [g1.D, YA.D], [g1.D])
                    yield
                    act(g1.t[:, :TB], g1.t[:, :TB], AF.Sigmoid, [g1.D], [g1.D], scale=1.5957691216057308)
                    yield
                    tt("dve", g1.t[:, :TB], g1.t[:, :TB], YA.t[:, :TB], ALU.mult, [g1.D, YA.D], [g1.D])
                    yield
                    tt("dve", h.t[:, :TB], h.t[:, :TB], g1.t[:, :TB], ALU.mult, [h.D, g1.D], [h.D])
                    yield
                    sq = S[7]
                    yield
                    act(sq.t[:, :TB], h.t[:, :TB], AF.Square, [h.D], [sq.D])
                    yield
                    sb_, sbd = nps()
                    yield
                    mm(sb_[:, :TB], bones, sq.t[:, :TB], True, True, [CST.D, sq.D], [sbd])
                    yield
                    act(sq.t[:, :TB], sb_[:, :TB], AF.Ln, [sbd, EPS.D], [sq.D], bias=eps6, scale=1.0 / 64)
                    yield
                    act(sq.t[:, :TB], sq.t[:, :TB], AF.Exp, [sq.D], [sq.D], scale=-0.5)
                    yield
                    stt("dve", YT.t[:, cc, :], h.t[:, :TB], col(f"beta{l}", cc), sq.t[:, :TB], ALU.mult, ALU.mult,
                        [h.D, sq.D, COLS.D], [YT.d[cc]])
                    yield
                interleave([_gen_mixa(_i) for _i in range(2)])
            else:
                for cc in range(2):
                    memset("dve", YT.t[:, cc, :], 0.0, [YT.d[cc]])

            if not do_mixer[1]:
                for hp in range(3):
                    memset("dve", YT.t[:, 2 + hp, :], 0.0, [YT.d[2 + hp]])
            else:
                PGs = [PT[6], PT[7], PT[8]]
                def _gen_bprep(hp):
                    S = SSETS[hp]
                    Pq, Pf, Pg = PT[hp], PT[3 + hp], PGs[hp]
                    yield
                    qa, ka, QT, KT, PVB, GB = QAs[hp], KAs[hp], QTs[hp], KTs[hp], PVBs[hp], GBs[hp]
                    yield
                    project(4 + hp, Pq.t[:, :TB], [Pq.D])
                    yield
                    project(7 + hp, Pf.t[:, :TB], [Pf.D], evac="dve")
                    yield
                    project(10 + hp, PVB.t[:], [PVB.D])
                    yield
                    project(13 + hp, Pg.t[:, :TB], [Pg.D], evac="dve")
                    yield
                    s1 = S[0]; kg = S[1]; bc = S[2]; dd = S[3]; e1 = S[4]
                    yield
                    act(s1.t[:, :TB], Pf.t[:, :TB], AF.Sigmoid, [Pf.D], [s1.D])
                    yield
                    act(s1.t[:, :TB], s1.t[:, :TB], AF.Identity, [s1.D, DER.D], [s1.D], bias=lbc(hp), scale=omlc(hp))
                    yield
                    act(s1.t[:, :TB], s1.t[:, :TB], AF.Ln, [s1.D], [s1.D])
                    yield
                    act(kg.t[:, :TB], Pf.t[:, :TB], AF.Sigmoid, [Pf.D], [kg.D], scale=-1.0)
                    yield
                    ts("dve", kg.t[:, :TB], kg.t[:, :TB], omlc(hp), None, ALU.mult, None, [kg.D, DER.D], [kg.D])
                    yield
                    scan(bc.t[:, :TB], rst, s1.t[:, :TB], zero_c, [CST.D, s1.D, EPS.D], [bc.D])
                    yield
                    bc3v = bc.t[:, :TB].rearrange("p (c s) -> p c s", s=CH)
                    yield
                    bmid = bc3v[:, :, 31:32]
                    yield
                    blast = bc3v[:, :, 63:64]
                    yield
                    act(e1.t[:, :TB], bc.t[:, :TB], AF.Exp, [bc.D], [e1.D])
                    yield
                    stt("dve", qa.t[:], Pq.t[:, :TB], 0.125, e1.t[:, :TB], ALU.mult, ALU.mult, [Pq.D, e1.D], [qa.D])
                    yield
                    ts("dve", e1.t[:, :TB], bc.t[:, :TB], -1.0, 60.0, ALU.mult, ALU.min, [bc.D], [e1.D])
                    yield
                    act(e1.t[:, :TB], e1.t[:, :TB], AF.Exp, [e1.D], [e1.D])
                    yield
                    tt("dve", ka.t[:], kg.t[:, :TB], e1.t[:, :TB], ALU.mult, [kg.D, e1.D], [ka.D])
                    yield
                    tt("dve", dd.t[:, :TB].rearrange("p (c s) -> p c s", s=CH), bc3v, bmid.to_broadcast([128, NCH, CH]),
                       ALU.subtract, [bc.D], [dd.D])
                    act(e1.t[:, :TB], dd.t[:, :TB], AF.Exp, [dd.D], [e1.D])
                    yield
                    stt("dve", QT.t[:], Pq.t[:, :TB], 0.125, e1.t[:, :TB], ALU.mult, ALU.mult, [Pq.D, e1.D], [QT.D])
                    yield
                    act(e1.t[:, :TB], dd.t[:, :TB], AF.Exp, [dd.D], [e1.D], scale=-1.0)
                    yield
                    tt("dve", KT.t[:], kg.t[:, :TB], e1.t[:, :TB], ALU.mult, [kg.D, e1.D], [KT.D])
                    yield
                    gbv = GB.t[:, 0:12].rearrange("p (a c) -> p a c", c=NCH)
                    yield
                    act(gbv[:, 0, :].unsqueeze(2), bmid, AF.Exp, [bc.D], [GB.D])
                    yield
                    act(gbv[:, 1, :].unsqueeze(2), blast, AF.Exp, [bc.D], [GB.D])
                    yield
                    tt("dve", gbv[:, 2, :].unsqueeze(2), blast, bmid, ALU.subtract, [bc.D], [GB.D])
                    yield
                    act(gbv[:, 2, :], gbv[:, 2, :], AF.Exp, [GB.D], [GB.D])
                    yield
                    yield
                interleave([_gen_bprep(_i) for _i in range(3)])
                for c in range(NCH):
                    cs_ = slice(c * CH, (c + 1) * CH)
                    ca_ = slice(c * CH, c * CH + 32)
                    cb_ = slice(c * CH + 32, (c + 1) * CH)
                    abs_ = []
                    for hp in range(3):
                        qa, ka, QT, KT, PVB = QAs[hp], KAs[hp], QTs[hp], KTs[hp], PVBs[hp]
                        ab, abd = nps()
                        abs_.append((ab, abd))
                        for hh in range(2):
                            ps_ = slice(hh * 64, hh * 64 + 64)
                            mm(ab[ps_, 0:32], ka.t[ps_, cs_], qa.t[ps_, ca_], True, True, [ka.D, qa.D], [abd])
                            mm(ab[ps_, 32:64], KT.t[ps_, cs_], QT.t[ps_, cb_], True, True, [KT.D, QT.D], [abd])
                            mm(ab[ps_, 64:128], PVB.t[ps_, cs_], identb[ps_, ps_], True, True, [PVB.D, IDB.D], [abd])
                            mm(ab[ps_, 128:192], KT.t[ps_, cs_], identb[ps_, ps_], True, True, [KT.D, IDB.D], [abd])
                    for hp in range(3):
                        ATS, VTK, KTK, STL, SBb = HBs[hp]
                        ab, abd = abs_[hp]
                        tt("dve", ATS.t[:], ab[:, 0:64], m_inc, ALU.mult, [abd, CST.D], [ATS.D])
                        cp("act", VTK.t[:], ab[:, 64:128], [abd], [VTK.D])
                        cp("act", KTK.t[:], ab[:, 128:192], [abd], [KTK.D])
                        ts("dve", STL.t[:], SB[hp].t[:], GBs[hp].t[:, c:c + 1], None, ALU.mult, None, [SB[hp].D, GBs[hp].D], [STL.D])
                        cp("act", SBb.t[:], SB[hp].t[:], [SB[hp].D], [SBb.D])
                    obs_ = []
                    for hp in range(3):
                        ATS, VTK, KTK, STL, SBb = HBs[hp]
                        qa, QT = QAs[hp], QTs[hp]
                        ob, obd = nps()
                        obs_.append((ob, obd))
                        for hh in range(2):
                            ps_ = slice(hh * 64, hh * 64 + 64)
                            mm(ob[ps_, 0:64], VTK.t[ps_, :], ATS.t[ps_, :], True, False, [VTK.D, ATS.D], [obd])
                            mm(ob[ps_, 0:32], SBb.t[ps_, :], qa.t[ps_, ca_], False, False, [SBb.D, qa.D], [obd])
                            mm(ob[ps_, 32:64], STL.t[ps_, :], QT.t[ps_, cb_], False, True, [STL.D, QT.D], [obd])
                            mm(ob[ps_, 64:128], KTK.t[ps_, :], VTK.t[ps_, :], True, True, [KTK.D, VTK.D], [obd])
                    for hp in range(3):
                        ob, obd = obs_[hp]
                        GB = GBs[hp]
                        cp("act", YC.t[:, hp, cs_], ob[:, 0:64], [obd], [OBD[hp]])
                        ts("dve", SNs[hp].t[:], ob[:, 64:128], GB.t[:, 8 + c:9 + c], None, ALU.mult, None, [obd, GB.D], [SNs[hp].D])
                        stt("dve", SB[hp].t[:], SB[hp].t[:], GB.t[:, 4 + c:5 + c], SNs[hp].t[:], ALU.mult, ALU.add,
                            [SB[hp].D, GB.D, SNs[hp].D], [SB[hp].D])
                def _gen_bepi(hp):
                    S = SSETS[hp]
                    Pg = PGs[hp]
                    yield
                    OBt = YC.t[:, hp, :]
                    yield
                    sq = S[0]
                    yield
                    act(sq.t[:, :TB], OBt, AF.Square, [OBD[hp]], [sq.D])
                    yield
                    sb_, sbd = nps()
                    yield
                    mm(sb_[:, :TB], bones, sq.t[:, :TB], True, True, [CST.D, sq.D], [sbd])
                    yield
                    act(sq.t[:, :TB], sb_[:, :TB], AF.Ln, [sbd, EPS.D], [sq.D], bias=eps6, scale=1.0 / 64)
                    yield
                    act(sq.t[:, :TB], sq.t[:, :TB], AF.Exp, [sq.D], [sq.D], scale=-0.5)
                    yield
                    sl = S[1]
                    yield
                    act(sl.t[:, :TB], Pg.t[:, :TB], AF.Silu, [Pg.D], [sl.D])
                    yield
                    tt("dve", sq.t[:, :TB], sq.t[:, :TB], OBt, ALU.mult, [sq.D, OBD[hp]], [sq.D])
                    yield
                    stt("dve", YT.t[:, 2 + hp, :], sq.t[:, :TB], col(f"beta{l}", 2 + hp), sl.t[:, :TB], ALU.mult, ALU.mult,
                        [sq.D, sl.D, COLS.D], [YT.d[2 + hp]])

                    yield
                interleave([_gen_bepi(_i) for _i in range(3)])
            if do_mixer[2]:
                for i in range(11):
                    Xt = PT[i]
                    project(16 + i, Xt.t[:, 1:1 + TB], [Xt.D], evac=("act" if i % 2 == 0 else "dve"))
                    cp("act", Xt.t[:, 0:1], CARRY.t[:, i:i + 1], [CARRY.D], [Xt.D])
                    cp("act", CARRY.t[:, i:i + 1], Xt.t[:, TB:TB + 1], [Xt.D], [CARRY.D])
                    d_ = tmp()
                    tt("dve", d_.t[:, :TB], Xt.t[:, 0:TB], Xt.t[:, 1:1 + TB], ALU.subtract, [Xt.D], [d_.D])
                    stt("dve", Xt.t[:, 1:1 + TB], d_.t[:, :TB], col(f"mu{l}", i), Xt.t[:, 1:1 + TB], ALU.mult, ALU.add,
                        [d_.D, Xt.D, COLS.D], [Xt.D])
                sR = [PT[i] for i in range(0, 3)]
                sK = [PT[i] for i in range(3, 6)]
                sV = [PT[i] for i in range(6, 9)]
                sWA, sG = PT[9], PT[10]
                V1 = lambda t_: t_.t[:, 1:1 + TB]
                tw = TW
                act(tw.t[0:64, :TB], sWA.t[0:64, 1:1 + TB], AF.Tanh, [sWA.D], [tw.D])
                act(SGG.t[:], V1(sG), AF.Sigmoid, [sG.D], [SGG.D])
                def _gen_cprep(hp):
                    S = SSETS[hp]
                    R_, K_, V_ = sR[hp], sK[hp], sV[hp]
                    yield
                    kk = S[0]; sq = S[1]; k2 = S[2]; e_ = S[3]; cum = S[4]; ex = S[5]; lw = S[6]; aa = S[7]
                    yield
                    hc = slice(hp * 128, (hp + 1) * 128)
                    yield
                    zb, zbd = nps()
                    yield
                    mm(zb[:, :TB], WAUP.t[0:64, hc], tw.t[0:64, :TB], True, True, [WAUP.D, tw.D], [zbd])
                    yield
                    act(lw.t[:, :TB], zb[:, :TB], AF.Sigmoid, [zbd, COLS.D], [lw.D], bias=col(f"w0{l}", hp))
                    yield
                    ts("dve", lw.t[:, :TB], lw.t[:, :TB], -0.6065306597126334, None, ALU.mult, None, [lw.D], [lw.D])
                    yield
                    ab_, abd_ = nps()
                    yield
                    mm(ab_[:, :TB], WAUP.t[64:128, hc], sWA.t[64:128, 1:1 + TB], True, True, [WAUP.D, sWA.D], [abd_])
                    yield
                    act(aa.t[:, :TB], ab_[:, :TB], AF.Sigmoid, [abd_, COLS.D], [aa.D], bias=col(f"a0{l}", hp))
                    yield
                    ts("dve", kk.t[:, :TB], V1(K_), col(f"k_k{l}", hp), None, ALU.mult, None, [K_.D, COLS.D], [kk.D])
                    yield
                    act(sq.t[:, :TB], kk.t[:, :TB], AF.Square, [kk.D], [sq.D])
                    yield
                    nb_, nbd_ = nps()
                    yield
                    mm(nb_[:, :TB], bones, sq.t[:, :TB], True, True, [CST.D, sq.D], [nbd_])
                    yield
                    act(sq.t[:, :TB], nb_[:, :TB], AF.Ln, [nbd_, EPS.D], [sq.D], bias=eps12)
                    yield
                    act(sq.t[:, :TB], sq.t[:, :TB], AF.Exp, [sq.D], [sq.D], scale=-0.5)
                    yield
                    tt("dve", kk.t[:, :TB], kk.t[:, :TB], sq.t[:, :TB], ALU.mult, [kk.D, sq.D], [kk.D])
                    yield
                    ts("dve", k2.t[:, :TB], aa.t[:, :TB], -1.0, col(f"k_a{l}", hp), ALU.add, ALU.mult, [aa.D, COLS.D], [k2.D])
                    yield
                    stt("dve", k2.t[:, :TB], k2.t[:, :TB], 1.0, V1(K_), ALU.add, ALU.mult, [k2.D, K_.D], [k2.D])
                    yield
                    tt("dve", e_.t[:, :TB], V1(R_), k2.t[:, :TB], ALU.mult, [R_.D, k2.D], [e_.D])
                    yield
                    ts("dve", e_.t[:, :TB], e_.t[:, :TB], col(f"r_k{l}", hp), None, ALU.mult, None, [e_.D, COLS.D], [e_.D])
                    yield
                    bb_, bbd_ = nps()
                    yield
                    mm(bb_[:, :TB], bones, e_.t[:, :TB], True, True, [CST.D, e_.D], [bbd_])
                    yield
                    tt("dve", BON[hp].t[:], bb_[:, :TB], V1(V_), ALU.mult, [bbd_, V_.D], [BON[hp].D])
                    yield
                    scan(cum.t[:, :TB], rst, lw.t[:, :TB], zero_c, [CST.D, lw.D, EPS.D], [cum.D])
                    yield
                    act(ex.t[:, :TB], cum.t[:, :TB], AF.Exp, [cum.D], [ex.D])
                    yield
                    cp("act", GAM.t[:, hp, :].unsqueeze(2), ex.t[:, :TB].rearrange("p (c s) -> p c s", s=CH)[:, :, 63:64], [ex.D], [GAM.D])
                    yield
                    tt("dve", RH[hp].t[:], V1(R_), ex.t[:, :TB], ALU.mult, [R_.D, ex.D], [RH[hp].D])
                    yield
                    cp("act", VB[hp].t[:], V1(V_), [V_.D], [VB[hp].D])
                    yield
                    act(ex.t[:, :TB], cum.t[:, :TB], AF.Exp, [cum.D], [ex.D], scale=-1.0)
                    yield
                    tt("dve", KH[hp].t[:], k2.t[:, :TB], ex.t[:, :TB], ALU.mult, [k2.D, ex.D], [KH[hp].D])
                    yield
                    tt("dve", e_.t[:, :TB], kk.t[:, :TB], aa.t[:, :TB], ALU.mult, [kk.D, aa.D], [e_.D])
                    yield
                    tt("dve", BH[hp].t[:], e_.t[:, :TB], ex.t[:, :TB], ALU.mult, [e_.D, ex.D], [BH[hp].D])
                    yield
                    tt("dve", cum.t[:, :TB], cum.t[:, :TB], lw.t[:, :TB], ALU.subtract, [cum.D, lw.D], [cum.D])
                    yield
                    act(ex.t[:, :TB], cum.t[:, :TB], AF.Exp, [cum.D], [ex.D])
                    yield
                    stt("dve", AH[hp].t[:], kk.t[:, :TB], -1.0, ex.t[:, :TB], ALU.mult, ALU.mult, [kk.D, ex.D], [AH[hp].D])
                    yield
                    yield
                interleave([_gen_cprep(_i) for _i in range(3)])
                for c in range(NCH):
                    cs1 = slice(1 + c * CH, 1 + (c + 1) * CH)
                    cs_ = slice(c * CH, (c + 1) * CH)
                    srcs = [(VB, False), (BH, False), (KH, False)]
                    for qi, (src, halo) in enumerate(srcs):
                        tb_, tbd_ = nps()
                        for hp in range(3):
                            for hh in range(2):
                                ps_ = slice(hh * 64, hh * 64 + 64)
                                sap = src[hp].t[ps_, cs1] if halo else src[hp].t[ps_, cs_]
                                mm(tb_[ps_, hp * 64:(hp + 1) * 64], sap, identb[ps_, ps_], True, True, [src[hp].D, IDB.D], [tbd_])
                        cp("act" if qi != 1 else "dve", TOK[qi].t[:, c, :, :], tb_[:, 0:192].rearrange("p (a b) -> p a b", b=64),
                           [tbd_], [TOK[qi].D])
                cur = [None] * NCH
                for c in range(NCH):
                    cs_ = slice(c * CH, (c + 1) * CH)
                    q0 = MQ[c][0]; p0 = MP[c][0]
                    specs = [
                        (q0, lambda hp, ps_: BH[hp].t[ps_, cs_], lambda hp, ps_: AH[hp].t[ps_, cs_], m_str, lambda hp: [BH[hp].D, AH[hp].D]),
                        (p0, lambda hp, ps_: AH[hp].t[ps_, cs_], lambda hp, ps_: BH[hp].t[ps_, cs_], m_tril, lambda hp: [BH[hp].D, AH[hp].D]),
                        (MAK[c], lambda hp, ps_: KH[hp].t[ps_, cs_], lambda hp, ps_: AH[hp].t[ps_, cs_], m_str, lambda hp: [KH[hp].D, AH[hp].D]),
                        (MRB[c], lambda hp, ps_: BH[hp].t[ps_, cs_], lambda hp, ps_: RH[hp].t[ps_, cs_], m_inc, lambda hp: [BH[hp].D, RH[hp].D]),
                        (MRK[c], lambda hp, ps_: KH[hp].t[ps_, cs_], lambda hp, ps_: RH[hp].t[ps_, cs_], m_inc, lambda hp: [KH[hp].D, RH[hp].D]),
                    ]
                    for si, (dst, lf, rf, msk, dps) in enumerate(specs):
                        b_, bd_ = nps()
                        for hp in range(3):
                            for hh in range(2):
                                ps_ = slice(hh * 64, hh * 64 + 64)
                                mm(b_[ps_, hp * 64:(hp + 1) * 64], lf(hp, ps_), rf(hp, ps_), True, True, dps(hp), [bd_])
                        tt("dve", dst.t[:], b_[:, 0:192].rearrange("p (a b) -> p a b", b=64), bc3(msk), ALU.mult, [bd_, CST.D], [dst.D])
                    tt("dve", MT[c].t[:], q0.t[:], bc3(idp), ALU.add, [q0.D, CST.D], [MT[c].D])
                    cur[c] = (q0, p0)
                for lev in range(1, 6):
                    pbs = []
                    for c in range(NCH):
                        qc, pc = cur[c]
                        pb_, pbd_ = nps()
                        pbs.append((pb_, pbd_))
                        for hp in range(3):
                            for hh in range(2):
                                ps_ = slice(hh * 64, hh * 64 + 64)
                                mm(pb_[ps_, hp * 64:(hp + 1) * 64], qc.t[ps_, hp, :], pc.t[ps_, hp, :], True, True, [qc.D, pc.D], [pbd_])
                                if lev < 5:
                                    mm(pb_[ps_, 192 + hp * 64:192 + (hp + 1) * 64], pc.t[ps_, hp, :], qc.t[ps_, hp, :], True, True,
                                       [qc.D, pc.D], [pbd_])
                    for c in range(NCH):
                        pb_, pbd_ = pbs[c]
                        qn = MQ[c][lev % 2]; pn = MP[c][lev % 2]
                        cp("act", pn.t[:], pb_[:, 0:192].rearrange("p (a b) -> p a b", b=64), [pbd_], [pn.D])
                        if lev < 5:
                            cp("dve", qn.t[:], pb_[:, 192:384].rearrange("p (a b) -> p a b", b=64), [pbd_], [qn.D])
                        cur[c] = (qn, pn)
                    tbs = []
                    for c in range(NCH):
                        qn, pn = cur[c]
                        tb2, tbd2 = nps()
                        tbs.append((tb2, tbd2))
                        for hp in range(3):
                            for hh in range(2):
                                ps_ = slice(hh * 64, hh * 64 + 64)
                                mm(tb2[ps_, hp * 64:(hp + 1) * 64], pn.t[ps_, hp, :], MT[c].t[ps_, hp, :], True, True, [pn.D, MT[c].D], [tbd2])
                    for c in range(NCH):
                        tb2, tbd2 = tbs[c]
                        tt("dve", MT[c].t[:], MT[c].t[:], tb2[:, 0:192].rearrange("p (a b) -> p a b", b=64), ALU.add, [MT[c].D, tbd2], [MT[c].D])
                for c in range(NCH):
                    cs1 = slice(1 + c * CH, 1 + (c + 1) * CH)
                    cs_ = slice(c * CH, (c + 1) * CH)
                    xb, xbd = nps()
                    for hp in range(3):
                        for hh in range(2):
                            ps_ = slice(hh * 64, hh * 64 + 64)
                            o_ = xb[ps_, hp * 64:(hp + 1) * 64]
                            mm(o_, AH[hp].t[ps_, cs_], ZB.t[ps_, hp, :], True, False, [AH[hp].D, ZB.D], [xbd])
                            mm(o_, MAK[c].t[ps_, hp, :], TOK[0].t[ps_, c, hp, :], False, True, [MAK[c].D, TOK[0].D], [xbd])
                    cp("act", XS.t[:], xb[:, 0:192].rearrange("p (a b) -> p a b", b=64), [xbd], [XS.D])
                    ub, ubd = nps()
                    for hp in range(3):
                        for hh in range(2):
                            ps_ = slice(hh * 64, hh * 64 + 64)
                            mm(ub[ps_, hp * 64:(hp + 1) * 64], MT[c].t[ps_, hp, :], XS.t[ps_, hp, :], True, True, [MT[c].D, XS.D], [ubd])
                    cp("dve", US.t[:], ub[:, 0:192].rearrange("p (a b) -> p a b", b=64), [ubd], [US.D])
                    yb, ybd = nps()
                    for hp in range(3):
                        for hh in range(2):
                            ps_ = slice(hh * 64, hh * 64 + 64)
                            o_ = yb[ps_, hp * 64:(hp + 1) * 64]
                            mm(o_, ZB.t[ps_, hp, :], RH[hp].t[ps_, cs_], True, False, [ZB.D, RH[hp].D], [ybd])
                            mm(o_, US.t[ps_, hp, :], MRB[c].t[ps_, hp, :], False, False, [US.D, MRB[c].D], [ybd])
                            mm(o_, TOK[0].t[ps_, c, hp, :], MRK[c].t[ps_, hp, :], False, True, [TOK[0].D, MRK[c].D], [ybd])
                            o2 = yb[ps_, 192 + hp * 64:192 + (hp + 1) * 64]
                            mm(o2, TOK[1].t[ps_, c, hp, :], US.t[ps_, hp, :], True, False, [TOK[1].D, US.D], [ybd])
                            mm(o2, TOK[2].t[ps_, c, hp, :], TOK[0].t[ps_, c, hp, :], False, True, [TOK[2].D, TOK[0].D], [ybd])
                    cp("act", YC.t[:, :, cs_], yb[:, 0:192].rearrange("p (a b) -> p a b", b=64), [ybd], [YC.D] + OBD)
                    tt("dve", ZTMP.t[:], yb[:, 192:384].rearrange("p (a b) -> p a b", b=64), ZC.t[:], ALU.add, [ybd, ZC.D], [ZTMP.D])
                    tt("dve", ZB.t[:], ZTMP.t[:], GAM.t[:, :, c:c + 1].to_broadcast([128, 3, 64]), ALU.mult, [ZTMP.D, GAM.D], [ZB.D])
                    tt("dve", ZC.t[:], ZTMP.t[:], GAM.t[:, :, c:c + 1].to_broadcast([128, 3, 64]), ALU.mult, [ZTMP.D, GAM.D], [ZC.D])
                def _gen_cgn(hp):
                    S = SSETS[hp]
                    mb_, mbd_ = nps()
                    yield
                    mm(mb_[:, :TB], bones, YC.t[:, hp, :], True, True, [CST.D, YC.D, OBD[hp]], [mbd_])
                    yield
                    yc = S[0]; sq = S[1]
                    yield
                    stt("dve", yc.t[:, :TB], mb_[:, :TB], -1.0 / 64, YC.t[:, hp, :], ALU.mult, ALU.add, [mbd_, YC.D, OBD[hp]], [yc.D])
                    yield
                    act(sq.t[:, :TB], yc.t[:, :TB], AF.Square, [yc.D], [sq.D])
                    yield
                    vb_, vbd_ = nps()
                    yield
                    mm(vb_[:, :TB], bones, sq.t[:, :TB], True, True, [CST.D, sq.D], [vbd_])
                    yield
                    act(sq.t[:, :TB], vb_[:, :TB], AF.Ln, [vbd_, EPS.D], [sq.D], bias=epsgn, scale=1.0 / 64)
                    yield
                    act(sq.t[:, :TB], sq.t[:, :TB], AF.Exp, [sq.D], [sq.D], scale=-0.5)
                    yield
                    tt("dve", yc.t[:, :TB], yc.t[:, :TB], sq.t[:, :TB], ALU.mult, [yc.D, sq.D], [yc.D])
                    yield
                    act(yc.t[:, :TB], yc.t[:, :TB], AF.Identity, [yc.D, COLS.D], [yc.D], bias=col(f"lnx_b{l}", hp), scale=col(f"lnx_w{l}", hp))
                    yield
                    tt("dve", yc.t[:, :TB], yc.t[:, :TB], BON[hp].t[:], ALU.add, [yc.D, BON[hp].D], [yc.D])
                    yield
                    gb_, gbd_ = nps()
                    yield
                    mm(gb_[:, :TB], GUP.t[:, hp * 128:(hp + 1) * 128], SGG.t[:], True, True, [GUP.D, SGG.D], [gbd_])
                    yield
                    stt("dve", YT.t[:, 5 + hp, :], yc.t[:, :TB], col(f"beta{l}", 5 + hp), gb_[:, :TB], ALU.mult, ALU.mult,
                        [yc.D, gbd_, COLS.D], [YT.d[5 + hp]])
                    yield
                interleave([_gen_cgn(_i) for _i in range(3)])
            else:
                for hp in range(3):
                    memset("dve", YT.t[:, 5 + hp, :], 0.0, [YT.d[5 + hp]])

            if tb == 0 and f"y{l}" in tap_d:
                ytmp = tmp()
                for j in range(8):
                    cp("dve", ytmp.t[:, :TB], YT.t[:, j, :], [YT.d[j]], [ytmp.D])
                    t = P.dma("sp", tap_d[f"y{l}"][j], ytmp.t[:, :TB], reads=[ytmp.D])
                    tap_toks.append(t)

            for f in range(8):
                bk, bd = nps()
                for k in range(8):
                    mm(bk[:, :TB], WOUT.t[:, k, f * 128:(f + 1) * 128], YT.t[:, k, :], k == 0, k == 7, [WOUT.D, YT.d[k]], [bd])
                stt("dve", X.t[:, f, t0:t0 + TB], bk[:, :TB], MOD.t[:, 16 + f:17 + f], X.t[:, f, t0:t0 + TB], ALU.mult, ALU.add,
                    [bd, MOD.D, X.d[f]], [X.d[f]])

        if f"x1_{l}" in tap_d:
            for j in range(8):
                for q in range(0, T, 512):
                    tap_toks.append(P.dma("sp", tap_d[f"x1_{l}"][j, :, q:q + 512], X.t[:, j, q:q + 512], reads=[X.d[j]]))

        if do_ffn:
            wgv = wg_d[l].rearrange("(k p) n -> p k n", p=128)
            wuv = wu_d[l].rearrange("(k p) n -> p k n", p=128)
            wdv = wd_d[l].rearrange("(m p) n -> p m n", p=128)
            P.barrier()
            def ffn_norm(fb):
                for th in range(TBF // 512):
                    norm_block(fb * TBF + th * 512, 512, lambda j: ADA.t[:, 8 + j:9 + j], lambda j: MOD.t[:, 24 + j:25 + j], H2.t, H2.D, doff=th * 512)
            ffn_norm(0)
            for fb in range(NTBF):
                t0 = fb * TBF
                for half in range(2):
                    for mi in range(11):
                        m = half * 11 + mi
                        wgb = wchunk(wgv[:, :, m * 128:(m + 1) * 128])
                        wub = wchunk(wuv[:, :, m * 128:(m + 1) * 128])
                        for th in range(TBF // 512):
                            tsl = slice(th * 512, (th + 1) * 512)
                            gk, gd = nps()
                            for k in range(8):
                                mm(gk[:, :], wgb.t[:, k, :], H2.t[:, k, tsl], k == 0, k == 7, [wgb.D, H2.D], [gd])
                            uk, ud = nps()
                            for k in range(8):
                                mm(uk[:, :], wub.t[:, k, :], H2.t[:, k, tsl], k == 0, k == 7, [wub.D, H2.D], [ud])
                            for h0 in range(0, 512, TB):
                                sg = tmp()
                                act(sg.t[:, :TB], gk[:, h0:h0 + TB], AF.Silu, [gd], [sg.D])
                                tt("dve", ACTT.t[:, mi, th * 512 + h0:th * 512 + h0 + TB], sg.t[:, :TB], uk[:, h0:h0 + TB], ALU.mult,
                                   [sg.D, ud], [ACTT.d[mi]])
                    if half == 1 and fb + 1 < NTBF:
                        ffn_norm(fb + 1)
                    for f in range(8):
                        w1 = wchunk(wdv[:, half * 11:half * 11 + 8, f * 128:(f + 1) * 128])
                        w2 = wchunk(wdv[:, half * 11 + 8:half * 11 + 11, f * 128:(f + 1) * 128], nk=3)
                        for th in range(TBF // 512):
                            tsl = slice(th * 512, (th + 1) * 512)
                            bk, bd = nps()
                            for mi in range(11):
                                wt = w1.t[:, mi, :] if mi < 8 else w2.t[:, mi - 8, :]
                                wd_ = w1.D if mi < 8 else w2.D
                                mm(bk[:, :], wt, ACTT.t[:, mi, tsl], mi == 0, mi == 10, [wd_, ACTT.d[mi]], [bd])
                            xs_ = slice(t0 + th * 512, t0 + (th + 1) * 512)
                            stt("dve", X.t[:, f, xs_], bk[:, :], MOD.t[:, 40 + f:41 + f], X.t[:, f, xs_], ALU.mult, ALU.add,
                                [bd, MOD.D, X.d[f]], [X.d[f]])
            P.barrier()
        if f"x2_{l}" in tap_d:
            for j in range(8):
                for q in range(0, T, 512):
                    tap_toks.append(P.dma("sp", tap_d[f"x2_{l}"][j, :, q:q + 512], X.t[:, j, q:q + 512], reads=[X.d[j]]))

    out_toks = []
    for fb in range(T // 512):
        t0 = fb * 512
        bk, bd = nps()
        for h0 in range(0, 512, TB):
            for j in range(8):
                s = tmp()
                act(s.t[:, :TB], X.t[:, j, t0 + h0:t0 + h0 + TB], AF.Square, [X.d[j]], [s.D])
                mm(bk[:, h0:h0 + TB], ONES.t[:], s.t[:, :TB], j == 0, j == 7, [s.D, ONES.D], [bd])
        act(RSTD.t[:, :512], bk[:, :512], AF.Ln, [bd, EPS.D], [RSTD.D], bias=eps6, scale=1.0 / D)
        act(RSTD.t[:, :512], RSTD.t[:, :512], AF.Exp, [RSTD.D], [RSTD.D], scale=-0.5)
        for j in range(8):
            for h0 in range(0, 512, TB):
                o = tmp()
                stt("dve", o.t[:, :TB], X.t[:, j, t0 + h0:t0 + h0 + TB], col("final_g", j), RSTD.t[:, h0:h0 + TB], ALU.mult, ALU.mult,
                    [X.d[j], RSTD.D, COLS.D], [o.D])
                out_toks.append(P.dma("sp", outT_d[j, :, t0 + h0:t0 + h0 + TB], o.t[:, :TB], reads=[o.D]))
    for t in out_toks + tap_toks:
        P.wait_tok("sp", t)
    build_program.last_counts = (dict(P.cnt), dict(P.dcnt))
    P.emit()
    P.close()
    es.close()
    return nc


def make_in_maps(inp):
    f = lambda a: np.ascontiguousarray(np.asarray(a, np.float32))
    cst = make_consts()
    rgbd = blockdiag(inp["rg_w"])
    igbd = blockdiag(inp["ig_w"])
    wa_up = f(np.concatenate([np.asarray(inp["rwkv_w_up"]), np.asarray(inp["rwkv_a_up"])], axis=1))
    shared = {
        "cst": cst, "ada_w": f(inp["ada_w"]), "w_in": f(inp["w_in"]), "rgbd": rgbd, "igbd": igbd,
        "wa_up": wa_up, "g_up": f(inp["rwkv_g_up"]), "w_out": f(inp["w_out"]),
        "wg": f(inp["ffn_w_gate"]), "wu": f(inp["ffn_w_up"]), "wd": f(inp["ffn_w_down"]),
    }
    x = np.asarray(inp["x"], np.float32)
    maps = []
    for b in range(NB):
        m = dict(shared)
        m["xT"] = np.ascontiguousarray(x[b].T.reshape(8, 128, T))
        m["cols"] = pack_cols(inp, b)
        maps.append(m)
    return maps


def kernel(**inputs):
    nc = build_program()
    maps = make_in_maps(inputs)
    res = run_bass_kernel_spmd(nc, maps, core_ids=list(range(NB)))
    out = np.empty((NB, T, D), np.float32)
    for b in range(NB):
        out[b] = res.results[b]["outT"].reshape(D, T).T
    return out
```

```python
import math
STOPB = 99
from contextlib import ExitStack
import numpy as np
import concourse.bass as bass
import concourse.mybir as mybir
from concourse.bass_utils import run_bass_kernel_spmd

F32 = mybir.dt.float32
BF16 = mybir.dt.bfloat16
ALU = mybir.AluOpType
AF = mybir.ActivationFunctionType

L = 2
D = 1024
T = 2048
NB = 8
PW = 3456
DFF = 2816
TB = 256
NTB = T // TB
CH = 64
NCH = TB // CH
TBF = 1024
NTBF = T // TBF
ENGS = ("pe", "act", "dve", "pool", "sp")
NDMA = 8


class Dep:
    __slots__ = ("w", "r")

    def __init__(self):
        self.w = None
        self.r = {}


class BDep(Dep):
    __slots__ = ()


class Prog:
    def __init__(self, nc):
        self.nc = nc
        self.q = {e: [] for e in ENGS}
        self.cnt = {e: 0 for e in ENGS}
        self.dcnt = {e: 0 for e in ENGS}
        self.seen = {e: {} for e in ENGS}
        self.sems = {}
        self.ctx = []
        for e in ENGS:
            self._mk(("c", e))
            for i in range(NDMA):
                self._mk(("d", e, i))

    def _mk(self, key):
        cm = self.nc.semaphore("s_" + "_".join(str(k) for k in key))
        self.sems[key] = cm.__enter__()
        self.ctx.append(cm)

    def _need(self, eng, reads, writes):
        needs = {}

        def add(k, v):
            if needs.get(k, 0) < v:
                needs[k] = v
        for d in reads:
            if d.w is not None:
                add(*d.w)
        for d in writes:
            if d.w is not None:
                add(*d.w)
            for k, v in d.r.items():
                add(k, v)
        for k, v in needs.items():
            if k == ("c", "pe") and eng == "pe":
                continue
            if self.seen[eng].get(k, 0) >= v:
                continue
            self.seen[eng][k] = v
            self.q[eng].append(("wait", k, v))

    def _mark(self, tok, reads, writes):
        k, v = tok
        for d in reads:
            if d.r.get(k, 0) < v:
                d.r[k] = v
        for d in writes:
            d.w = tok
            d.r = {}

    def op(self, eng, fn, reads=(), writes=()):
        if any(isinstance(d, BDep) for d in reads):
            writes = list(writes) + [d for d in reads if isinstance(d, BDep)]
            reads = [d for d in reads if not isinstance(d, BDep)]
        self._need(eng, reads, writes)
        self.cnt[eng] += 1
        tok = (("c", eng), self.cnt[eng])
        self.q[eng].append(("op", fn, tok[0], 1))
        self._mark(tok, reads, writes)
        return tok

    def dma(self, eng, out, in_, reads=(), writes=()):
        self._need(eng, reads, writes)
        i = self.dcnt[eng]
        self.dcnt[eng] += 1
        slot, rnd = i % NDMA, i // NDMA
        key = ("d", eng, slot)
        if rnd > 0 and self.seen[eng].get(key, 0) < 16 * rnd:
            self.seen[eng][key] = 16 * rnd
            self.q[eng].append(("wait", key, 16 * rnd))
        tok = (key, 16 * (rnd + 1))
        self.q[eng].append(("op", lambda e: e.dma_start(out=out, in_=in_), key, 16))
        self._mark(tok, reads, writes)
        return tok

    def barrier(self):
        toks = [(("c", e), self.cnt[e]) for e in ENGS if self.cnt[e] > 0]
        for e in ENGS:
            for sl in range(NDMA):
                if self.dcnt[e] > sl:
                    toks.append((("d", e, sl), 16 * ((self.dcnt[e] - sl + NDMA - 1) // NDMA)))
        for e in ENGS:
            for k, v in toks:
                if k == ("c", "pe") and e == "pe":
                    continue
                self.wait_tok(e, (k, v))

    def wait_tok(self, eng, tok):
        k, v = tok
        if self.seen[eng].get(k, 0) < v:
            self.seen[eng][k] = v
            self.q[eng].append(("wait", k, v))

    def emit(self):
        sems = self.sems
        waited = {}
        for e in ENGS:
            for it in self.q[e]:
                if it[0] == "wait" and it[1][0] == "c":
                    waited.setdefault(it[1], set()).add(it[2])
        rank = {k: {v: i + 1 for i, v in enumerate(sorted(vs))} for k, vs in waited.items()}

        def run(name):
            def body(e):
                n = 0
                for it in self.q[name]:
                    if it[0] == "wait":
                        k, v = it[1], it[2]
                        if k[0] == "c":
                            v = rank[k][v]
                        e.wait_ge(sems[k], v)
                    else:
                        ins = it[1](e)
                        if it[2][0] == "c":
                            n += 1
                            if n in rank.get(it[2], ()):
                                ins.then_inc(sems[it[2]], 1)
                        else:
                            ins.then_inc(sems[it[2]], it[3])
            return body
        with self.nc.Block() as block:
            block.tensor(run("pe"))
            block.scalar(run("act"))
            block.vector(run("dve"))
            block.gpsimd(run("pool"))
            block.sync(run("sp"))

    def close(self):
        for cm in reversed(self.ctx):
            cm.__exit__(None, None, None)


def col_layout():
    idx = {}
    n = [0]

    def add(name, k):
        idx[name] = n[0]
        n[0] += k
    add("c", 8)
    add("final_g", 8)
    for l in range(L):
        add(f"norm1_g{l}", 8)
        add(f"norm2_g{l}", 8)
        add(f"ada_b{l}", 48)
        add(f"conv_w{l}", 8)
        add(f"conv_b{l}", 2)
        add(f"rg_b{l}", 2)
        add(f"ig_b{l}", 2)
        add(f"lam{l}", 2)
        add(f"hgrn_lb{l}", 3)
        add(f"mu{l}", 11)
        add(f"w0{l}", 3)
        add(f"a0{l}", 3)
        add(f"k_k{l}", 3)
        add(f"k_a{l}", 3)
        add(f"r_k{l}", 3)
        add(f"lnx_w{l}", 3)
        add(f"lnx_b{l}", 3)
        add(f"beta{l}", 8)
    return idx, n[0]


CI, NCOL = col_layout()
K_ID, K_BO, K_MINC, K_MSTR, K_MTRIL, K_IDP, K_RST = 0, 128, 256, 320, 384, 448, 512
NCONST = 512 + TB


def make_consts():
    cst = np.zeros((128, NCONST), np.float32)
    p = np.arange(128)
    cst[:, K_ID:K_ID + 128] = np.eye(128, dtype=np.float32)
    cst[:, K_BO:K_BO + 128] = (p[:, None] // 64 == p[None, :] // 64).astype(np.float32)
    j = np.arange(64)
    cst[:, K_MINC:K_MINC + 64] = ((p[:, None] % 64) <= j[None, :]).astype(np.float32)
    cst[:, K_MSTR:K_MSTR + 64] = ((p[:, None] % 64) < j[None, :]).astype(np.float32)
    cst[:, K_MTRIL:K_MTRIL + 64] = (j[None, :] < (p[:, None] % 64)).astype(np.float32)
    cst[:, K_IDP:K_IDP + 64] = ((p[:, None] % 64) == j[None, :]).astype(np.float32)
    t = np.arange(TB)
    cst[:, K_RST:K_RST + TB] = (t % CH != 0).astype(np.float32)[None, :]
    return cst


def pack_cols(inp, b):
    cols = np.zeros((128, NCOL), np.float32)

    def put(name, v):
        v = np.asarray(v, np.float32).reshape(-1, 128)
        cols[:, CI[name]:CI[name] + v.shape[0]] = v.T
    put("c", inp["c"][b])
    put("final_g", inp["final_g"])
    for l in range(L):
        put(f"norm1_g{l}", inp["norm1_g"][l])
        put(f"norm2_g{l}", inp["norm2_g"][l])
        put(f"ada_b{l}", inp["ada_b"][l])
        cw = np.asarray(inp["conv_w"][l], np.float32)
        cwp = np.stack([cw[j, cc * 128:(cc + 1) * 128] for cc in range(2) for j in range(4)], 0)
        put(f"conv_w{l}", cwp)
        put(f"conv_b{l}", inp["conv_b"][l])
        put(f"rg_b{l}", inp["rg_b"][l])
        put(f"ig_b{l}", inp["ig_b"][l])
        put(f"lam{l}", inp["lru_lam"][l])
        put(f"hgrn_lb{l}", inp["hgrn_lb"][l])
        put(f"mu{l}", inp["rwkv_mu"][l])
        put(f"w0{l}", inp["rwkv_w0"][l])
        put(f"a0{l}", inp["rwkv_a0"][l])
        put(f"k_k{l}", inp["rwkv_k_k"][l])
        put(f"k_a{l}", inp["rwkv_k_a"][l])
        put(f"r_k{l}", inp["rwkv_r_k"][l])
        put(f"lnx_w{l}", inp["rwkv_lnx_w"][l])
        put(f"lnx_b{l}", inp["rwkv_lnx_b"][l])
        put(f"beta{l}", inp["mix_beta"][l])
    return cols


def blockdiag(w):
    w = np.asarray(w, np.float32)
    out = np.zeros((L, 2, 128, 128), np.float32)
    for l in range(L):
        for g in range(4):
            cc, gg = g // 2, g % 2
            out[l, cc, gg * 64:(gg + 1) * 64, gg * 64:(gg + 1) * 64] = w[l, g]
    return out


def build_program(n_layers=L, taps=(), do_mixer=(True, True, True), do_ffn=True):
    nc = bass.Bass("TRN2", target_bir_lowering=False)
    es = ExitStack()

    def din(name, shape):
        return nc.dram_tensor(name, list(shape), F32, kind="ExternalInput").ap()
    xT_d = din("xT", [8, 128, T])
    cols_d = din("cols", [128, NCOL])
    cst_d = din("cst", [128, NCONST])
    ada_d = din("ada_w", [L, D, 6 * D])
    win_d = din("w_in", [L, D, PW])
    rgbd_d = din("rgbd", [L, 2, 128, 128])
    igbd_d = din("igbd", [L, 2, 128, 128])
    waup_d = din("wa_up", [L, 128, 384])
    gup_d = din("g_up", [L, 128, 384])
    wout_d = din("w_out", [L, D, D])
    wg_d = din("wg", [L, D, DFF])
    wu_d = din("wu", [L, D, DFF])
    wd_d = din("wd", [L, DFF, D])
    outT_d = nc.dram_tensor("outT", [8, 128, T], F32, kind="ExternalOutput").ap()
    tap_d = {}
    for name, shape in taps:
        tap_d[name] = nc.dram_tensor("tap_" + name, list(shape), F32, kind="ExternalOutput").ap()

    P = Prog(nc)

    class Tl:
        def __init__(self, name, shape, dt=F32, nd=1):
            self.t = es.enter_context(nc.sbuf_tensor(name, list(shape), dt))
            self.d = [Dep() for _ in range(nd)]
            self.D = self.d[0]

    EOBJ = {"dve": "vector", "pool": "gpsimd", "act": "scalar"}

    def tt(eng, out, in0, in1, op, R, W):
        P.op(eng, lambda e: e.tensor_tensor(out, in0, in1, op), reads=R, writes=W)

    def ts(eng, out, in0, s1, s2, op0, op1, R, W):
        if s2 is None:
            P.op(eng, lambda e: e.tensor_scalar(out, in0, s1, None, op0), reads=R, writes=W)
        else:
            P.op(eng, lambda e: e.tensor_scalar(out, in0, s1, s2, op0, op1), reads=R, writes=W)

    def stt(eng, out, in0, sc, in1, op0, op1, R, W):
        P.op(eng, lambda e: e.scalar_tensor_tensor(out, in0, sc, in1, op0, op1), reads=R, writes=W)

    def act(out, in_, func, R, W, bias=None, scale=None):
        kw = {}
        if bias is not None:
            kw["bias"] = bias
        if scale is not None:
            kw["scale"] = scale
        P.op("act", lambda e: e.activation(out, in_, func, **kw), reads=R, writes=W)

    def cp(eng, out, in_, R, W):
        if eng == "act":
            P.op("act", lambda e: e.copy(out, in_), reads=R, writes=W)
        else:
            P.op(eng, lambda e: e.tensor_copy(out, in_), reads=R, writes=W)

    def mm(out, lhsT, rhs, start, stop, R, W):
        P.op("pe", lambda e: e.matmul(out, lhsT, rhs, start=start, stop=stop), reads=R, writes=W)

    def memset(eng, ap, val, W):
        P.op(eng, lambda e: e.memset(ap, val), writes=W)

    def scan(out, d0, d1, init, R, W):
        P.op("dve", lambda e: e.tensor_tensor_scan(out, d0, d1, init, ALU.mult, ALU.add), reads=R, writes=W)

    banks = []
    for i in range(8):
        banks.append((es.enter_context(nc.psum_tensor(f"bank{i}", [128, 512], F32)), BDep()))
    bctr = [0]

    def nps():
        b = banks[bctr[0] % 8]
        bctr[0] += 1
        return b

    X = Tl("X", [128, 8, T], F32, nd=8)
    COLS = Tl("COLS", [128, NCOL])
    CST = Tl("CST", [128, NCONST])
    ident = CST.t[:, K_ID:K_ID + 128]
    bones = CST.t[:, K_BO:K_BO + 128]
    m_inc = CST.t[:, K_MINC:K_MINC + 64]
    m_str = CST.t[:, K_MSTR:K_MSTR + 64]
    m_tril = CST.t[:, K_MTRIL:K_MTRIL + 64]
    idp = CST.t[:, K_IDP:K_IDP + 64]
    rst = CST.t[:, K_RST:K_RST + TB]

    def col(name, j=0, n=1):
        return COLS.t[:, CI[name] + j:CI[name] + j + n]

    P.dma("sp", COLS.t[:], cols_d, writes=[COLS.D])
    P.dma("sp", CST.t[:], cst_d, writes=[CST.D])
    for j in range(8):
        P.dma("act", X.t[:, j, :], xT_d[j], writes=[X.d[j]])

    NSCR = 12
    scr = [Tl(f"scr{i}", [128, TB + 4]) for i in range(NSCR)]
    sctr = [0]

    def tmp():
        s = scr[sctr[0] % NSCR]
        sctr[0] += 1
        return s

    def interleave(gens):
        gens = list(gens)
        while gens:
            for g in list(gens):
                try:
                    next(g)
                except StopIteration:
                    gens.remove(g)

    CSI = Tl("CSI", [128, 8])
    MOD = Tl("MOD", [128, 48])
    ADA = Tl("ADA", [128, 32])
    DER = Tl("DER", [128, 16])

    act(CSI.t[:], col("c", 0, 8), AF.Silu, [COLS.D], [CSI.D])


    NST = 2
    NWB = 7
    wst = [Tl(f"wst{i}", [128, 8, 128]) for i in range(NST)]
    wbf = [Tl(f"wbf{i}", [128, 8, 128], BF16) for i in range(NWB)]
    wctr = [0]
    sctr2 = [0]

    def wchunk(src_ap, nk=8, cast=True):
        if not cast:
            i = sctr2[0] % NST
            sctr2[0] += 1
            P.dma("sp", wst[i].t[:, 0:nk, :], src_ap, writes=[wst[i].D])
            return wst[i]
        i = wctr[0] % NWB
        wctr[0] += 1
        P.dma("pool", wbf[i].t[:, 0:nk, :], src_ap, writes=[wbf[i].D])
        return wbf[i]

    WOUT = Tl("WOUT", [128, 8, D], BF16)
    HBF = Tl("HBF", [128, 8, TB], BF16)
    YT = Tl("YT", [128, 8, TB], BF16, nd=8)
    RSTD = Tl("RSTD", [128, 512])

    def tap(name, ap, deps):
        if name in tap_d:
            t = P.dma("sp", tap_d[name], ap, reads=deps)
            tap_toks.append(t)
    tap_toks = []

    ONES = Tl("ONES", [128, 128])
    memset("pool", ONES.t[:], 1.0, [ONES.D])
    EPS = Tl("EPS", [128, 8])
    memset("pool", EPS.t[:, 0:1], 1e-6, [EPS.D])
    memset("pool", EPS.t[:, 1:2], 1e-12, [EPS.D])
    memset("pool", EPS.t[:, 2:3], 64e-5, [EPS.D])
    memset("pool", EPS.t[:, 3:4], 1.0, [EPS.D])
    memset("pool", EPS.t[:, 4:5], 0.0, [EPS.D])
    eps6, eps12, epsgn, one_c, zero_c = EPS.t[:, 0:1], EPS.t[:, 1:2], EPS.t[:, 2:3], EPS.t[:, 3:4], EPS.t[:, 4:5]

    def norm_block(t0, ntok, Acol, Bcol, dstbf, dstdep, doff=0):
        bk, bd = nps()
        for h0 in range(0, ntok, TB):
            for j in range(8):
                s = tmp()
                act(s.t[:, :TB], X.t[:, j, t0 + h0:t0 + h0 + TB], AF.Square, [X.d[j]], [s.D])
                mm(bk[:, h0:h0 + TB], ONES.t[:], s.t[:, :TB], j == 0, j == 7, [s.D, ONES.D], [bd])
        act(RSTD.t[:, :ntok], bk[:, :ntok], AF.Ln, [bd, EPS.D], [RSTD.D], bias=eps6, scale=1.0 / D)
        act(RSTD.t[:, :ntok], RSTD.t[:, :ntok], AF.Exp, [RSTD.D], [RSTD.D], scale=-0.5)
        for j in range(8):
            for h0 in range(0, ntok, TB):
                s = tmp()
                stt("dve", s.t[:, :TB], X.t[:, j, t0 + h0:t0 + h0 + TB], Acol(j), RSTD.t[:, h0:h0 + TB],
                    ALU.mult, ALU.mult, [X.d[j], RSTD.D, ADA.D], [s.D])
                act(dstbf[:, j, doff + h0:doff + h0 + TB], s.t[:, :TB], AF.Identity, [s.D, MOD.D], [dstdep], bias=Bcol(j))

    class V:
        def __init__(self, ap, nd=1):
            self.t = ap
            self.d = [Dep() for _ in range(nd)]
            self.D = self.d[0]
    NBIG = 9728
    BIG = Tl("BIG", [128, NBIG])
    H2 = V(BIG.t[:, 0:4096].bitcast(BF16).rearrange("p (k t) -> p k t", k=8))
    ACTT = V(BIG.t[:, 4096:9728].bitcast(BF16).rearrange("p (m t) -> p m t", m=11), nd=11)

    scr2 = [V(BIG.t[:, 6060 + i * 260:6060 + (i + 1) * 260]) for i in range(12)]
    SSETS = [scr[0:4] + scr2[0:4], scr[4:8] + scr2[4:8], scr[8:12] + scr2[8:12]]
    PT = [V(BIG.t[:, i * 260:(i + 1) * 260]) for i in range(11)]
    CARRY = Tl("CARRY", [128, 11])
    HST = Tl("HST", [128, 2])
    XAH = Tl("XAH", [128, 2, 3])
    SB = [Tl(f"SB{i}", [128, 64]) for i in range(3)]
    QTs = [Tl(f"QT{i}", [128, TB], BF16) for i in range(3)]
    KTs = [Tl(f"KT{i}", [128, TB], BF16) for i in range(3)]
    QAs = [Tl(f"QA{i}", [128, TB], BF16) for i in range(3)]
    KAs = [Tl(f"KA{i}", [128, TB], BF16) for i in range(3)]
    PVBs = [Tl(f"PVB{i}", [128, TB], BF16) for i in range(3)]
    IDB = Tl("IDB", [128, 128], BF16)
    HBs = [[Tl(f"HB{j}_{i}", [128, 64], BF16) for i in range(5)] for j in range(3)]
    SNs = [Tl(f"SN{i}", [128, 64]) for i in range(3)]
    OBD = [Dep() for _ in range(3)]
    GBs = [Tl(f"GB{i}", [128, 16]) for i in range(3)]
    TOK = [V(BIG.t[:, 4908 + i * 384:4908 + (i + 1) * 384].bitcast(BF16).rearrange("p (c a b) -> p c a b", c=NCH, a=3))
           for i in range(3)]
    ZC = Tl("ZC", [128, 3, 64])
    BH = [Tl(f"BH{i}", [128, TB], BF16) for i in range(3)]
    AH = [Tl(f"AH{i}", [128, TB], BF16) for i in range(3)]
    RH = [Tl(f"RH{i}", [128, TB], BF16) for i in range(3)]
    KH = [Tl(f"KH{i}", [128, TB], BF16) for i in range(3)]
    VB = [Tl(f"VB{i}", [128, TB], BF16) for i in range(3)]
    ZB = Tl("ZB", [128, 3, 64], BF16)
    SGG = V(BIG.t[:, 4396:4652])
    TW = V(BIG.t[:, 4652:4908])
    BON = [V(BIG.t[:, 3628 + i * 256:3628 + (i + 1) * 256]) for i in range(3)]
    YC = V(BIG.t[:, 2860:3628].rearrange("p (a t) -> p a t", a=3))
    GAM = Tl("GAM", [128, 3, NCH])
    MQ = [[Tl(f"MQ{c}_{i}", [128, 3, 64], BF16) for i in range(2)] for c in range(NCH)]
    MP = [[Tl(f"MP{c}_{i}", [128, 3, 64], BF16) for i in range(2)] for c in range(NCH)]
    MT = [Tl(f"MT{i}", [128, 3, 64], BF16) for i in range(NCH)]
    MAK = [Tl(f"MAK{i}", [128, 3, 64], BF16) for i in range(NCH)]
    MRB = [Tl(f"MRB{i}", [128, 3, 64], BF16) for i in range(NCH)]
    MRK = [Tl(f"MRK{i}", [128, 3, 64], BF16) for i in range(NCH)]
    XS = Tl("XS", [128, 3, 64], BF16); US = Tl("US", [128, 3, 64], BF16); ZTMP = Tl("ZTMP", [128, 3, 64])
    RGBD = Tl("RGBD", [128, 2, 128]); IGBD = Tl("IGBD", [128, 2, 128])
    WAUP = Tl("WAUP", [128, 384]); GUP = Tl("GUP", [128, 384])

    mqctr = [0]
    cp("dve", IDB.t[:], ident, [CST.D], [IDB.D])
    identb = IDB.t

    def bc3(ap2):
        return ap2.unsqueeze(1).to_broadcast([128, 3, 64])

    for l in range(n_layers):
        if l == 0:
            memset("dve", DER.t[:, 2:5], 0.0, [DER.D])
        else:
            tt("dve", DER.t[:, 2:5], col("hgrn_lb1", 0, 3), col("hgrn_lb0", 0, 3), ALU.subtract, [COLS.D], [DER.D])
            act(DER.t[:, 2:5], DER.t[:, 2:5], AF.Sigmoid, [DER.D], [DER.D])
        ts("dve", DER.t[:, 5:8], DER.t[:, 2:5], -1.0, 1.0, ALU.mult, ALU.add, [DER.D], [DER.D])
        act(DER.t[:, 0:2], col(f"lam{l}", 0, 2), AF.Exp, [COLS.D], [DER.D], scale=-1.0)
        act(DER.t[:, 0:2], DER.t[:, 0:2], AF.Ln, [DER.D, EPS.D], [DER.D], bias=one_c)
        ts("dve", DER.t[:, 0:2], DER.t[:, 0:2], -8.0, None, ALU.mult, None, [DER.D], [DER.D])
        lbc = lambda hp: DER.t[:, 2 + hp:3 + hp]
        omlc = lambda hp: DER.t[:, 5 + hp:6 + hp]
        nsp8 = lambda cc: DER.t[:, cc:cc + 1]

        P.dma("sp", RGBD.t[:], rgbd_d[l].rearrange("c p n -> p c n"), writes=[RGBD.D])
        P.dma("sp", IGBD.t[:], igbd_d[l].rearrange("c p n -> p c n"), writes=[IGBD.D])
        P.dma("sp", WAUP.t[:], waup_d[l], writes=[WAUP.D])
        P.dma("sp", GUP.t[:], gup_d[l], writes=[GUP.D])

        adv = ada_d[l].rearrange("(k p) m -> p k m", p=128)
        mbk, mbd = nps()
        for j in range(48):
            st = wchunk(adv[:, :, j * 128:(j + 1) * 128], cast=False)
            for k in range(8):
                mm(mbk[:, j:j + 1], st.t[:, k, :], CSI.t[:, k:k + 1], k == 0, k == 7, [st.D, CSI.D], [mbd])
        tt("dve", MOD.t[:], mbk[:, 0:48], col(f"ada_b{l}", 0, 48), ALU.add, [mbd, COLS.D], [MOD.D])
        stt("dve", ADA.t[:, 0:8], MOD.t[:, 8:16], 1.0, col(f"norm1_g{l}", 0, 8), ALU.add, ALU.mult, [MOD.D, COLS.D], [ADA.D])
        stt("dve", ADA.t[:, 8:16], MOD.t[:, 32:40], 1.0, col(f"norm2_g{l}", 0, 8), ALU.add, ALU.mult, [MOD.D, COLS.D], [ADA.D])
        tap(f"mod{l}", MOD.t[:], [MOD.D])

        wov = wout_d[l].rearrange("(k p) n -> p k n", p=128)
        for f in range(8):
            wb = wchunk(wov[:, :, f * 128:(f + 1) * 128])
            cp("act", WOUT.t[:, :, f * 128:(f + 1) * 128], wb.t[:], [wb.D], [WOUT.D])

        wiv = win_d[l].rearrange("(k p) n -> p k n", p=128)

        def project(chunk, dst_ap, dst_deps, evac="act"):
            wb = wchunk(wiv[:, :, chunk * 128:(chunk + 1) * 128])
            bk, bd = nps()
            for k in range(8):
                mm(bk[:, :TB], wb.t[:, k, :], HBF.t[:, k, :], k == 0, k == 7, [wb.D, HBF.D], [bd])
            cp(evac, dst_ap, bk[:, :TB], [bd], dst_deps)

        memset("dve", HST.t[:], 0.0, [HST.D])
        memset("dve", CARRY.t[:], 0.0, [CARRY.D])
        for hp in range(3):
            memset("dve", SB[hp].t[:], 0.0, [SB[hp].D])
        memset("dve", ZC.t[:], 0.0, [ZC.D])
        memset("dve", ZB.t[:], 0.0, [ZB.D])
        memset("dve", XAH.t[:], 0.0, [XAH.D])

        for tb in range(NTB):
            t0 = tb * TB
            norm_block(t0, TB, lambda j: ADA.t[:, j:j + 1], lambda j: MOD.t[:, j:j + 1], HBF.t, HBF.D)

            if do_mixer[0]:
                def _gen_mixa(cc):
                    S = SSETS[cc]
                    XA, YA = PT[cc], PT[2 + cc]
                    yield
                    project(cc, XA.t[:, 3:3 + TB], [XA.D])
                    yield
                    cp("act", XA.t[:, 0:3], XAH.t[:, cc, :], [XAH.D], [XA.D])
                    yield
                    cp("act", XAH.t[:, cc, :], XA.t[:, TB:TB + 3], [XA.D], [XAH.D])
                    yield
                    project(2 + cc, YA.t[:, :TB], [YA.D], evac="dve")
                    yield
                    u = S[0]
                    yield
                    cw = lambda j: col(f"conv_w{l}", cc * 4 + j)
                    yield
                    ts("dve", u.t[:, :TB], XA.t[:, 0:TB], cw(0), col(f"conv_b{l}", cc), ALU.mult, ALU.add, [XA.D, COLS.D], [u.D])
                    yield
                    for j in range(1, 4):
                        stt("dve", u.t[:, :TB], XA.t[:, j:j + TB], cw(j), u.t[:, :TB], ALU.mult, ALU.add, [XA.D, COLS.D, u.D], [u.D])
                    rb, rbd = nps()
                    yield
                    mm(rb[:, :TB], RGBD.t[:, cc, :], u.t[:, :TB], True, True, [RGBD.D, u.D], [rbd])
                    yield
                    ib, ibd = nps()
                    yield
                    mm(ib[:, :TB], IGBD.t[:, cc, :], u.t[:, :TB], True, True, [IGBD.D, u.D], [ibd])
                    yield
                    r = S[1]; ig = S[2]
                    yield
                    act(r.t[:, :TB], rb[:, :TB], AF.Sigmoid, [rbd, COLS.D], [r.D], bias=col(f"rg_b{l}", cc))
                    yield
                    act(ig.t[:, :TB], ib[:, :TB], AF.Sigmoid, [ibd, COLS.D], [ig.D], bias=col(f"ig_b{l}", cc))
                    yield
                    a = S[3]
                    yield
                    act(a.t[:, :TB], r.t[:, :TB], AF.Exp, [r.D, DER.D], [a.D], scale=nsp8(cc))
                    yield
                    m = S[4]
                    yield
                    act(m.t[:, :TB], a.t[:, :TB], AF.Square, [a.D], [m.D])
                    yield
                    act(m.t[:, :TB], m.t[:, :TB], AF.Sqrt, [m.D, EPS.D], [m.D], bias=one_c, scale=-1.0)
                    yield
                    if tb == 0:
                        memset("dve", m.t[:, 0:1], 1.0, [m.D])
                    tt("dve", m.t[:, :TB], m.t[:, :TB], ig.t[:, :TB], ALU.mult, [m.D, ig.D], [m.D])
                    yield
                    tt("dve", m.t[:, :TB], m.t[:, :TB], u.t[:, :TB], ALU.mult, [m.D, u.D], [m.D])
                    yield
                    h = S[5]
                    yield
                    scan(h.t[:, :TB], a.t[:, :TB], m.t[:, :TB], HST.t[:, cc:cc + 1], [a.D, m.D, HST.D], [h.D])
                    yield
                    cp("dve", HST.t[:, cc:cc + 1], h.t[:, TB - 1:TB], [h.D], [HST.D])
                    yield
                    g1 = S[6]
                    yield
                    act(g1.t[:, :TB], YA.t[:, :TB], AF.Square, [YA.D], [g1.D])
                    yield
                    ts("dve", g1.t[:, :TB], g1.t[:, :TB], 0.044715, 1.0, ALU.mult, ALU.add, [g1.D], [g1.D])
                    yield
                    tt("dve", g1.t[:, :TB], g1.t[:, :TB], YA.t[:, :TB], ALU.mult, [g1.D, YA.D], [g1.D])
                    yield
                    act(g1.t[:, :TB], g1.t[:, :TB], AF.Sigmoid, [g1.D], [g1.D], scale=1.5957691216057308)
                    yield
                    tt("dve", g1.t[:, :TB], g1.t[:, :TB], YA.t[:, :TB], ALU.mult, [g1.D, YA.D], [g1.D])
                    yield
                    tt("dve", h.t[:, :TB], h.t[:, :TB], g1.t[:, :TB], ALU.mult, [h.D, g1.D], [h.D])
                    yield
                    sq = S[7]
                    yield
                    act(sq.t[:, :TB], h.t[:, :TB], AF.Square, [h.D], [sq.D])
                    yield
                    sb_, sbd = nps()
                    yield
                    mm(sb_[:, :TB], bones, sq.t[:, :TB], True, True, [CST.D, sq.D], [sbd])
                    yield
                    act(sq.t[:, :TB], sb_[:, :TB], AF.Ln, [sbd, EPS.D], [sq.D], bias=eps6, scale=1.0 / 64)
                    yield
                    act(sq.t[:, :TB], sq.t[:, :TB], AF.Exp, [sq.D], [sq.D], scale=-0.5)
                    yield
                    stt("dve", YT.t[:, cc, :], h.t[:, :TB], col(f"beta{l}", cc), sq.t[:, :TB], ALU.mult, ALU.mult,
                        [h.D, sq.D, COLS.D], [YT.d[cc]])
                    yield
                interleave([_gen_mixa(_i) for _i in range(2)])
            else:
                for cc in range(2):
                    memset("dve", YT.t[:, cc, :], 0.0, [YT.d[cc]])

            if not do_mixer[1]:
                for hp in range(3):
                    memset("dve", YT.t[:, 2 + hp, :], 0.0, [YT.d[2 + hp]])
            else:
                PGs = [PT[6], PT[7], PT[8]]
                def _gen_bprep(hp):
                    S = SSETS[hp]
                    Pq, Pf, Pg = PT[hp], PT[3 + hp], PGs[hp]
                    yield
                    qa, ka, QT, KT, PVB, GB = QAs[hp], KAs[hp], QTs[hp], KTs[hp], PVBs[hp], GBs[hp]
                    yield
                    project(4 + hp, Pq.t[:, :TB], [Pq.D])
                    yield
                    project(7 + hp, Pf.t[:, :TB], [Pf.D], evac="dve")
                    yield
                    project(10 + hp, PVB.t[:], [PVB.D])
                    yield
                    project(13 + hp, Pg.t[:, :TB], [Pg.D], evac="dve")
                    yield
                    s1 = S[0]; kg = S[1]; bc = S[2]; dd = S[3]; e1 = S[4]
                    yield
                    act(s1.t[:, :TB], Pf.t[:, :TB], AF.Sigmoid, [Pf.D], [s1.D])
                    yield
                    act(s1.t[:, :TB], s1.t[:, :TB], AF.Identity, [s1.D, DER.D], [s1.D], bias=lbc(hp), scale=omlc(hp))
                    yield
                    act(s1.t[:, :TB], s1.t[:, :TB], AF.Ln, [s1.D], [s1.D])
                    yield
                    act(kg.t[:, :TB], Pf.t[:, :TB], AF.Sigmoid, [Pf.D], [kg.D], scale=-1.0)
                    yield
                    ts("dve", kg.t[:, :TB], kg.t[:, :TB], omlc(hp), None, ALU.mult, None, [kg.D, DER.D], [kg.D])
                    yield
                    scan(bc.t[:, :TB], rst, s1.t[:, :TB], zero_c, [CST.D, s1.D, EPS.D], [bc.D])
                    yield
                    bc3v = bc.t[:, :TB].rearrange("p (c s) -> p c s", s=CH)
                    yield
                    bmid = bc3v[:, :, 31:32]
                    yield
                    blast = bc3v[:, :, 63:64]
                    yield
                    act(e1.t[:, :TB], bc.t[:, :TB], AF.Exp, [bc.D], [e1.D])
                    yield
                    stt("dve", qa.t[:], Pq.t[:, :TB], 0.125, e1.t[:, :TB], ALU.mult, ALU.mult, [Pq.D, e1.D], [qa.D])
                    yield
                    ts("dve", e1.t[:, :TB], bc.t[:, :TB], -1.0, 60.0, ALU.mult, ALU.min, [bc.D], [e1.D])
                    yield
                    act(e1.t[:, :TB], e1.t[:, :TB], AF.Exp, [e1.D], [e1.D])
                    yield
                    tt("dve", ka.t[:], kg.t[:, :TB], e1.t[:, :TB], ALU.mult, [kg.D, e1.D], [ka.D])
                    yield
                    tt("dve", dd.t[:, :TB].rearrange("p (c s) -> p c s", s=CH), bc3v, bmid.to_broadcast([128, NCH, CH]),
                       ALU.subtract, [bc.D], [dd.D])
                    act(e1.t[:, :TB], dd.t[:, :TB], AF.Exp, [dd.D], [e1.D])
                    yield
                    stt("dve", QT.t[:], Pq.t[:, :TB], 0.125, e1.t[:, :TB], ALU.mult, ALU.mult, [Pq.D, e1.D], [QT.D])
                    yield
                    act(e1.t[:, :TB], dd.t[:, :TB], AF.Exp, [dd.D], [e1.D], scale=-1.0)
                    yield
                    tt("dve", KT.t[:], kg.t[:, :TB], e1.t[:, :TB], ALU.mult, [kg.D, e1.D], [KT.D])
                    yield
                    gbv = GB.t[:, 0:12].rearrange("p (a c) -> p a c", c=NCH)
                    yield
                    act(gbv[:, 0, :].unsqueeze(2), bmid, AF.Exp, [bc.D], [GB.D])
                    yield
                    act(gbv[:, 1, :].unsqueeze(2), blast, AF.Exp, [bc.D], [GB.D])
                    yield
                    tt("dve", gbv[:, 2, :].unsqueeze(2), blast, bmid, ALU.subtract, [bc.D], [GB.D])
                    yield
                    act(gbv[:, 2, :], gbv[:, 2, :], AF.Exp, [GB.D], [GB.D])
                    yield
                    yield
                interleave([_gen_bprep(_i) for _i in range(3)])
                for c in range(NCH):
                    cs_ = slice(c * CH, (c + 1) * CH)
                    ca_ = slice(c * CH, c * CH + 32)
                    cb_ = slice(c * CH + 32, (c + 1) * CH)
                    abs_ = []
                    for hp in range(3):
                        qa, ka, QT, KT, PVB = QAs[hp], KAs[hp], QTs[hp], KTs[hp], PVBs[hp]
                        ab, abd = nps()
                        abs_.append((ab, abd))
                        for hh in range(2):
                            ps_ = slice(hh * 64, hh * 64 + 64)
                            mm(ab[ps_, 0:32], ka.t[ps_, cs_], qa.t[ps_, ca_], True, True, [ka.D, qa.D], [abd])
                            mm(ab[ps_, 32:64], KT.t[ps_, cs_], QT.t[ps_, cb_], True, True, [KT.D, QT.D], [abd])
                            mm(ab[ps_, 64:128], PVB.t[ps_, cs_], identb[ps_, ps_], True, True, [PVB.D, IDB.D], [abd])
                            mm(ab[ps_, 128:192], KT.t[ps_, cs_], identb[ps_, ps_], True, True, [KT.D, IDB.D], [abd])
                    for hp in range(3):
                        ATS, VTK, KTK, STL, SBb = HBs[hp]
                        ab, abd = abs_[hp]
                        tt("dve", ATS.t[:], ab[:, 0:64], m_inc, ALU.mult, [abd, CST.D], [ATS.D])
                        cp("act", VTK.t[:], ab[:, 64:128], [abd], [VTK.D])
                        cp("act", KTK.t[:], ab[:, 128:192], [abd], [KTK.D])
                        ts("dve", STL.t[:], SB[hp].t[:], GBs[hp].t[:, c:c + 1], None, ALU.mult, None, [SB[hp].D, GBs[hp].D], [STL.D])
                        cp("act", SBb.t[:], SB[hp].t[:], [SB[hp].D], [SBb.D])
                    obs_ = []
                    for hp in range(3):
                        ATS, VTK, KTK, STL, SBb = HBs[hp]
                        qa, QT = QAs[hp], QTs[hp]
                        ob, obd = nps()
                        obs_.append((ob, obd))
                        for hh in range(2):
                            ps_ = slice(hh * 64, hh * 64 + 64)
                            mm(ob[ps_, 0:64], VTK.t[ps_, :], ATS.t[ps_, :], True, False, [VTK.D, ATS.D], [obd])
                            mm(ob[ps_, 0:32], SBb.t[ps_, :], qa.t[ps_, ca_], False, False, [SBb.D, qa.D], [obd])
                            mm(ob[ps_, 32:64], STL.t[ps_, :], QT.t[ps_, cb_], False, True, [STL.D, QT.D], [obd])
                            mm(ob[ps_, 64:128], KTK.t[ps_, :], VTK.t[ps_, :], True, True, [KTK.D, VTK.D], [obd])
                    for hp in range(3):
                        ob, obd = obs_[hp]
                        GB = GBs[hp]
                        cp("act", YC.t[:, hp, cs_], ob[:, 0:64], [obd], [OBD[hp]])
                        ts("dve", SNs[hp].t[:], ob[:, 64:128], GB.t[:, 8 + c:9 + c], None, ALU.mult, None, [obd, GB.D], [SNs[hp].D])
                        stt("dve", SB[hp].t[:], SB[hp].t[:], GB.t[:, 4 + c:5 + c], SNs[hp].t[:], ALU.mult, ALU.add,
                            [SB[hp].D, GB.D, SNs[hp].D], [SB[hp].D])
                def _gen_bepi(hp):
                    S = SSETS[hp]
                    Pg = PGs[hp]
                    yield
                    OBt = YC.t[:, hp, :]
                    yield
                    sq = S[0]
                    yield
                    act(sq.t[:, :TB], OBt, AF.Square, [OBD[hp]], [sq.D])
                    yield
                    sb_, sbd = nps()
                    yield
                    mm(sb_[:, :TB], bones, sq.t[:, :TB], True, True, [CST.D, sq.D], [sbd])
                    yield
                    act(sq.t[:, :TB], sb_[:, :TB], AF.Ln, [sbd, EPS.D], [sq.D], bias=eps6, scale=1.0 / 64)
                    yield
                    act(sq.t[:, :TB], sq.t[:, :TB], AF.Exp, [sq.D], [sq.D], scale=-0.5)
                    yield
                    sl = S[1]
                    yield
                    act(sl.t[:, :TB], Pg.t[:, :TB], AF.Silu, [Pg.D], [sl.D])
                    yield
                    tt("dve", sq.t[:, :TB], sq.t[:, :TB], OBt, ALU.mult, [sq.D, OBD[hp]], [sq.D])
                    yield
                    stt("dve", YT.t[:, 2 + hp, :], sq.t[:, :TB], col(f"beta{l}", 2 + hp), sl.t[:, :TB], ALU.mult, ALU.mult,
                        [sq.D, sl.D, COLS.D], [YT.d[2 + hp]])

                    yield
                interleave([_gen_bepi(_i) for _i in range(3)])
            if do_mixer[2]:
                for i in range(11):
                    Xt = PT[i]
                    project(16 + i, Xt.t[:, 1:1 + TB], [Xt.D], evac=("act" if i % 2 == 0 else "dve"))
                    cp("act", Xt.t[:, 0:1], CARRY.t[:, i:i + 1], [CARRY.D], [Xt.D])
                    cp("act", CARRY.t[:, i:i + 1], Xt.t[:, TB:TB + 1], [Xt.D], [CARRY.D])
                    d_ = tmp()
                    tt("dve", d_.t[:, :TB], Xt.t[:, 0:TB], Xt.t[:, 1:1 + TB], ALU.subtract, [Xt.D], [d_.D])
                    stt("dve", Xt.t[:, 1:1 + TB], d_.t[:, :TB], col(f"mu{l}", i), Xt.t[:, 1:1 + TB], ALU.mult, ALU.add,
                        [d_.D, Xt.D, COLS.D], [Xt.D])
                sR = [PT[i] for i in range(0, 3)]
                sK = [PT[i] for i in range(3, 6)]
                sV = [PT[i] for i in range(6, 9)]
                sWA, sG = PT[9], PT[10]
                V1 = lambda t_: t_.t[:, 1:1 + TB]
                tw = TW
                act(tw.t[0:64, :TB], sWA.t[0:64, 1:1 + TB], AF.Tanh, [sWA.D], [tw.D])
                act(SGG.t[:], V1(sG), AF.Sigmoid, [sG.D], [SGG.D])
                def _gen_cprep(hp):
                    S = SSETS[hp]
                    R_, K_, V_ = sR[hp], sK[hp], sV[hp]
                    yield
                    kk = S[0]; sq = S[1]; k2 = S[2]; e_ = S[3]; cum = S[4]; ex = S[5]; lw = S[6]; aa = S[7]
                    yield
                    hc = slice(hp * 128, (hp + 1) * 128)
                    yield
                    zb, zbd = nps()
                    yield
                    mm(zb[:, :TB], WAUP.t[0:64, hc], tw.t[0:64, :TB], True, True, [WAUP.D, tw.D], [zbd])
                    yield
                    act(lw.t[:, :TB], zb[:, :TB], AF.Sigmoid, [zbd, COLS.D], [lw.D], bias=col(f"w0{l}", hp))
                    yield
                    ts("dve", lw.t[:, :TB], lw.t[:, :TB], -0.6065306597126334, None, ALU.mult, None, [lw.D], [lw.D])
                    yield
                    ab_, abd_ = nps()
                    yield
                    mm(ab_[:, :TB], WAUP.t[64:128, hc], sWA.t[64:128, 1:1 + TB], True, True, [WAUP.D, sWA.D], [abd_])
                    yield
                    act(aa.t[:, :TB], ab_[:, :TB], AF.Sigmoid, [abd_, COLS.D], [aa.D], bias=col(f"a0{l}", hp))
                    yield
                    ts("dve", kk.t[:, :TB], V1(K_), col(f"k_k{l}", hp), None, ALU.mult, None, [K_.D, COLS.D], [kk.D])
                    yield
                    act(sq.t[:, :TB], kk.t[:, :TB], AF.Square, [kk.D], [sq.D])
                    yield
                    nb_, nbd_ = nps()
                    yield
                    mm(nb_[:, :TB], bones, sq.t[:, :TB], True, True, [CST.D, sq.D], [nbd_])
                    yield
                    act(sq.t[:, :TB], nb_[:, :TB], AF.Ln, [nbd_, EPS.D], [sq.D], bias=eps12)
                    yield
                    act(sq.t[:, :TB], sq.t[:, :TB], AF.Exp, [sq.D], [sq.D], scale=-0.5)
                    yield
                    tt("dve", kk.t[:, :TB], kk.t[:, :TB], sq.t[:, :TB], ALU.mult, [kk.D, sq.D], [kk.D])
                    yield
                    ts("dve", k2.t[:, :TB], aa.t[:, :TB], -1.0, col(f"k_a{l}", hp), ALU.add, ALU.mult, [aa.D, COLS.D], [k2.D])
                    yield
                    stt("dve", k2.t[:, :TB], k2.t[:, :TB], 1.0, V1(K_), ALU.add, ALU.mult, [k2.D, K_.D], [k2.D])
                    yield
                    tt("dve", e_.t[:, :TB], V1(R_), k2.t[:, :TB], ALU.mult, [R_.D, k2.D], [e_.D])
                    yield
                    ts("dve", e_.t[:, :TB], e_.t[:, :TB], col(f"r_k{l}", hp), None, ALU.mult, None, [e_.D, COLS.D], [e_.D])
                    yield
                    bb_, bbd_ = nps()
                    yield
                    mm(bb_[:, :TB], bones, e_.t[:, :TB], True, True, [CST.D, e_.D], [bbd_])
                    yield
                    tt("dve", BON[hp].t[:], bb_[:, :TB], V1(V_), ALU.mult, [bbd_, V_.D], [BON[hp].D])
                    yield
                    scan(cum.t[:, :TB], rst, lw.t[:, :TB], zero_c, [CST.D, lw.D, EPS.D], [cum.D])
                    yield
                    act(ex.t[:, :TB], cum.t[:, :TB], AF.Exp, [cum.D], [ex.D])
                    yield
                    cp("act", GAM.t[:, hp, :].unsqueeze(2), ex.t[:, :TB].rearrange("p (c s) -> p c s", s=CH)[:, :, 63:64], [ex.D], [GAM.D])
                    yield
                    tt("dve", RH[hp].t[:], V1(R_), ex.t[:, :TB], ALU.mult, [R_.D, ex.D], [RH[hp].D])
                    yield
                    cp("act", VB[hp].t[:], V1(V_), [V_.D], [VB[hp].D])
                    yield
                    act(ex.t[:, :TB], cum.t[:, :TB], AF.Exp, [cum.D], [ex.D], scale=-1.0)
                    yield
                    tt("dve", KH[hp].t[:], k2.t[:, :TB], ex.t[:, :TB], ALU.mult, [k2.D, ex.D], [KH[hp].D])
                    yield
                    tt("dve", e_.t[:, :TB], kk.t[:, :TB], aa.t[:, :TB], ALU.mult, [kk.D, aa.D], [e_.D])
                    yield
                    tt("dve", BH[hp].t[:], e_.t[:, :TB], ex.t[:, :TB], ALU.mult, [e_.D, ex.D], [BH[hp].D])
                    yield
                    tt("dve", cum.t[:, :TB], cum.t[:, :TB], lw.t[:, :TB], ALU.subtract, [cum.D, lw.D], [cum.D])
                    yield
                    act(ex.t[:, :TB], cum.t[:, :TB], AF.Exp, [cum.D], [ex.D])
                    yield
                    stt("dve", AH[hp].t[:], kk.t[:, :TB], -1.0, ex.t[:, :TB], ALU.mult, ALU.mult, [kk.D, ex.D], [AH[hp].D])
                    yield
                    yield
                interleave([_gen_cprep(_i) for _i in range(3)])
                for c in range(NCH):
                    cs1 = slice(1 + c * CH, 1 + (c + 1) * CH)
                    cs_ = slice(c * CH, (c + 1) * CH)
                    srcs = [(VB, False), (BH, False), (KH, False)]
                    for qi, (src, halo) in enumerate(srcs):
                        tb_, tbd_ = nps()
                        for hp in range(3):
                            for hh in range(2):
                                ps_ = slice(hh * 64, hh * 64 + 64)
                                sap = src[hp].t[ps_, cs1] if halo else src[hp].t[ps_, cs_]
                                mm(tb_[ps_, hp * 64:(hp + 1) * 64], sap, identb[ps_, ps_], True, True, [src[hp].D, IDB.D], [tbd_])
                        cp("act" if qi != 1 else "dve", TOK[qi].t[:, c, :, :], tb_[:, 0:192].rearrange("p (a b) -> p a b", b=64),
                           [tbd_], [TOK[qi].D])
                cur = [None] * NCH
                for c in range(NCH):
                    cs_ = slice(c * CH, (c + 1) * CH)
                    q0 = MQ[c][0]; p0 = MP[c][0]
                    specs = [
                        (q0, lambda hp, ps_: BH[hp].t[ps_, cs_], lambda hp, ps_: AH[hp].t[ps_, cs_], m_str, lambda hp: [BH[hp].D, AH[hp].D]),
                        (p0, lambda hp, ps_: AH[hp].t[ps_, cs_], lambda hp, ps_: BH[hp].t[ps_, cs_], m_tril, lambda hp: [BH[hp].D, AH[hp].D]),
                        (MAK[c], lambda hp, ps_: KH[hp].t[ps_, cs_], lambda hp, ps_: AH[hp].t[ps_, cs_], m_str, lambda hp: [KH[hp].D, AH[hp].D]),
                        (MRB[c], lambda hp, ps_: BH[hp].t[ps_, cs_], lambda hp, ps_: RH[hp].t[ps_, cs_], m_inc, lambda hp: [BH[hp].D, RH[hp].D]),
                        (MRK[c], lambda hp, ps_: KH[hp].t[ps_, cs_], lambda hp, ps_: RH[hp].t[ps_, cs_], m_inc, lambda hp: [KH[hp].D, RH[hp].D]),
                    ]
                    for si, (dst, lf, rf, msk, dps) in enumerate(specs):
                        b_, bd_ = nps()
                        for hp in range(3):
                            for hh in range(2):
                                ps_ = slice(hh * 64, hh * 64 + 64)
                                mm(b_[ps_, hp * 64:(hp + 1) * 64], lf(hp, ps_), rf(hp, ps_), True, True, dps(hp), [bd_])
                        tt("dve", dst.t[:], b_[:, 0:192].rearrange("p (a b) -> p a b", b=64), bc3(msk), ALU.mult, [bd_, CST.D], [dst.D])
                    tt("dve", MT[c].t[:], q0.t[:], bc3(idp), ALU.add, [q0.D, CST.D], [MT[c].D])
                    cur[c] = (q0, p0)
                for lev in range(1, 6):
                    pbs = []
                    for c in range(NCH):
                        qc, pc = cur[c]
                        pb_, pbd_ = nps()
                        pbs.append((pb_, pbd_))
                        for hp in range(3):
                            for hh in range(2):
                                ps_ = slice(hh * 64, hh * 64 + 64)
                                mm(pb_[ps_, hp * 64:(hp + 1) * 64], qc.t[ps_, hp, :], pc.t[ps_, hp, :], True, True, [qc.D, pc.D], [pbd_])
                                if lev < 5:
                                    mm(pb_[ps_, 192 + hp * 64:192 + (hp + 1) * 64], pc.t[ps_, hp, :], qc.t[ps_, hp, :], True, True,
                                       [qc.D, pc.D], [pbd_])
                    for c in range(NCH):
                        pb_, pbd_ = pbs[c]
                        qn = MQ[c][lev % 2]; pn = MP[c][lev % 2]
                        cp("act", pn.t[:], pb_[:, 0:192].rearrange("p (a b) -> p a b", b=64), [pbd_], [pn.D])
                        if lev < 5:
                            cp("dve", qn.t[:], pb_[:, 192:384].rearrange("p (a b) -> p a b", b=64), [pbd_], [qn.D])
                        cur[c] = (qn, pn)
                    tbs = []
                    for c in range(NCH):
                        qn, pn = cur[c]
                        tb2, tbd2 = nps()
                        tbs.append((tb2, tbd2))
                        for hp in range(3):
                            for hh in range(2):
                                ps_ = slice(hh * 64, hh * 64 + 64)
                                mm(tb2[ps_, hp * 64:(hp + 1) * 64], pn.t[ps_, hp, :], MT[c].t[ps_, hp, :], True, True, [pn.D, MT[c].D], [tbd2])
                    for c in range(NCH):
                        tb2, tbd2 = tbs[c]
                        tt("dve", MT[c].t[:], MT[c].t[:], tb2[:, 0:192].rearrange("p (a b) -> p a b", b=64), ALU.add, [MT[c].D, tbd2], [MT[c].D])
                for c in range(NCH):
                    cs1 = slice(1 + c * CH, 1 + (c + 1) * CH)
                    cs_ = slice(c * CH, (c + 1) * CH)
                    xb, xbd = nps()
                    for hp in range(3):
                        for hh in range(2):
                            ps_ = slice(hh * 64, hh * 64 + 64)
                            o_ = xb[ps_, hp * 64:(hp + 1) * 64]
                            mm(o_, AH[hp].t[ps_, cs_], ZB.t[ps_, hp, :], True, False, [AH[hp].D, ZB.D], [xbd])
                            mm(o_, MAK[c].t[ps_, hp, :], TOK[0].t[ps_, c, hp, :], False, True, [MAK[c].D, TOK[0].D], [xbd])
                    cp("act", XS.t[:], xb[:, 0:192].rearrange("p (a b) -> p a b", b=64), [xbd], [XS.D])
                    ub, ubd = nps()
                    for hp in range(3):
                        for hh in range(2):
                            ps_ = slice(hh * 64, hh * 64 + 64)
                            mm(ub[ps_, hp * 64:(hp + 1) * 64], MT[c].t[ps_, hp, :], XS.t[ps_, hp, :], True, True, [MT[c].D, XS.D], [ubd])
                    cp("dve", US.t[:], ub[:, 0:192].rearrange("p (a b) -> p a b", b=64), [ubd], [US.D])
                    yb, ybd = nps()
                    znb, znd = nps()
                    for hp in range(3):
                        for hh in range(2):
                            ps_ = slice(hh * 64, hh * 64 + 64)
                            o2 = znb[ps_, hp * 64:(hp + 1) * 64]
                            mm(o2, TOK[1].t[ps_, c, hp, :], US.t[ps_, hp, :], True, False, [TOK[1].D, US.D], [znd])
                            mm(o2, TOK[2].t[ps_, c, hp, :], TOK[0].t[ps_, c, hp, :], False, True, [TOK[2].D, TOK[0].D], [znd])
                    for hp in range(3):
                        for hh in range(2):
                            ps_ = slice(hh * 64, hh * 64 + 64)
                            o_ = yb[ps_, hp * 64:(hp + 1) * 64]
                            mm(o_, ZB.t[ps_, hp, :], RH[hp].t[ps_, cs_], True, False, [ZB.D, RH[hp].D], [ybd])
                            mm(o_, US.t[ps_, hp, :], MRB[c].t[ps_, hp, :], False, False, [US.D, MRB[c].D], [ybd])
                            mm(o_, TOK[0].t[ps_, c, hp, :], MRK[c].t[ps_, hp, :], False, True, [TOK[0].D, MRK[c].D], [ybd])
                    cp("act", YC.t[:, :, cs_], yb[:, 0:192].rearrange("p (a b) -> p a b", b=64), [ybd], [YC.D] + OBD)
                    tt("dve", ZTMP.t[:], znb[:, 0:192].rearrange("p (a b) -> p a b", b=64), ZC.t[:], ALU.add, [znd, ZC.D], [ZTMP.D])
                    tt("dve", ZB.t[:], ZTMP.t[:], GAM.t[:, :, c:c + 1].to_broadcast([128, 3, 64]), ALU.mult, [ZTMP.D, GAM.D], [ZB.D])
                    tt("dve", ZC.t[:], ZTMP.t[:], GAM.t[:, :, c:c + 1].to_broadcast([128, 3, 64]), ALU.mult, [ZTMP.D, GAM.D], [ZC.D])
                def _gen_cgn(hp):
                    S = SSETS[hp]
                    mb_, mbd_ = nps()
                    yield
                    mm(mb_[:, :TB], bones, YC.t[:, hp, :], True, True, [CST.D, YC.D, OBD[hp]], [mbd_])
                    yield
                    yc = S[0]; sq = S[1]
                    yield
                    stt("dve", yc.t[:, :TB], mb_[:, :TB], -1.0 / 64, YC.t[:, hp, :], ALU.mult, ALU.add, [mbd_, YC.D, OBD[hp]], [yc.D])
                    yield
                    act(sq.t[:, :TB], yc.t[:, :TB], AF.Square, [yc.D], [sq.D])
                    yield
                    vb_, vbd_ = nps()
                    yield
                    mm(vb_[:, :TB], bones, sq.t[:, :TB], True, True, [CST.D, sq.D], [vbd_])
                    yield
                    act(sq.t[:, :TB], vb_[:, :TB], AF.Ln, [vbd_, EPS.D], [sq.D], bias=epsgn, scale=1.0 / 64)
                    yield
                    act(sq.t[:, :TB], sq.t[:, :TB], AF.Exp, [sq.D], [sq.D], scale=-0.5)
                    yield
                    tt("dve", yc.t[:, :TB], yc.t[:, :TB], sq.t[:, :TB], ALU.mult, [yc.D, sq.D], [yc.D])
                    yield
                    act(yc.t[:, :TB], yc.t[:, :TB], AF.Identity, [yc.D, COLS.D], [yc.D], bias=col(f"lnx_b{l}", hp), scale=col(f"lnx_w{l}", hp))
                    yield
                    tt("dve", yc.t[:, :TB], yc.t[:, :TB], BON[hp].t[:], ALU.add, [yc.D, BON[hp].D], [yc.D])
                    yield
                    gb_, gbd_ = nps()
                    yield
                    mm(gb_[:, :TB], GUP.t[:, hp * 128:(hp + 1) * 128], SGG.t[:], True, True, [GUP.D, SGG.D], [gbd_])
                    yield
                    stt("dve", YT.t[:, 5 + hp, :], yc.t[:, :TB], col(f"beta{l}", 5 + hp), gb_[:, :TB], ALU.mult, ALU.mult,
                        [yc.D, gbd_, COLS.D], [YT.d[5 + hp]])
                    yield
                interleave([_gen_cgn(_i) for _i in range(3)])
            else:
                for hp in range(3):
                    memset("dve", YT.t[:, 5 + hp, :], 0.0, [YT.d[5 + hp]])

            if tb == 0 and f"y{l}" in tap_d:
                ytmp = tmp()
                for j in range(8):
                    cp("dve", ytmp.t[:, :TB], YT.t[:, j, :], [YT.d[j]], [ytmp.D])
                    t = P.dma("sp", tap_d[f"y{l}"][j], ytmp.t[:, :TB], reads=[ytmp.D])
                    tap_toks.append(t)

            for f in range(8):
                bk, bd = nps()
                for k in range(8):
                    mm(bk[:, :TB], WOUT.t[:, k, f * 128:(f + 1) * 128], YT.t[:, k, :], k == 0, k == 7, [WOUT.D, YT.d[k]], [bd])
                stt("dve", X.t[:, f, t0:t0 + TB], bk[:, :TB], MOD.t[:, 16 + f:17 + f], X.t[:, f, t0:t0 + TB], ALU.mult, ALU.add,
                    [bd, MOD.D, X.d[f]], [X.d[f]])

        if f"x1_{l}" in tap_d:
            for j in range(8):
                for q in range(0, T, 512):
                    tap_toks.append(P.dma("sp", tap_d[f"x1_{l}"][j, :, q:q + 512], X.t[:, j, q:q + 512], reads=[X.d[j]]))

        if do_ffn:
            wgv = wg_d[l].rearrange("(k p) n -> p k n", p=128)
            wuv = wu_d[l].rearrange("(k p) n -> p k n", p=128)
            wdv = wd_d[l].rearrange("(m p) n -> p m n", p=128)
            P.barrier()
            def ffn_norm(fb):
                for th in range(TBF // 512):
                    norm_block(fb * TBF + th * 512, 512, lambda j: ADA.t[:, 8 + j:9 + j], lambda j: MOD.t[:, 24 + j:25 + j], H2.t, H2.D, doff=th * 512)
            ffn_norm(0)
            for fb in range(NTBF):
                t0 = fb * TBF
                for half in range(2):
                    for mi in range(11):
                        m = half * 11 + mi
                        wgb = wchunk(wgv[:, :, m * 128:(m + 1) * 128])
                        wub = wchunk(wuv[:, :, m * 128:(m + 1) * 128])
                        for th in range(TBF // 512):
                            tsl = slice(th * 512, (th + 1) * 512)
                            gk, gd = nps()
                            for k in range(8):
                                mm(gk[:, :], wgb.t[:, k, :], H2.t[:, k, tsl], k == 0, k == 7, [wgb.D, H2.D], [gd])
                            uk, ud = nps()
                            for k in range(8):
                                mm(uk[:, :], wub.t[:, k, :], H2.t[:, k, tsl], k == 0, k == 7, [wub.D, H2.D], [ud])
                            for h0 in range(0, 512, TB):
                                sg = tmp()
                                act(sg.t[:, :TB], gk[:, h0:h0 + TB], AF.Silu, [gd], [sg.D])
                                tt("dve", ACTT.t[:, mi, th * 512 + h0:th * 512 + h0 + TB], sg.t[:, :TB], uk[:, h0:h0 + TB], ALU.mult,
                                   [sg.D, ud], [ACTT.d[mi]])
                    if half == 1 and fb + 1 < NTBF:
                        ffn_norm(fb + 1)
                    for f in range(8):
                        w1 = wchunk(wdv[:, half * 11:half * 11 + 8, f * 128:(f + 1) * 128])
                        w2 = wchunk(wdv[:, half * 11 + 8:half * 11 + 11, f * 128:(f + 1) * 128], nk=3)
                        for th in range(TBF // 512):
                            tsl = slice(th * 512, (th + 1) * 512)
                            bk, bd = nps()
                            for mi in range(11):
                                wt = w1.t[:, mi, :] if mi < 8 else w2.t[:, mi - 8, :]
                                wd_ = w1.D if mi < 8 else w2.D
                                mm(bk[:, :], wt, ACTT.t[:, mi, tsl], mi == 0, mi == 10, [wd_, ACTT.d[mi]], [bd])
                            xs_ = slice(t0 + th * 512, t0 + (th + 1) * 512)
                            stt("dve", X.t[:, f, xs_], bk[:, :], MOD.t[:, 40 + f:41 + f], X.t[:, f, xs_], ALU.mult, ALU.add,
                                [bd, MOD.D, X.d[f]], [X.d[f]])
            P.barrier()
        if f"x2_{l}" in tap_d:
            for j in range(8):
                for q in range(0, T, 512):
                    tap_toks.append(P.dma("sp", tap_d[f"x2_{l}"][j, :, q:q + 512], X.t[:, j, q:q + 512], reads=[X.d[j]]))

    out_toks = []
    for fb in range(T // 512):
        t0 = fb * 512
        bk, bd = nps()
        for h0 in range(0, 512, TB):
            for j in range(8):
                s = tmp()
                act(s.t[:, :TB], X.t[:, j, t0 + h0:t0 + h0 + TB], AF.Square, [X.d[j]], [s.D])
                mm(bk[:, h0:h0 + TB], ONES.t[:], s.t[:, :TB], j == 0, j == 7, [s.D, ONES.D], [bd])
        act(RSTD.t[:, :512], bk[:, :512], AF.Ln, [bd, EPS.D], [RSTD.D], bias=eps6, scale=1.0 / D)
        act(RSTD.t[:, :512], RSTD.t[:, :512], AF.Exp, [RSTD.D], [RSTD.D], scale=-0.5)
        for j in range(8):
            for h0 in range(0, 512, TB):
                o = tmp()
                stt("dve", o.t[:, :TB], X.t[:, j, t0 + h0:t0 + h0 + TB], col("final_g", j), RSTD.t[:, h0:h0 + TB], ALU.mult, ALU.mult,
                    [X.d[j], RSTD.D, COLS.D], [o.D])
                out_toks.append(P.dma("sp", outT_d[j, :, t0 + h0:t0 + h0 + TB], o.t[:, :TB], reads=[o.D]))
    for t in out_toks + tap_toks:
        P.wait_tok("sp", t)
    build_program.last_counts = (dict(P.cnt), dict(P.dcnt))
    P.emit()
    P.close()
    es.close()
    return nc


def make_in_maps(inp):
    f = lambda a: np.ascontiguousarray(np.asarray(a, np.float32))
    cst = make_consts()
    rgbd = blockdiag(inp["rg_w"])
    igbd = blockdiag(inp["ig_w"])
    wa_up = f(np.concatenate([np.asarray(inp["rwkv_w_up"]), np.asarray(inp["rwkv_a_up"])], axis=1))
    shared = {
        "cst": cst, "ada_w": f(inp["ada_w"]), "w_in": f(inp["w_in"]), "rgbd": rgbd, "igbd": igbd,
        "wa_up": wa_up, "g_up": f(inp["rwkv_g_up"]), "w_out": f(inp["w_out"]),
        "wg": f(inp["ffn_w_gate"]), "wu": f(inp["ffn_w_up"]), "wd": f(inp["ffn_w_down"]),
    }
    x = np.asarray(inp["x"], np.float32)
    maps = []
    for b in range(NB):
        m = dict(shared)
        m["xT"] = np.ascontiguousarray(x[b].T.reshape(8, 128, T))
        m["cols"] = pack_cols(inp, b)
        maps.append(m)
    return maps


def kernel(**inputs):
    nc = build_program()
    maps = make_in_maps(inputs)
    res = run_bass_kernel_spmd(nc, maps, core_ids=list(range(NB)))
    out = np.empty((NB, T, D), np.float32)
    for b in range(NB):
        out[b] = res.results[b]["outT"].reshape(D, T).T
    return out
```

```python
import math
STOPB = 99
from contextlib import ExitStack
import numpy as np
import concourse.bass as bass
import concourse.mybir as mybir
from concourse.bass_utils import run_bass_kernel_spmd

F32 = mybir.dt.float32
BF16 = mybir.dt.bfloat16
ALU = mybir.AluOpType
AF = mybir.ActivationFunctionType

L = 2
D = 1024
T = 2048
NB = 8
PW = 3456
DFF = 2816
TB = 256
NTB = T // TB
CH = 64
NCH = TB // CH
TBF = 1024
NTBF = T // TBF
ENGS = ("pe", "act", "dve", "pool", "sp")
NDMA = 8


class Dep:
    __slots__ = ("w", "r")

    def __init__(self):
        self.w = None
        self.r = {}


class BDep(Dep):
    __slots__ = ()


class Prog:
    def __init__(self, nc):
        self.nc = nc
        self.q = {e: [] for e in ENGS}
        self.cnt = {e: 0 for e in ENGS}
        self.dcnt = {e: 0 for e in ENGS}
        self.seen = {e: {} for e in ENGS}
        self.sems = {}
        self.ctx = []
        for e in ENGS:
            self._mk(("c", e))
            for i in range(NDMA):
                self._mk(("d", e, i))

    def _mk(self, key):
        cm = self.nc.semaphore("s_" + "_".join(str(k) for k in key))
        self.sems[key] = cm.__enter__()
        self.ctx.append(cm)

    def _need(self, eng, reads, writes):
        needs = {}

        def add(k, v):
            if needs.get(k, 0) < v:
                needs[k] = v
        for d in reads:
            if d.w is not None:
                add(*d.w)
        for d in writes:
            if d.w is not None:
                add(*d.w)
            for k, v in d.r.items():
                add(k, v)
        for k, v in needs.items():
            if k == ("c", "pe") and eng == "pe":
                continue
            if self.seen[eng].get(k, 0) >= v:
                continue
            self.seen[eng][k] = v
            self.q[eng].append(("wait", k, v))

    def _mark(self, tok, reads, writes):
        k, v = tok
        for d in reads:
            if d.r.get(k, 0) < v:
                d.r[k] = v
        for d in writes:
            d.w = tok
            d.r = {}

    def op(self, eng, fn, reads=(), writes=()):
        if any(isinstance(d, BDep) for d in reads):
            writes = list(writes) + [d for d in reads if isinstance(d, BDep)]
            reads = [d for d in reads if not isinstance(d, BDep)]
        self._need(eng, reads, writes)
        self.cnt[eng] += 1
        tok = (("c", eng), self.cnt[eng])
        self.q[eng].append(("op", fn, tok[0], 1))
        self._mark(tok, reads, writes)
        return tok

    def dma(self, eng, out, in_, reads=(), writes=()):
        self._need(eng, reads, writes)
        i = self.dcnt[eng]
        self.dcnt[eng] += 1
        slot, rnd = i % NDMA, i // NDMA
        key = ("d", eng, slot)
        if rnd > 0 and self.seen[eng].get(key, 0) < 16 * rnd:
            self.seen[eng][key] = 16 * rnd
            self.q[eng].append(("wait", key, 16 * rnd))
        tok = (key, 16 * (rnd + 1))
        self.q[eng].append(("op", lambda e: e.dma_start(out=out, in_=in_), key, 16))
        self._mark(tok, reads, writes)
        return tok

    def barrier(self):
        toks = [(("c", e), self.cnt[e]) for e in ENGS if self.cnt[e] > 0]
        for e in ENGS:
            for sl in range(NDMA):
                if self.dcnt[e] > sl:
                    toks.append((("d", e, sl), 16 * ((self.dcnt[e] - sl + NDMA - 1) // NDMA)))
        for e in ENGS:
            for k, v in toks:
                if k == ("c", "pe") and e == "pe":
                    continue
                self.wait_tok(e, (k, v))

    def wait_tok(self, eng, tok):
        k, v = tok
        if self.seen[eng].get(k, 0) < v:
            self.seen[eng][k] = v
            self.q[eng].append(("wait", k, v))

    def emit(self):
        sems = self.sems
        waited = {}
        for e in ENGS:
            for it in self.q[e]:
                if it[0] == "wait" and it[1][0] == "c":
                    waited.setdefault(it[1], set()).add(it[2])
        rank = {k: {v: i + 1 for i, v in enumerate(sorted(vs))} for k, vs in waited.items()}

        def run(name):
            def body(e):
                n = 0
                for it in self.q[name]:
                    if it[0] == "wait":
                        k, v = it[1], it[2]
                        if k[0] == "c":
                            v = rank[k][v]
                        e.wait_ge(sems[k], v)
                    else:
                        ins = it[1](e)
                        if it[2][0] == "c":
                            n += 1
                            if n in rank.get(it[2], ()):
                                ins.then_inc(sems[it[2]], 1)
                        else:
                            ins.then_inc(sems[it[2]], it[3])
            return body
        with self.nc.Block() as block:
            block.tensor(run("pe"))
            block.scalar(run("act"))
            block.vector(run("dve"))
            block.gpsimd(run("pool"))
            block.sync(run("sp"))

    def close(self):
        for cm in reversed(self.ctx):
            cm.__exit__(None, None, None)


def col_layout():
    idx = {}
    n = [0]

    def add(name, k):
        idx[name] = n[0]
        n[0] += k
    add("c", 8)
    add("final_g", 8)
    for l in range(L):
        add(f"norm1_g{l}", 8)
        add(f"norm2_g{l}", 8)
        add(f"ada_b{l}", 48)
        add(f"conv_w{l}", 8)
        add(f"conv_b{l}", 2)
        add(f"rg_b{l}", 2)
        add(f"ig_b{l}", 2)
        add(f"lam{l}", 2)
        add(f"hgrn_lb{l}", 3)
        add(f"mu{l}", 11)
        add(f"w0{l}", 3)
        add(f"a0{l}", 3)
        add(f"k_k{l}", 3)
        add(f"k_a{l}", 3)
        add(f"r_k{l}", 3)
        add(f"lnx_w{l}", 3)
        add(f"lnx_b{l}", 3)
        add(f"beta{l}", 8)
    return idx, n[0]


CI, NCOL = col_layout()
K_ID, K_BO, K_MINC, K_MSTR, K_MTRIL, K_IDP, K_RST = 0, 128, 256, 320, 384, 448, 512
NCONST = 512 + TB


def make_consts():
    cst = np.zeros((128, NCONST), np.float32)
    p = np.arange(128)
    cst[:, K_ID:K_ID + 128] = np.eye(128, dtype=np.float32)
    cst[:, K_BO:K_BO + 128] = (p[:, None] // 64 == p[None, :] // 64).astype(np.float32)
    j = np.arange(64)
    cst[:, K_MINC:K_MINC + 64] = ((p[:, None] % 64) <= j[None, :]).astype(np.float32)
    cst[:, K_MSTR:K_MSTR + 64] = ((p[:, None] % 64) < j[None, :]).astype(np.float32)
    cst[:, K_MTRIL:K_MTRIL + 64] = (j[None, :] < (p[:, None] % 64)).astype(np.float32)
    cst[:, K_IDP:K_IDP + 64] = ((p[:, None] % 64) == j[None, :]).astype(np.float32)
    t = np.arange(TB)
    cst[:, K_RST:K_RST + TB] = (t % CH != 0).astype(np.float32)[None, :]
    return cst


def pack_cols(inp, b):
    cols = np.zeros((128, NCOL), np.float32)

    def put(name, v):
        v = np.asarray(v, np.float32).reshape(-1, 128)
        cols[:, CI[name]:CI[name] + v.shape[0]] = v.T
    put("c", inp["c"][b])
    put("final_g", inp["final_g"])
    for l in range(L):
        put(f"norm1_g{l}", inp["norm1_g"][l])
        put(f"norm2_g{l}", inp["norm2_g"][l])
        put(f"ada_b{l}", inp["ada_b"][l])
        cw = np.asarray(inp["conv_w"][l], np.float32)
        cwp = np.stack([cw[j, cc * 128:(cc + 1) * 128] for cc in range(2) for j in range(4)], 0)
        put(f"conv_w{l}", cwp)
        put(f"conv_b{l}", inp["conv_b"][l])
        put(f"rg_b{l}", inp["rg_b"][l])
        put(f"ig_b{l}", inp["ig_b"][l])
        put(f"lam{l}", inp["lru_lam"][l])
        put(f"hgrn_lb{l}", inp["hgrn_lb"][l])
        put(f"mu{l}", inp["rwkv_mu"][l])
        put(f"w0{l}", inp["rwkv_w0"][l])
        put(f"a0{l}", inp["rwkv_a0"][l])
        put(f"k_k{l}", inp["rwkv_k_k"][l])
        put(f"k_a{l}", inp["rwkv_k_a"][l])
        put(f"r_k{l}", inp["rwkv_r_k"][l])
        put(f"lnx_w{l}", inp["rwkv_lnx_w"][l])
        put(f"lnx_b{l}", inp["rwkv_lnx_b"][l])
        put(f"beta{l}", inp["mix_beta"][l])
    return cols


def blockdiag(w):
    w = np.asarray(w, np.float32)
    out = np.zeros((L, 2, 128, 128), np.float32)
    for l in range(L):
        for g in range(4):
            cc, gg = g // 2, g % 2
            out[l, cc, gg * 64:(gg + 1) * 64, gg * 64:(gg + 1) * 64] = w[l, g]
    return out


def build_program(n_layers=L, taps=(), do_mixer=(True, True, True), do_ffn=True):
    nc = bass.Bass("TRN2", target_bir_lowering=False)
    es = ExitStack()

    def din(name, shape):
        return nc.dram_tensor(name, list(shape), F32, kind="ExternalInput").ap()
    xT_d = din("xT", [8, 128, T])
    cols_d = din("cols", [128, NCOL])
    cst_d = din("cst", [128, NCONST])
    ada_d = din("ada_w", [L, D, 6 * D])
    win_d = din("w_in", [L, D, PW])
    rgbd_d = din("rgbd", [L, 2, 128, 128])
    igbd_d = din("igbd", [L, 2, 128, 128])
    waup_d = din("wa_up", [L, 128, 384])
    gup_d = din("g_up", [L, 128, 384])
    wout_d = din("w_out", [L, D, D])
    wg_d = din("wg", [L, D, DFF])
    wu_d = din("wu", [L, D, DFF])
    wd_d = din("wd", [L, DFF, D])
    outT_d = nc.dram_tensor("outT", [8, 128, T], F32, kind="ExternalOutput").ap()
    tap_d = {}
    for name, shape in taps:
        tap_d[name] = nc.dram_tensor("tap_" + name, list(shape), F32, kind="ExternalOutput").ap()

    P = Prog(nc)

    class Tl:
        def __init__(self, name, shape, dt=F32, nd=1):
            self.t = es.enter_context(nc.sbuf_tensor(name, list(shape), dt))
            self.d = [Dep() for _ in range(nd)]
            self.D = self.d[0]

    EOBJ = {"dve": "vector", "pool": "gpsimd", "act": "scalar"}

    def tt(eng, out, in0, in1, op, R, W):
        P.op(eng, lambda e: e.tensor_tensor(out, in0, in1, op), reads=R, writes=W)

    def ts(eng, out, in0, s1, s2, op0, op1, R, W):
        if s2 is None:
            P.op(eng, lambda e: e.tensor_scalar(out, in0, s1, None, op0), reads=R, writes=W)
        else:
            P.op(eng, lambda e: e.tensor_scalar(out, in0, s1, s2, op0, op1), reads=R, writes=W)

    def stt(eng, out, in0, sc, in1, op0, op1, R, W):
        P.op(eng, lambda e: e.scalar_tensor_tensor(out, in0, sc, in1, op0, op1), reads=R, writes=W)

    def act(out, in_, func, R, W, bias=None, scale=None):
        kw = {}
        if bias is not None:
            kw["bias"] = bias
        if scale is not None:
            kw["scale"] = scale
        P.op("act", lambda e: e.activation(out, in_, func, **kw), reads=R, writes=W)

    def cp(eng, out, in_, R, W):
        if eng == "act":
            P.op("act", lambda e: e.copy(out, in_), reads=R, writes=W)
        else:
            P.op(eng, lambda e: e.tensor_copy(out, in_), reads=R, writes=W)

    def mm(out, lhsT, rhs, start, stop, R, W):
        P.op("pe", lambda e: e.matmul(out, lhsT, rhs, start=start, stop=stop), reads=R, writes=W)

    def memset(eng, ap, val, W):
        P.op(eng, lambda e: e.memset(ap, val), writes=W)

    def scan(out, d0, d1, init, R, W):
        P.op("dve", lambda e: e.tensor_tensor_scan(out, d0, d1, init, ALU.mult, ALU.add), reads=R, writes=W)

    banks = []
    for i in range(8):
        banks.append((es.enter_context(nc.psum_tensor(f"bank{i}", [128, 512], F32)), BDep()))
    bctr = [0]

    def nps():
        b = banks[bctr[0] % 8]
        bctr[0] += 1
        return b

    X = Tl("X", [128, 8, T], F32, nd=8)
    COLS = Tl("COLS", [128, NCOL])
    CST = Tl("CST", [128, NCONST])
    ident = CST.t[:, K_ID:K_ID + 128]
    bones = CST.t[:, K_BO:K_BO + 128]
    m_inc = CST.t[:, K_MINC:K_MINC + 64]
    m_str = CST.t[:, K_MSTR:K_MSTR + 64]
    m_tril = CST.t[:, K_MTRIL:K_MTRIL + 64]
    idp = CST.t[:, K_IDP:K_IDP + 64]
    rst = CST.t[:, K_RST:K_RST + TB]

    def col(name, j=0, n=1):
        return COLS.t[:, CI[name] + j:CI[name] + j + n]

    P.dma("sp", COLS.t[:], cols_d, writes=[COLS.D])
    P.dma("sp", CST.t[:], cst_d, writes=[CST.D])
    for j in range(8):
        P.dma("act", X.t[:, j, :], xT_d[j], writes=[X.d[j]])

    NSCR = 12
    scr = [Tl(f"scr{i}", [128, TB + 4]) for i in range(NSCR)]
    sctr = [0]

    def tmp():
        s = scr[sctr[0] % NSCR]
        sctr[0] += 1
        return s

    def interleave(gens):
        gens = list(gens)
        while gens:
            for g in list(gens):
                try:
                    next(g)
                except StopIteration:
                    gens.remove(g)

    CSI = Tl("CSI", [128, 8])
    MOD = Tl("MOD", [128, 48])
    ADA = Tl("ADA", [128, 32])
    DER = Tl("DER", [128, 16])

    act(CSI.t[:], col("c", 0, 8), AF.Silu, [COLS.D], [CSI.D])
    CSB = Tl("CSB", [128, 8], BF16)
    cp("dve", CSB.t[:], CSI.t[:], [CSI.D], [CSB.D])


    NST = 2
    NWB = 7
    wst = [Tl(f"wst{i}", [128, 8, 128]) for i in range(NST)]
    wbf = [Tl(f"wbf{i}", [128, 8, 128], BF16) for i in range(NWB)]
    wctr = [0]
    sctr2 = [0]

    def wchunk(src_ap, nk=8, cast=True):
        if not cast:
            i = sctr2[0] % NST
            sctr2[0] += 1
            P.dma("sp", wst[i].t[:, 0:nk, :], src_ap, writes=[wst[i].D])
            return wst[i]
        i = wctr[0] % NWB
        wctr[0] += 1
        P.dma("pool", wbf[i].t[:, 0:nk, :], src_ap, writes=[wbf[i].D])
        return wbf[i]

    WOUT = Tl("WOUT", [128, 8, D], BF16)
    HBF = Tl("HBF", [128, 8, TB], BF16)
    YT = Tl("YT", [128, 8, TB], BF16, nd=8)
    RSTD = Tl("RSTD", [128, 512])

    def tap(name, ap, deps):
        if name in tap_d:
            t = P.dma("sp", tap_d[name], ap, reads=deps)
            tap_toks.append(t)
    tap_toks = []

    ONES = Tl("ONES", [128, 128])
    memset("pool", ONES.t[:], 1.0, [ONES.D])
    EPS = Tl("EPS", [128, 8])
    memset("pool", EPS.t[:, 0:1], 1e-6, [EPS.D])
    memset("pool", EPS.t[:, 1:2], 1e-12, [EPS.D])
    memset("pool", EPS.t[:, 2:3], 64e-5, [EPS.D])
    memset("pool", EPS.t[:, 3:4], 1.0, [EPS.D])
    memset("pool", EPS.t[:, 4:5], 0.0, [EPS.D])
    eps6, eps12, epsgn, one_c, zero_c = EPS.t[:, 0:1], EPS.t[:, 1:2], EPS.t[:, 2:3], EPS.t[:, 3:4], EPS.t[:, 4:5]

    def norm_block(t0, ntok, Acol, Bcol, dstbf, dstdep, doff=0):
        bk, bd = nps()
        for h0 in range(0, ntok, TB):
            for j in range(8):
                s = tmp()
                act(s.t[:, :TB], X.t[:, j, t0 + h0:t0 + h0 + TB], AF.Square, [X.d[j]], [s.D])
                mm(bk[:, h0:h0 + TB], ONES.t[:], s.t[:, :TB], j == 0, j == 7, [s.D, ONES.D], [bd])
        act(RSTD.t[:, :ntok], bk[:, :ntok], AF.Ln, [bd, EPS.D], [RSTD.D], bias=eps6, scale=1.0 / D)
        act(RSTD.t[:, :ntok], RSTD.t[:, :ntok], AF.Exp, [RSTD.D], [RSTD.D], scale=-0.5)
        for j in range(8):
            for h0 in range(0, ntok, TB):
                s = tmp()
                stt("dve", s.t[:, :TB], X.t[:, j, t0 + h0:t0 + h0 + TB], Acol(j), RSTD.t[:, h0:h0 + TB],
                    ALU.mult, ALU.mult, [X.d[j], RSTD.D, ADA.D], [s.D])
                act(dstbf[:, j, doff + h0:doff + h0 + TB], s.t[:, :TB], AF.Identity, [s.D, MOD.D], [dstdep], bias=Bcol(j))

    class V:
        def __init__(self, ap, nd=1):
            self.t = ap
            self.d = [Dep() for _ in range(nd)]
            self.D = self.d[0]
    NBIG = 9728
    BIG = Tl("BIG", [128, NBIG])
    H2 = V(BIG.t[:, 0:4096].bitcast(BF16).rearrange("p (k t) -> p k t", k=8))
    ACTT = V(BIG.t[:, 4096:9728].bitcast(BF16).rearrange("p (m t) -> p m t", m=11), nd=11)

    scr2 = [V(BIG.t[:, 6060 + i * 260:6060 + (i + 1) * 260]) for i in range(12)]
    SSETS = [scr[0:4] + scr2[0:4], scr[4:8] + scr2[4:8], scr[8:12] + scr2[8:12]]
    PT = [V(BIG.t[:, i * 260:(i + 1) * 260]) for i in range(11)]
    CARRY = Tl("CARRY", [128, 11])
    HST = Tl("HST", [128, 2])
    XAH = Tl("XAH", [128, 2, 3])
    SB = [Tl(f"SB{i}", [128, 64]) for i in range(3)]
    QTs = [Tl(f"QT{i}", [128, TB], BF16) for i in range(3)]
    KTs = [Tl(f"KT{i}", [128, TB], BF16) for i in range(3)]
    QAs = [Tl(f"QA{i}", [128, TB], BF16) for i in range(3)]
    KAs = [Tl(f"KA{i}", [128, TB], BF16) for i in range(3)]
    PVBs = [Tl(f"PVB{i}", [128, TB], BF16) for i in range(3)]
    IDB = Tl("IDB", [128, 128], BF16)
    HBs = [[Tl(f"HB{j}_{i}", [128, 64], BF16) for i in range(5)] for j in range(3)]
    SNs = [Tl(f"SN{i}", [128, 64]) for i in range(3)]
    OBD = [Dep() for _ in range(3)]
    GBs = [Tl(f"GB{i}", [128, 16]) for i in range(3)]
    TOK = [V(BIG.t[:, 4908 + i * 384:4908 + (i + 1) * 384].bitcast(BF16).rearrange("p (c a b) -> p c a b", c=NCH, a=3))
           for i in range(3)]
    ZC = Tl("ZC", [128, 3, 64])
    BH = [Tl(f"BH{i}", [128, TB], BF16) for i in range(3)]
    AH = [Tl(f"AH{i}", [128, TB], BF16) for i in range(3)]
    RH = [Tl(f"RH{i}", [128, TB], BF16) for i in range(3)]
    KH = [Tl(f"KH{i}", [128, TB], BF16) for i in range(3)]
    VB = [Tl(f"VB{i}", [128, TB], BF16) for i in range(3)]
    ZB = Tl("ZB", [128, 3, 64], BF16)
    SGG = V(BIG.t[:, 4396:4652])
    TW = V(BIG.t[:, 4652:4908])
    BON = [V(BIG.t[:, 3628 + i * 256:3628 + (i + 1) * 256]) for i in range(3)]
    YC = V(BIG.t[:, 2860:3628].rearrange("p (a t) -> p a t", a=3))
    GAM = Tl("GAM", [128, 3, NCH])
    MQ = [[Tl(f"MQ{c}_{i}", [128, 3, 64], BF16) for i in range(2)] for c in range(NCH)]
    MP = [[Tl(f"MP{c}_{i}", [128, 3, 64], BF16) for i in range(2)] for c in range(NCH)]
    MT = [Tl(f"MT{i}", [128, 3, 64], BF16) for i in range(NCH)]
    MAK = [Tl(f"MAK{i}", [128, 3, 64], BF16) for i in range(NCH)]
    MRB = [Tl(f"MRB{i}", [128, 3, 64], BF16) for i in range(NCH)]
    MRK = [Tl(f"MRK{i}", [128, 3, 64], BF16) for i in range(NCH)]
    XS = Tl("XS", [128, 3, 64], BF16); US = Tl("US", [128, 3, 64], BF16); ZTMP = Tl("ZTMP", [128, 3, 64])
    RGBD = Tl("RGBD", [128, 2, 128]); IGBD = Tl("IGBD", [128, 2, 128])
    WAUP = Tl("WAUP", [128, 384]); GUP = Tl("GUP", [128, 384])

    mqctr = [0]
    cp("dve", IDB.t[:], ident, [CST.D], [IDB.D])
    identb = IDB.t

    def bc3(ap2):
        return ap2.unsqueeze(1).to_broadcast([128, 3, 64])

    for l in range(n_layers):
        if l == 0:
            memset("dve", DER.t[:, 2:5], 0.0, [DER.D])
        else:
            tt("dve", DER.t[:, 2:5], col("hgrn_lb1", 0, 3), col("hgrn_lb0", 0, 3), ALU.subtract, [COLS.D], [DER.D])
            act(DER.t[:, 2:5], DER.t[:, 2:5], AF.Sigmoid, [DER.D], [DER.D])
        ts("dve", DER.t[:, 5:8], DER.t[:, 2:5], -1.0, 1.0, ALU.mult, ALU.add, [DER.D], [DER.D])
        act(DER.t[:, 0:2], col(f"lam{l}", 0, 2), AF.Exp, [COLS.D], [DER.D], scale=-1.0)
        act(DER.t[:, 0:2], DER.t[:, 0:2], AF.Ln, [DER.D, EPS.D], [DER.D], bias=one_c)
        ts("dve", DER.t[:, 0:2], DER.t[:, 0:2], -8.0, None, ALU.mult, None, [DER.D], [DER.D])
        lbc = lambda hp: DER.t[:, 2 + hp:3 + hp]
        omlc = lambda hp: DER.t[:, 5 + hp:6 + hp]
        nsp8 = lambda cc: DER.t[:, cc:cc + 1]

        P.dma("sp", RGBD.t[:], rgbd_d[l].rearrange("c p n -> p c n"), writes=[RGBD.D])
        P.dma("sp", IGBD.t[:], igbd_d[l].rearrange("c p n -> p c n"), writes=[IGBD.D])
        P.dma("sp", WAUP.t[:], waup_d[l], writes=[WAUP.D])
        P.dma("sp", GUP.t[:], gup_d[l], writes=[GUP.D])

        adv = ada_d[l].rearrange("(k p) m -> p k m", p=128)
        mbk, mbd = nps()
        for j in range(48):
            st = wchunk(adv[:, :, j * 128:(j + 1) * 128])
            for k in range(8):
                mm(mbk[:, j:j + 1], st.t[:, k, :], CSB.t[:, k:k + 1], k == 0, k == 7, [st.D, CSB.D], [mbd])
        tt("dve", MOD.t[:], mbk[:, 0:48], col(f"ada_b{l}", 0, 48), ALU.add, [mbd, COLS.D], [MOD.D])
        stt("dve", ADA.t[:, 0:8], MOD.t[:, 8:16], 1.0, col(f"norm1_g{l}", 0, 8), ALU.add, ALU.mult, [MOD.D, COLS.D], [ADA.D])
        stt("dve", ADA.t[:, 8:16], MOD.t[:, 32:40], 1.0, col(f"norm2_g{l}", 0, 8), ALU.add, ALU.mult, [MOD.D, COLS.D], [ADA.D])
        tap(f"mod{l}", MOD.t[:], [MOD.D])

        wov = wout_d[l].rearrange("(k p) n -> p k n", p=128)
        for f in range(8):
            wb = wchunk(wov[:, :, f * 128:(f + 1) * 128])
            cp("act", WOUT.t[:, :, f * 128:(f + 1) * 128], wb.t[:], [wb.D], [WOUT.D])

        wiv = win_d[l].rearrange("(k p) n -> p k n", p=128)

        def project(chunk, dst_ap, dst_deps, evac="act"):
            wb = wchunk(wiv[:, :, chunk * 128:(chunk + 1) * 128])
            bk, bd = nps()
            for k in range(8):
                mm(bk[:, :TB], wb.t[:, k, :], HBF.t[:, k, :], k == 0, k == 7, [wb.D, HBF.D], [bd])
            cp(evac, dst_ap, bk[:, :TB], [bd], dst_deps)

        memset("dve", HST.t[:], 0.0, [HST.D])
        memset("dve", CARRY.t[:], 0.0, [CARRY.D])
        for hp in range(3):
            memset("dve", SB[hp].t[:], 0.0, [SB[hp].D])
        memset("dve", ZC.t[:], 0.0, [ZC.D])
        memset("dve", ZB.t[:], 0.0, [ZB.D])
        memset("dve", XAH.t[:], 0.0, [XAH.D])

        for tb in range(NTB):
            t0 = tb * TB
            norm_block(t0, TB, lambda j: ADA.t[:, j:j + 1], lambda j: MOD.t[:, j:j + 1], HBF.t, HBF.D)

            if do_mixer[0]:
                def _gen_mixa(cc):
                    S = SSETS[cc]
                    XA, YA = PT[cc], PT[2 + cc]
                    yield
                    project(cc, XA.t[:, 3:3 + TB], [XA.D])
                    yield
                    cp("act", XA.t[:, 0:3], XAH.t[:, cc, :], [XAH.D], [XA.D])
                    yield
                    cp("act", XAH.t[:, cc, :], XA.t[:, TB:TB + 3], [XA.D], [XAH.D])
                    yield
                    project(2 + cc, YA.t[:, :TB], [YA.D], evac="dve")
                    yield
                    u = S[0]
                    yield
                    cw = lambda j: col(f"conv_w{l}", cc * 4 + j)
                    yield
                    ts("dve", u.t[:, :TB], XA.t[:, 0:TB], cw(0), col(f"conv_b{l}", cc), ALU.mult, ALU.add, [XA.D, COLS.D], [u.D])
                    yield
                    for j in range(1, 4):
                        stt("dve", u.t[:, :TB], XA.t[:, j:j + TB], cw(j), u.t[:, :TB], ALU.mult, ALU.add, [XA.D, COLS.D, u.D], [u.D])
                    rb, rbd = nps()
                    yield
                    mm(rb[:, :TB], RGBD.t[:, cc, :], u.t[:, :TB], True, True, [RGBD.D, u.D], [rbd])
                    yield
                    ib, ibd = nps()
                    yield
                    mm(ib[:, :TB], IGBD.t[:, cc, :], u.t[:, :TB], True, True, [IGBD.D, u.D], [ibd])
                    yield
                    r = S[1]; ig = S[2]
                    yield
                    act(r.t[:, :TB], rb[:, :TB], AF.Sigmoid, [rbd, COLS.D], [r.D], bias=col(f"rg_b{l}", cc))
                    yield
                    act(ig.t[:, :TB], ib[:, :TB], AF.Sigmoid, [ibd, COLS.D], [ig.D], bias=col(f"ig_b{l}", cc))
                    yield
                    a = S[3]
                    yield
                    act(a.t[:, :TB], r.t[:, :TB], AF.Exp, [r.D, DER.D], [a.D], scale=nsp8(cc))
                    yield
                    m = S[4]
                    yield
                    act(m.t[:, :TB], a.t[:, :TB], AF.Square, [a.D], [m.D])
                    yield
                    act(m.t[:, :TB], m.t[:, :TB], AF.Sqrt, [m.D, EPS.D], [m.D], bias=one_c, scale=-1.0)
                    yield
                    if tb == 0:
                        memset("dve", m.t[:, 0:1], 1.0, [m.D])
                    tt("dve", m.t[:, :TB], m.t[:, :TB], ig.t[:, :TB], ALU.mult, [m.D, ig.D], [m.D])
                    yield
                    tt("dve", m.t[:, :TB], m.t[:, :TB], u.t[:, :TB], ALU.mult, [m.D, u.D], [m.D])
                    yield
                    h = S[5]
                    yield
                    scan(h.t[:, :TB], a.t[:, :TB], m.t[:, :TB], HST.t[:, cc:cc + 1], [a.D, m.D, HST.D], [h.D])
                    yield
                    cp("dve", HST.t[:, cc:cc + 1], h.t[:, TB - 1:TB], [h.D], [HST.D])
                    yield
                    g1 = S[6]
                    yield
                    act(g1.t[:, :TB], YA.t[:, :TB], AF.Square, [YA.D], [g1.D])
                    yield
                    ts("dve", g1.t[:, :TB], g1.t[:, :TB], 0.044715, 1.0, ALU.mult, ALU.add, [g1.D], [g1.D])
                    yield
                    tt("dve", g1.t[:, :TB], g1.t[:, :TB], YA.t[:, :TB], ALU.mult, [g1.D, YA.D], [g1.D])
                    yield
                    act(g1.t[:, :TB], g1.t[:, :TB], AF.Sigmoid, [g1.D], [g1.D], scale=1.5957691216057308)
                    yield
                    tt("dve", g1.t[:, :TB], g1.t[:, :TB], YA.t[:, :TB], ALU.mult, [g1.D, YA.D], [g1.D])
                    yield
                    tt("dve", h.t[:, :TB], h.t[:, :TB], g1.t[:, :TB], ALU.mult, [h.D, g1.D], [h.D])
                    yield
                    sq = S[7]
                    yield
                    act(sq.t[:, :TB], h.t[:, :TB], AF.Square, [h.D], [sq.D])
                    yield
                    sb_, sbd = nps()
                    yield
                    mm(sb_[:, :TB], bones, sq.t[:, :TB], True, True, [CST.D, sq.D], [sbd])
                    yield
                    act(sq.t[:, :TB], sb_[:, :TB], AF.Ln, [sbd, EPS.D], [sq.D], bias=eps6, scale=1.0 / 64)
                    yield
                    act(sq.t[:, :TB], sq.t[:, :TB], AF.Exp, [sq.D], [sq.D], scale=-0.5)
                    yield
                    stt("dve", YT.t[:, cc, :], h.t[:, :TB], col(f"beta{l}", cc), sq.t[:, :TB], ALU.mult, ALU.mult,
                        [h.D, sq.D, COLS.D], [YT.d[cc]])
                    yield
                interleave([_gen_mixa(_i) for _i in range(2)])
            else:
                for cc in range(2):
                    memset("dve", YT.t[:, cc, :], 0.0, [YT.d[cc]])

            if not do_mixer[1]:
                for hp in range(3):
                    memset("dve", YT.t[:, 2 + hp, :], 0.0, [YT.d[2 + hp]])
            else:
                PGs = [PT[6], PT[7], PT[8]]
                def _gen_bprep(hp):
                    S = SSETS[hp]
                    Pq, Pf, Pg = PT[hp], PT[3 + hp], PGs[hp]
                    yield
                    qa, ka, QT, KT, PVB, GB = QAs[hp], KAs[hp], QTs[hp], KTs[hp], PVBs[hp], GBs[hp]
                    yield
                    project(4 + hp, Pq.t[:, :TB], [Pq.D])
                    yield
                    project(7 + hp, Pf.t[:, :TB], [Pf.D], evac="dve")
                    yield
                    project(10 + hp, PVB.t[:], [PVB.D])
                    yield
                    project(13 + hp, Pg.t[:, :TB], [Pg.D], evac="dve")
                    yield
                    s1 = S[0]; kg = S[1]; bc = S[2]; dd = S[3]; e1 = S[4]
                    yield
                    act(s1.t[:, :TB], Pf.t[:, :TB], AF.Sigmoid, [Pf.D], [s1.D])
                    yield
                    act(s1.t[:, :TB], s1.t[:, :TB], AF.Identity, [s1.D, DER.D], [s1.D], bias=lbc(hp), scale=omlc(hp))
                    yield
                    act(s1.t[:, :TB], s1.t[:, :TB], AF.Ln, [s1.D], [s1.D])
                    yield
                    act(kg.t[:, :TB], Pf.t[:, :TB], AF.Sigmoid, [Pf.D], [kg.D], scale=-1.0)
                    yield
                    ts("dve", kg.t[:, :TB], kg.t[:, :TB], omlc(hp), None, ALU.mult, None, [kg.D, DER.D], [kg.D])
                    yield
                    scan(bc.t[:, :TB], rst, s1.t[:, :TB], zero_c, [CST.D, s1.D, EPS.D], [bc.D])
                    yield
                    bc3v = bc.t[:, :TB].rearrange("p (c s) -> p c s", s=CH)
                    yield
                    bmid = bc3v[:, :, 31:32]
                    yield
                    blast = bc3v[:, :, 63:64]
                    yield
                    act(e1.t[:, :TB], bc.t[:, :TB], AF.Exp, [bc.D], [e1.D])
                    yield
                    stt("dve", qa.t[:], Pq.t[:, :TB], 0.125, e1.t[:, :TB], ALU.mult, ALU.mult, [Pq.D, e1.D], [qa.D])
                    yield
                    ts("dve", e1.t[:, :TB], bc.t[:, :TB], -1.0, 60.0, ALU.mult, ALU.min, [bc.D], [e1.D])
                    yield
                    act(e1.t[:, :TB], e1.t[:, :TB], AF.Exp, [e1.D], [e1.D])
                    yield
                    tt("dve", ka.t[:], kg.t[:, :TB], e1.t[:, :TB], ALU.mult, [kg.D, e1.D], [ka.D])
                    yield
                    tt("dve", dd.t[:, :TB].rearrange("p (c s) -> p c s", s=CH), bc3v, bmid.to_broadcast([128, NCH, CH]),
                       ALU.subtract, [bc.D], [dd.D])
                    act(e1.t[:, :TB], dd.t[:, :TB], AF.Exp, [dd.D], [e1.D])
                    yield
                    stt("dve", QT.t[:], Pq.t[:, :TB], 0.125, e1.t[:, :TB], ALU.mult, ALU.mult, [Pq.D, e1.D], [QT.D])
                    yield
                    act(e1.t[:, :TB], dd.t[:, :TB], AF.Exp, [dd.D], [e1.D], scale=-1.0)
                    yield
                    tt("dve", KT.t[:], kg.t[:, :TB], e1.t[:, :TB], ALU.mult, [kg.D, e1.D], [KT.D])
                    yield
                    gbv = GB.t[:, 0:12].rearrange("p (a c) -> p a c", c=NCH)
                    yield
                    act(gbv[:, 0, :].unsqueeze(2), bmid, AF.Exp, [bc.D], [GB.D])
                    yield
                    act(gbv[:, 1, :].unsqueeze(2), blast, AF.Exp, [bc.D], [GB.D])
                    yield
                    tt("dve", gbv[:, 2, :].unsqueeze(2), blast, bmid, ALU.subtract, [bc.D], [GB.D])
                    yield
                    act(gbv[:, 2, :], gbv[:, 2, :], AF.Exp, [GB.D], [GB.D])
                    yield
                    yield
                interleave([_gen_bprep(_i) for _i in range(3)])
                for c in range(NCH):
                    cs_ = slice(c * CH, (c + 1) * CH)
                    ca_ = slice(c * CH, c * CH + 32)
                    cb_ = slice(c * CH + 32, (c + 1) * CH)
                    abs_ = []
                    for hp in range(3):
                        qa, ka, QT, KT, PVB = QAs[hp], KAs[hp], QTs[hp], KTs[hp], PVBs[hp]
                        ab, abd = nps()
                        abs_.append((ab, abd))
                        for hh in range(2):
                            ps_ = slice(hh * 64, hh * 64 + 64)
                            mm(ab[ps_, 0:32], ka.t[ps_, cs_], qa.t[ps_, ca_], True, True, [ka.D, qa.D], [abd])
                            mm(ab[ps_, 32:64], KT.t[ps_, cs_], QT.t[ps_, cb_], True, True, [KT.D, QT.D], [abd])
                            mm(ab[ps_, 64:128], PVB.t[ps_, cs_], identb[ps_, ps_], True, True, [PVB.D, IDB.D], [abd])
                            mm(ab[ps_, 128:192], KT.t[ps_, cs_], identb[ps_, ps_], True, True, [KT.D, IDB.D], [abd])
                    for hp in range(3):
                        ATS, VTK, KTK, STL, SBb = HBs[hp]
                        ab, abd = abs_[hp]
                        tt("dve", ATS.t[:], ab[:, 0:64], m_inc, ALU.mult, [abd, CST.D], [ATS.D])
                        cp("act", VTK.t[:], ab[:, 64:128], [abd], [VTK.D])
                        cp("act", KTK.t[:], ab[:, 128:192], [abd], [KTK.D])
                        ts("dve", STL.t[:], SB[hp].t[:], GBs[hp].t[:, c:c + 1], None, ALU.mult, None, [SB[hp].D, GBs[hp].D], [STL.D])
                        cp("act", SBb.t[:], SB[hp].t[:], [SB[hp].D], [SBb.D])
                    obs_ = []
                    for hp in range(3):
                        ATS, VTK, KTK, STL, SBb = HBs[hp]
                        qa, QT = QAs[hp], QTs[hp]
                        snb, snd = nps()
                        for hh in range(2):
                            ps_ = slice(hh * 64, hh * 64 + 64)
                            mm(snb[ps_, 0:64], KTK.t[ps_, :], VTK.t[ps_, :], True, True, [KTK.D, VTK.D], [snd])
                        ob, obd = nps()
                        obs_.append((ob, obd, snb, snd))
                        for hh in range(2):
                            ps_ = slice(hh * 64, hh * 64 + 64)
                            mm(ob[ps_, 0:64], VTK.t[ps_, :], ATS.t[ps_, :], True, False, [VTK.D, ATS.D], [obd])
                            mm(ob[ps_, 0:32], SBb.t[ps_, :], qa.t[ps_, ca_], False, False, [SBb.D, qa.D], [obd])
                            mm(ob[ps_, 32:64], STL.t[ps_, :], QT.t[ps_, cb_], False, True, [STL.D, QT.D], [obd])
                    for hp in range(3):
                        ob, obd, snb, snd = obs_[hp]
                        GB = GBs[hp]
                        ts("dve", SNs[hp].t[:], snb[:, 0:64], GB.t[:, 8 + c:9 + c], None, ALU.mult, None, [snd, GB.D], [SNs[hp].D])
                        cp("act", YC.t[:, hp, cs_], ob[:, 0:64], [obd], [OBD[hp]])
                        stt("dve", SB[hp].t[:], SB[hp].t[:], GB.t[:, 4 + c:5 + c], SNs[hp].t[:], ALU.mult, ALU.add,
                            [SB[hp].D, GB.D, SNs[hp].D], [SB[hp].D])
                def _gen_bepi(hp):
                    S = SSETS[hp]
                    Pg = PGs[hp]
                    yield
                    OBt = YC.t[:, hp, :]
                    yield
                    sq = S[0]
                    yield
                    act(sq.t[:, :TB], OBt, AF.Square, [OBD[hp]], [sq.D])
                    yield
                    sb_, sbd = nps()
                    yield
                    mm(sb_[:, :TB], bones, sq.t[:, :TB], True, True, [CST.D, sq.D], [sbd])
                    yield
                    act(sq.t[:, :TB], sb_[:, :TB], AF.Ln, [sbd, EPS.D], [sq.D], bias=eps6, scale=1.0 / 64)
                    yield
                    act(sq.t[:, :TB], sq.t[:, :TB], AF.Exp, [sq.D], [sq.D], scale=-0.5)
                    yield
                    sl = S[1]
                    yield
                    act(sl.t[:, :TB], Pg.t[:, :TB], AF.Silu, [Pg.D], [sl.D])
                    yield
                    tt("dve", sq.t[:, :TB], sq.t[:, :TB], OBt, ALU.mult, [sq.D, OBD[hp]], [sq.D])
                    yield
                    stt("dve", YT.t[:, 2 + hp, :], sq.t[:, :TB], col(f"beta{l}", 2 + hp), sl.t[:, :TB], ALU.mult, ALU.mult,
                        [sq.D, sl.D, COLS.D], [YT.d[2 + hp]])

                    yield
                interleave([_gen_bepi(_i) for _i in range(3)])
            if do_mixer[2]:
                for i in range(11):
                    Xt = PT[i]
                    project(16 + i, Xt.t[:, 1:1 + TB], [Xt.D], evac=("act" if i % 2 == 0 else "dve"))
                    cp("act", Xt.t[:, 0:1], CARRY.t[:, i:i + 1], [CARRY.D], [Xt.D])
                    cp("act", CARRY.t[:, i:i + 1], Xt.t[:, TB:TB + 1], [Xt.D], [CARRY.D])
                    d_ = tmp()
                    tt("dve", d_.t[:, :TB], Xt.t[:, 0:TB], Xt.t[:, 1:1 + TB], ALU.subtract, [Xt.D], [d_.D])
                    stt("dve", Xt.t[:, 1:1 + TB], d_.t[:, :TB], col(f"mu{l}", i), Xt.t[:, 1:1 + TB], ALU.mult, ALU.add,
                        [d_.D, Xt.D, COLS.D], [Xt.D])
                sR = [PT[i] for i in range(0, 3)]
                sK = [PT[i] for i in range(3, 6)]
                sV = [PT[i] for i in range(6, 9)]
                sWA, sG = PT[9], PT[10]
                V1 = lambda t_: t_.t[:, 1:1 + TB]
                tw = TW
                act(tw.t[0:64, :TB], sWA.t[0:64, 1:1 + TB], AF.Tanh, [sWA.D], [tw.D])
                act(SGG.t[:], V1(sG), AF.Sigmoid, [sG.D], [SGG.D])
                def _gen_cprep(hp):
                    S = SSETS[hp]
                    R_, K_, V_ = sR[hp], sK[hp], sV[hp]
                    yield
                    kk = S[0]; sq = S[1]; k2 = S[2]; e_ = S[3]; cum = S[4]; ex = S[5]; lw = S[6]; aa = S[7]
                    yield
                    hc = slice(hp * 128, (hp + 1) * 128)
                    yield
                    zb, zbd = nps()
                    yield
                    mm(zb[:, :TB], WAUP.t[0:64, hc], tw.t[0:64, :TB], True, True, [WAUP.D, tw.D], [zbd])
                    yield
                    act(lw.t[:, :TB], zb[:, :TB], AF.Sigmoid, [zbd, COLS.D], [lw.D], bias=col(f"w0{l}", hp))
                    yield
                    ts("dve", lw.t[:, :TB], lw.t[:, :TB], -0.6065306597126334, None, ALU.mult, None, [lw.D], [lw.D])
                    yield
                    ab_, abd_ = nps()
                    yield
                    mm(ab_[:, :TB], WAUP.t[64:128, hc], sWA.t[64:128, 1:1 + TB], True, True, [WAUP.D, sWA.D], [abd_])
                    yield
                    act(aa.t[:, :TB], ab_[:, :TB], AF.Sigmoid, [abd_, COLS.D], [aa.D], bias=col(f"a0{l}", hp))
                    yield
                    ts("dve", kk.t[:, :TB], V1(K_), col(f"k_k{l}", hp), None, ALU.mult, None, [K_.D, COLS.D], [kk.D])
                    yield
                    act(sq.t[:, :TB], kk.t[:, :TB], AF.Square, [kk.D], [sq.D])
                    yield
                    nb_, nbd_ = nps()
                    yield
                    mm(nb_[:, :TB], bones, sq.t[:, :TB], True, True, [CST.D, sq.D], [nbd_])
                    yield
                    act(sq.t[:, :TB], nb_[:, :TB], AF.Ln, [nbd_, EPS.D], [sq.D], bias=eps12)
                    yield
                    act(sq.t[:, :TB], sq.t[:, :TB], AF.Exp, [sq.D], [sq.D], scale=-0.5)
                    yield
                    tt("dve", kk.t[:, :TB], kk.t[:, :TB], sq.t[:, :TB], ALU.mult, [kk.D, sq.D], [kk.D])
                    yield
                    ts("dve", k2.t[:, :TB], aa.t[:, :TB], -1.0, col(f"k_a{l}", hp), ALU.add, ALU.mult, [aa.D, COLS.D], [k2.D])
                    yield
                    stt("dve", k2.t[:, :TB], k2.t[:, :TB], 1.0, V1(K_), ALU.add, ALU.mult, [k2.D, K_.D], [k2.D])
                    yield
                    tt("dve", e_.t[:, :TB], V1(R_), k2.t[:, :TB], ALU.mult, [R_.D, k2.D], [e_.D])
                    yield
                    ts("dve", e_.t[:, :TB], e_.t[:, :TB], col(f"r_k{l}", hp), None, ALU.mult, None, [e_.D, COLS.D], [e_.D])
                    yield
                    bb_, bbd_ = nps()
                    yield
                    mm(bb_[:, :TB], bones, e_.t[:, :TB], True, True, [CST.D, e_.D], [bbd_])
                    yield
                    tt("dve", BON[hp].t[:], bb_[:, :TB], V1(V_), ALU.mult, [bbd_, V_.D], [BON[hp].D])
                    yield
                    scan(cum.t[:, :TB], rst, lw.t[:, :TB], zero_c, [CST.D, lw.D, EPS.D], [cum.D])
                    yield
                    act(ex.t[:, :TB], cum.t[:, :TB], AF.Exp, [cum.D], [ex.D])
                    yield
                    cp("act", GAM.t[:, hp, :].unsqueeze(2), ex.t[:, :TB].rearrange("p (c s) -> p c s", s=CH)[:, :, 63:64], [ex.D], [GAM.D])
                    yield
                    tt("dve", RH[hp].t[:], V1(R_), ex.t[:, :TB], ALU.mult, [R_.D, ex.D], [RH[hp].D])
                    yield
                    cp("act", VB[hp].t[:], V1(V_), [V_.D], [VB[hp].D])
                    yield
                    act(ex.t[:, :TB], cum.t[:, :TB], AF.Exp, [cum.D], [ex.D], scale=-1.0)
                    yield
                    tt("dve", KH[hp].t[:], k2.t[:, :TB], ex.t[:, :TB], ALU.mult, [k2.D, ex.D], [KH[hp].D])
                    yield
                    tt("dve", e_.t[:, :TB], kk.t[:, :TB], aa.t[:, :TB], ALU.mult, [kk.D, aa.D], [e_.D])
                    yield
                    tt("dve", BH[hp].t[:], e_.t[:, :TB], ex.t[:, :TB], ALU.mult, [e_.D, ex.D], [BH[hp].D])
                    yield
                    tt("dve", cum.t[:, :TB], cum.t[:, :TB], lw.t[:, :TB], ALU.subtract, [cum.D, lw.D], [cum.D])
                    yield
                    act(ex.t[:, :TB], cum.t[:, :TB], AF.Exp, [cum.D], [ex.D])
                    yield
                    stt("dve", AH[hp].t[:], kk.t[:, :TB], -1.0, ex.t[:, :TB], ALU.mult, ALU.mult, [kk.D, ex.D], [AH[hp].D])
                    yield
                    yield
                interleave([_gen_cprep(_i) for _i in range(3)])
                for c in range(NCH):
                    cs1 = slice(1 + c * CH, 1 + (c + 1) * CH)
                    cs_ = slice(c * CH, (c + 1) * CH)
                    srcs = [(VB, False), (BH, False), (KH, False)]
                    for qi, (src, halo) in enumerate(srcs):
                        tb_, tbd_ = nps()
                        for hp in range(3):
                            for hh in range(2):
                                ps_ = slice(hh * 64, hh * 64 + 64)
                                sap = src[hp].t[ps_, cs1] if halo else src[hp].t[ps_, cs_]
                                mm(tb_[ps_, hp * 64:(hp + 1) * 64], sap, identb[ps_, ps_], True, True, [src[hp].D, IDB.D], [tbd_])
                        cp("act" if qi != 1 else "dve", TOK[qi].t[:, c, :, :], tb_[:, 0:192].rearrange("p (a b) -> p a b", b=64),
                           [tbd_], [TOK[qi].D])
                cur = [None] * NCH
                for c in range(NCH):
                    cs_ = slice(c * CH, (c + 1) * CH)
                    q0 = MQ[c][0]; p0 = MP[c][0]
                    specs = [
                        (q0, lambda hp, ps_: BH[hp].t[ps_, cs_], lambda hp, ps_: AH[hp].t[ps_, cs_], m_str, lambda hp: [BH[hp].D, AH[hp].D]),
                        (p0, lambda hp, ps_: AH[hp].t[ps_, cs_], lambda hp, ps_: BH[hp].t[ps_, cs_], m_tril, lambda hp: [BH[hp].D, AH[hp].D]),
                        (MAK[c], lambda hp, ps_: KH[hp].t[ps_, cs_], lambda hp, ps_: AH[hp].t[ps_, cs_], m_str, lambda hp: [KH[hp].D, AH[hp].D]),
                        (MRB[c], lambda hp, ps_: BH[hp].t[ps_, cs_], lambda hp, ps_: RH[hp].t[ps_, cs_], m_inc, lambda hp: [BH[hp].D, RH[hp].D]),
                        (MRK[c], lambda hp, ps_: KH[hp].t[ps_, cs_], lambda hp, ps_: RH[hp].t[ps_, cs_], m_inc, lambda hp: [KH[hp].D, RH[hp].D]),
                    ]
                    for si, (dst, lf, rf, msk, dps) in enumerate(specs):
                        b_, bd_ = nps()
                        for hp in range(3):
                            for hh in range(2):
                                ps_ = slice(hh * 64, hh * 64 + 64)
                                mm(b_[ps_, hp * 64:(hp + 1) * 64], lf(hp, ps_), rf(hp, ps_), True, True, dps(hp), [bd_])
                        tt("dve", dst.t[:], b_[:, 0:192].rearrange("p (a b) -> p a b", b=64), bc3(msk), ALU.mult, [bd_, CST.D], [dst.D])
                    tt("dve", MT[c].t[:], q0.t[:], bc3(idp), ALU.add, [q0.D, CST.D], [MT[c].D])
                    cur[c] = (q0, p0)
                for lev in range(1, 6):
                    pbs = []
                    for c in range(NCH):
                        qc, pc = cur[c]
                        pb_, pbd_ = nps()
                        pbs.append((pb_, pbd_))
                        for hp in range(3):
                            for hh in range(2):
                                ps_ = slice(hh * 64, hh * 64 + 64)
                                mm(pb_[ps_, hp * 64:(hp + 1) * 64], qc.t[ps_, hp, :], pc.t[ps_, hp, :], True, True, [qc.D, pc.D], [pbd_])
                                if lev < 5:
                                    mm(pb_[ps_, 192 + hp * 64:192 + (hp + 1) * 64], pc.t[ps_, hp, :], qc.t[ps_, hp, :], True, True,
                                       [qc.D, pc.D], [pbd_])
                    for c in range(NCH):
                        pb_, pbd_ = pbs[c]
                        qn = MQ[c][lev % 2]; pn = MP[c][lev % 2]
                        cp("act", pn.t[:], pb_[:, 0:192].rearrange("p (a b) -> p a b", b=64), [pbd_], [pn.D])
                        if lev < 5:
                            cp("dve", qn.t[:], pb_[:, 192:384].rearrange("p (a b) -> p a b", b=64), [pbd_], [qn.D])
                        cur[c] = (qn, pn)
                    tbs = []
                    for c in range(NCH):
                        qn, pn = cur[c]
                        tb2, tbd2 = nps()
                        tbs.append((tb2, tbd2))
                        for hp in range(3):
                            for hh in range(2):
                                ps_ = slice(hh * 64, hh * 64 + 64)
                                mm(tb2[ps_, hp * 64:(hp + 1) * 64], pn.t[ps_, hp, :], MT[c].t[ps_, hp, :], True, True, [pn.D, MT[c].D], [tbd2])
                    for c in range(NCH):
                        tb2, tbd2 = tbs[c]
                        tt("dve", MT[c].t[:], MT[c].t[:], tb2[:, 0:192].rearrange("p (a b) -> p a b", b=64), ALU.add, [MT[c].D, tbd2], [MT[c].D])
                for c in range(NCH):
                    cs1 = slice(1 + c * CH, 1 + (c + 1) * CH)
                    cs_ = slice(c * CH, (c + 1) * CH)
                    xb, xbd = nps()
                    for hp in range(3):
                        for hh in range(2):
                            ps_ = slice(hh * 64, hh * 64 + 64)
                            o_ = xb[ps_, hp * 64:(hp + 1) * 64]
                            mm(o_, AH[hp].t[ps_, cs_], ZB.t[ps_, hp, :], True, False, [AH[hp].D, ZB.D], [xbd])
                            mm(o_, MAK[c].t[ps_, hp, :], TOK[0].t[ps_, c, hp, :], False, True, [MAK[c].D, TOK[0].D], [xbd])
                    cp("act", XS.t[:], xb[:, 0:192].rearrange("p (a b) -> p a b", b=64), [xbd], [XS.D])
                    ub, ubd = nps()
                    for hp in range(3):
                        for hh in range(2):
                            ps_ = slice(hh * 64, hh * 64 + 64)
                            mm(ub[ps_, hp * 64:(hp + 1) * 64], MT[c].t[ps_, hp, :], XS.t[ps_, hp, :], True, True, [MT[c].D, XS.D], [ubd])
                    cp("dve", US.t[:], ub[:, 0:192].rearrange("p (a b) -> p a b", b=64), [ubd], [US.D])
                    yb, ybd = nps()
                    znb, znd = nps()
                    for hp in range(3):
                        for hh in range(2):
                            ps_ = slice(hh * 64, hh * 64 + 64)
                            o2 = znb[ps_, hp * 64:(hp + 1) * 64]
                            mm(o2, TOK[1].t[ps_, c, hp, :], US.t[ps_, hp, :], True, False, [TOK[1].D, US.D], [znd])
                            mm(o2, TOK[2].t[ps_, c, hp, :], TOK[0].t[ps_, c, hp, :], False, True, [TOK[2].D, TOK[0].D], [znd])
                    for hp in range(3):
                        for hh in range(2):
                            ps_ = slice(hh * 64, hh * 64 + 64)
                            o_ = yb[ps_, hp * 64:(hp + 1) * 64]
                            mm(o_, ZB.t[ps_, hp, :], RH[hp].t[ps_, cs_], True, False, [ZB.D, RH[hp].D], [ybd])
                            mm(o_, US.t[ps_, hp, :], MRB[c].t[ps_, hp, :], False, False, [US.D, MRB[c].D], [ybd])
                            mm(o_, TOK[0].t[ps_, c, hp, :], MRK[c].t[ps_, hp, :], False, True, [TOK[0].D, MRK[c].D], [ybd])
                    cp("act", YC.t[:, :, cs_], yb[:, 0:192].rearrange("p (a b) -> p a b", b=64), [ybd], [YC.D] + OBD)
                    tt("dve", ZTMP.t[:], znb[:, 0:192].rearrange("p (a b) -> p a b", b=64), ZC.t[:], ALU.add, [znd, ZC.D], [ZTMP.D])
                    tt("dve", ZB.t[:], ZTMP.t[:], GAM.t[:, :, c:c + 1].to_broadcast([128, 3, 64]), ALU.mult, [ZTMP.D, GAM.D], [ZB.D])
                    tt("dve", ZC.t[:], ZTMP.t[:], GAM.t[:, :, c:c + 1].to_broadcast([128, 3, 64]), ALU.mult, [ZTMP.D, GAM.D], [ZC.D])
                def _gen_cgn(hp):
                    S = SSETS[hp]
                    mb_, mbd_ = nps()
                    yield
                    mm(mb_[:, :TB], bones, YC.t[:, hp, :], True, True, [CST.D, YC.D, OBD[hp]], [mbd_])
                    yield
                    yc = S[0]; sq = S[1]
                    yield
                    stt("dve", yc.t[:, :TB], mb_[:, :TB], -1.0 / 64, YC.t[:, hp, :], ALU.mult, ALU.add, [mbd_, YC.D, OBD[hp]], [yc.D])
                    yield
                    act(sq.t[:, :TB], yc.t[:, :TB], AF.Square, [yc.D], [sq.D])
                    yield
                    vb_, vbd_ = nps()
                    yield
                    mm(vb_[:, :TB], bones, sq.t[:, :TB], True, True, [CST.D, sq.D], [vbd_])
                    yield
                    act(sq.t[:, :TB], vb_[:, :TB], AF.Ln, [vbd_, EPS.D], [sq.D], bias=epsgn, scale=1.0 / 64)
                    yield
                    act(sq.t[:, :TB], sq.t[:, :TB], AF.Exp, [sq.D], [sq.D], scale=-0.5)
                    yield
                    tt("dve", yc.t[:, :TB], yc.t[:, :TB], sq.t[:, :TB], ALU.mult, [yc.D, sq.D], [yc.D])
                    yield
                    act(yc.t[:, :TB], yc.t[:, :TB], AF.Identity, [yc.D, COLS.D], [yc.D], bias=col(f"lnx_b{l}", hp), scale=col(f"lnx_w{l}", hp))
                    yield
                    tt("dve", yc.t[:, :TB], yc.t[:, :TB], BON[hp].t[:], ALU.add, [yc.D, BON[hp].D], [yc.D])
                    yield
                    gb_, gbd_ = nps()
                    yield
                    mm(gb_[:, :TB], GUP.t[:, hp * 128:(hp + 1) * 128], SGG.t[:], True, True, [GUP.D, SGG.D], [gbd_])
                    yield
                    stt("dve", YT.t[:, 5 + hp, :], yc.t[:, :TB], col(f"beta{l}", 5 + hp), gb_[:, :TB], ALU.mult, ALU.mult,
                        [yc.D, gbd_, COLS.D], [YT.d[5 + hp]])
                    yield
                interleave([_gen_cgn(_i) for _i in range(3)])
            else:
                for hp in range(3):
                    memset("dve", YT.t[:, 5 + hp, :], 0.0, [YT.d[5 + hp]])

            if tb == 0 and f"y{l}" in tap_d:
                ytmp = tmp()
                for j in range(8):
                    cp("dve", ytmp.t[:, :TB], YT.t[:, j, :], [YT.d[j]], [ytmp.D])
                    t = P.dma("sp", tap_d[f"y{l}"][j], ytmp.t[:, :TB], reads=[ytmp.D])
                    tap_toks.append(t)

            for f in range(8):
                bk, bd = nps()
                for k in range(8):
                    mm(bk[:, :TB], WOUT.t[:, k, f * 128:(f + 1) * 128], YT.t[:, k, :], k == 0, k == 7, [WOUT.D, YT.d[k]], [bd])
                stt("dve", X.t[:, f, t0:t0 + TB], bk[:, :TB], MOD.t[:, 16 + f:17 + f], X.t[:, f, t0:t0 + TB], ALU.mult, ALU.add,
                    [bd, MOD.D, X.d[f]], [X.d[f]])

        if f"x1_{l}" in tap_d:
            for j in range(8):
                for q in range(0, T, 512):
                    tap_toks.append(P.dma("sp", tap_d[f"x1_{l}"][j, :, q:q + 512], X.t[:, j, q:q + 512], reads=[X.d[j]]))

        if do_ffn:
            wgv = wg_d[l].rearrange("(k p) n -> p k n", p=128)
            wuv = wu_d[l].rearrange("(k p) n -> p k n", p=128)
            wdv = wd_d[l].rearrange("(m p) n -> p m n", p=128)
            P.barrier()
            def ffn_norm(fb):
                for th in range(TBF // 512):
                    norm_block(fb * TBF + th * 512, 512, lambda j: ADA.t[:, 8 + j:9 + j], lambda j: MOD.t[:, 24 + j:25 + j], H2.t, H2.D, doff=th * 512)
            ffn_norm(0)
            for fb in range(NTBF):
                t0 = fb * TBF
                for half in range(2):
                    for mi in range(11):
                        m = half * 11 + mi
                        wgb = wchunk(wgv[:, :, m * 128:(m + 1) * 128])
                        wub = wchunk(wuv[:, :, m * 128:(m + 1) * 128])
                        for th in range(TBF // 512):
                            tsl = slice(th * 512, (th + 1) * 512)
                            gk, gd = nps()
                            for k in range(8):
                                mm(gk[:, :], wgb.t[:, k, :], H2.t[:, k, tsl], k == 0, k == 7, [wgb.D, H2.D], [gd])
                            uk, ud = nps()
                            for k in range(8):
                                mm(uk[:, :], wub.t[:, k, :], H2.t[:, k, tsl], k == 0, k == 7, [wub.D, H2.D], [ud])
                            for h0 in range(0, 512, TB):
                                sg = tmp()
                                act(sg.t[:, :TB], gk[:, h0:h0 + TB], AF.Silu, [gd], [sg.D])
                                tt("dve", ACTT.t[:, mi, th * 512 + h0:th * 512 + h0 + TB], sg.t[:, :TB], uk[:, h0:h0 + TB], ALU.mult,
                                   [sg.D, ud], [ACTT.d[mi]])
                    if half == 1 and fb + 1 < NTBF:
                        ffn_norm(fb + 1)
                    for f in range(8):
                        w1 = wchunk(wdv[:, half * 11:half * 11 + 8, f * 128:(f + 1) * 128])
                        w2 = wchunk(wdv[:, half * 11 + 8:half * 11 + 11, f * 128:(f + 1) * 128], nk=3)
                        for th in range(TBF // 512):
                            tsl = slice(th * 512, (th + 1) * 512)
                            bk, bd = nps()
                            for mi in range(11):
                                wt = w1.t[:, mi, :] if mi < 8 else w2.t[:, mi - 8, :]
                                wd_ = w1.D if mi < 8 else w2.D
                                mm(bk[:, :], wt, ACTT.t[:, mi, tsl], mi == 0, mi == 10, [wd_, ACTT.d[mi]], [bd])
                            xs_ = slice(t0 + th * 512, t0 + (th + 1) * 512)
                            stt("dve", X.t[:, f, xs_], bk[:, :], MOD.t[:, 40 + f:41 + f], X.t[:, f, xs_], ALU.mult, ALU.add,
                                [bd, MOD.D, X.d[f]], [X.d[f]])
            P.barrier()
        if f"x2_{l}" in tap_d:
            for j in range(8):
                for q in range(0, T, 512):
                    tap_toks.append(P.dma("sp", tap_d[f"x2_{l}"][j, :, q:q + 512], X.t[:, j, q:q + 512], reads=[X.d[j]]))

    out_toks = []
    for fb in range(T // 512):
        t0 = fb * 512
        bk, bd = nps()
        for h0 in range(0, 512, TB):
            for j in range(8):
                s = tmp()
                act(s.t[:, :TB], X.t[:, j, t0 + h0:t0 + h0 + TB], AF.Square, [X.d[j]], [s.D])
                mm(bk[:, h0:h0 + TB], ONES.t[:], s.t[:, :TB], j == 0, j == 7, [s.D, ONES.D], [bd])
        act(RSTD.t[:, :512], bk[:, :512], AF.Ln, [bd, EPS.D], [RSTD.D], bias=eps6, scale=1.0 / D)
        act(RSTD.t[:, :512], RSTD.t[:, :512], AF.Exp, [RSTD.D], [RSTD.D], scale=-0.5)
        for j in range(8):
            for h0 in range(0, 512, TB):
                o = tmp()
                stt("dve", o.t[:, :TB], X.t[:, j, t0 + h0:t0 + h0 + TB], col("final_g", j), RSTD.t[:, h0:h0 + TB], ALU.mult, ALU.mult,
                    [X.d[j], RSTD.D, COLS.D], [o.D])
                out_toks.append(P.dma("sp", outT_d[j, :, t0 + h0:t0 + h0 + TB], o.t[:, :TB], reads=[o.D]))
    for t in out_toks + tap_toks:
        P.wait_tok("sp", t)
    build_program.last_counts = (dict(P.cnt), dict(P.dcnt))
    P.emit()
    P.close()
    es.close()
    return nc


def make_in_maps(inp):
    f = lambda a: np.ascontiguousarray(np.asarray(a, np.float32))
    cst = make_consts()
    rgbd = blockdiag(inp["rg_w"])
    igbd = blockdiag(inp["ig_w"])
    wa_up = f(np.concatenate([np.asarray(inp["rwkv_w_up"]), np.asarray(inp["rwkv_a_up"])], axis=1))
    shared = {
        "cst": cst, "ada_w": f(inp["ada_w"]), "w_in": f(inp["w_in"]), "rgbd": rgbd, "igbd": igbd,
        "wa_up": wa_up, "g_up": f(inp["rwkv_g_up"]), "w_out": f(inp["w_out"]),
        "wg": f(inp["ffn_w_gate"]), "wu": f(inp["ffn_w_up"]), "wd": f(inp["ffn_w_down"]),
    }
    x = np.asarray(inp["x"], np.float32)
    maps = []
    for b in range(NB):
        m = dict(shared)
        m["xT"] = np.ascontiguousarray(x[b].T.reshape(8, 128, T))
        m["cols"] = pack_cols(inp, b)
        maps.append(m)
    return maps


def kernel(**inputs):
    nc = build_program()
    maps = make_in_maps(inputs)
    res = run_bass_kernel_spmd(nc, maps, core_ids=list(range(NB)))
    out = np.empty((NB, T, D), np.float32)
    for b in range(NB):
        out[b] = res.results[b]["outT"].reshape(D, T).T
    return out
```

```python
import math
STOPB = 99
from contextlib import ExitStack
import numpy as np
import concourse.bass as bass
import concourse.mybir as mybir
from concourse.bass_utils import run_bass_kernel_spmd

F32 = mybir.dt.float32
BF16 = mybir.dt.bfloat16
ALU = mybir.AluOpType
AF = mybir.ActivationFunctionType

L = 2
D = 1024
T = 2048
NB = 8
PW = 3456
DFF = 2816
TB = 256
NTB = T // TB
CH = 64
NCH = TB // CH
TBF = 1024
NTBF = T // TBF
ENGS = ("pe", "act", "dve", "pool", "sp")
NDMA = 8


class Dep:
    __slots__ = ("w", "r")

    def __init__(self):
        self.w = None
        self.r = {}


class BDep(Dep):
    __slots__ = ()


class Prog:
    def __init__(self, nc):
        self.nc = nc
        self.q = {e: [] for e in ENGS}
        self.cnt = {e: 0 for e in ENGS}
        self.dcnt = {e: 0 for e in ENGS}
        self.seen = {e: {} for e in ENGS}
        self.sems = {}
        self.ctx = []
        for e in ENGS:
            self._mk(("c", e))
            for i in range(NDMA):
                self._mk(("d", e, i))

    def _mk(self, key):
        cm = self.nc.semaphore("s_" + "_".join(str(k) for k in key))
        self.sems[key] = cm.__enter__()
        self.ctx.append(cm)

    def _need(self, eng, reads, writes):
        needs = {}

        def add(k, v):
            if needs.get(k, 0) < v:
                needs[k] = v
        for d in reads:
            if d.w is not None:
                add(*d.w)
        for d in writes:
            if d.w is not None:
                add(*d.w)
            for k, v in d.r.items():
                add(k, v)
        for k, v in needs.items():
            if k == ("c", "pe") and eng == "pe":
                continue
            if self.seen[eng].get(k, 0) >= v:
                continue
            self.seen[eng][k] = v
            self.q[eng].append(("wait", k, v))

    def _mark(self, tok, reads, writes):
        k, v = tok
        for d in reads:
            if d.r.get(k, 0) < v:
                d.r[k] = v
        for d in writes:
            d.w = tok
            d.r = {}

    def op(self, eng, fn, reads=(), writes=()):
        if any(isinstance(d, BDep) for d in reads):
            writes = list(writes) + [d for d in reads if isinstance(d, BDep)]
            reads = [d for d in reads if not isinstance(d, BDep)]
        self._need(eng, reads, writes)
        self.cnt[eng] += 1
        tok = (("c", eng), self.cnt[eng])
        self.q[eng].append(("op", fn, tok[0], 1))
        self._mark(tok, reads, writes)
        return tok

    def dma(self, eng, out, in_, reads=(), writes=()):
        self._need(eng, reads, writes)
        i = self.dcnt[eng]
        self.dcnt[eng] += 1
        slot, rnd = i % NDMA, i // NDMA
        key = ("d", eng, slot)
        if rnd > 0 and self.seen[eng].get(key, 0) < 16 * rnd:
            self.seen[eng][key] = 16 * rnd
            self.q[eng].append(("wait", key, 16 * rnd))
        tok = (key, 16 * (rnd + 1))
        self.q[eng].append(("op", lambda e: e.dma_start(out=out, in_=in_), key, 16))
        self._mark(tok, reads, writes)
        return tok

    def barrier(self):
        toks = [(("c", e), self.cnt[e]) for e in ENGS if self.cnt[e] > 0]
        for e in ENGS:
            for sl in range(NDMA):
                if self.dcnt[e] > sl:
                    toks.append((("d", e, sl), 16 * ((self.dcnt[e] - sl + NDMA - 1) // NDMA)))
        for e in ENGS:
            for k, v in toks:
                if k == ("c", "pe") and e == "pe":
                    continue
                self.wait_tok(e, (k, v))

    def wait_tok(self, eng, tok):
        k, v = tok
        if self.seen[eng].get(k, 0) < v:
            self.seen[eng][k] = v
            self.q[eng].append(("wait", k, v))

    def emit(self):
        sems = self.sems
        waited = {}
        for e in ENGS:
            for it in self.q[e]:
                if it[0] == "wait" and it[1][0] == "c":
                    waited.setdefault(it[1], set()).add(it[2])
        rank = {k: {v: i + 1 for i, v in enumerate(sorted(vs))} for k, vs in waited.items()}

        def run(name):
            def body(e):
                n = 0
                for it in self.q[name]:
                    if it[0] == "wait":
                        k, v = it[1], it[2]
                        if k[0] == "c":
                            v = rank[k][v]
                        e.wait_ge(sems[k], v)
                    else:
                        ins = it[1](e)
                        if it[2][0] == "c":
                            n += 1
                            if n in rank.get(it[2], ()):
                                ins.then_inc(sems[it[2]], 1)
                        else:
                            ins.then_inc(sems[it[2]], it[3])
            return body
        with self.nc.Block() as block:
            block.tensor(run("pe"))
            block.scalar(run("act"))
            block.vector(run("dve"))
            block.gpsimd(run("pool"))
            block.sync(run("sp"))

    def close(self):
        for cm in reversed(self.ctx):
            cm.__exit__(None, None, None)


def col_layout():
    idx = {}
    n = [0]

    def add(name, k):
        idx[name] = n[0]
        n[0] += k
    add("c", 8)
    add("final_g", 8)
    for l in range(L):
        add(f"norm1_g{l}", 8)
        add(f"norm2_g{l}", 8)
        add(f"ada_b{l}", 48)
        add(f"conv_w{l}", 8)
        add(f"conv_b{l}", 2)
        add(f"rg_b{l}", 2)
        add(f"ig_b{l}", 2)
        add(f"lam{l}", 2)
        add(f"hgrn_lb{l}", 3)
        add(f"mu{l}", 11)
        add(f"w0{l}", 3)
        add(f"a0{l}", 3)
        add(f"k_k{l}", 3)
        add(f"k_a{l}", 3)
        add(f"r_k{l}", 3)
        add(f"lnx_w{l}", 3)
        add(f"lnx_b{l}", 3)
        add(f"beta{l}", 8)
    return idx, n[0]


CI, NCOL = col_layout()
K_ID, K_BO, K_MINC, K_MSTR, K_MTRIL, K_IDP, K_RST = 0, 128, 256, 320, 384, 448, 512
NCONST = 512 + TB


def make_consts():
    cst = np.zeros((128, NCONST), np.float32)
    p = np.arange(128)
    cst[:, K_ID:K_ID + 128] = np.eye(128, dtype=np.float32)
    cst[:, K_BO:K_BO + 128] = (p[:, None] // 64 == p[None, :] // 64).astype(np.float32)
    j = np.arange(64)
    cst[:, K_MINC:K_MINC + 64] = ((p[:, None] % 64) <= j[None, :]).astype(np.float32)
    cst[:, K_MSTR:K_MSTR + 64] = ((p[:, None] % 64) < j[None, :]).astype(np.float32)
    cst[:, K_MTRIL:K_MTRIL + 64] = (j[None, :] < (p[:, None] % 64)).astype(np.float32)
    cst[:, K_IDP:K_IDP + 64] = ((p[:, None] % 64) == j[None, :]).astype(np.float32)
    t = np.arange(TB)
    cst[:, K_RST:K_RST + TB] = (t % CH != 0).astype(np.float32)[None, :]
    return cst


def pack_cols(inp, b):
    cols = np.zeros((128, NCOL), np.float32)

    def put(name, v):
        v = np.asarray(v, np.float32).reshape(-1, 128)
        cols[:, CI[name]:CI[name] + v.shape[0]] = v.T
    put("c", inp["c"][b])
    put("final_g", inp["final_g"])
    for l in range(L):
        put(f"norm1_g{l}", inp["norm1_g"][l])
        put(f"norm2_g{l}", inp["norm2_g"][l])
        put(f"ada_b{l}", inp["ada_b"][l])
        cw = np.asarray(inp["conv_w"][l], np.float32)
        cwp = np.stack([cw[j, cc * 128:(cc + 1) * 128] for cc in range(2) for j in range(4)], 0)
        put(f"conv_w{l}", cwp)
        put(f"conv_b{l}", inp["conv_b"][l])
        put(f"rg_b{l}", inp["rg_b"][l])
        put(f"ig_b{l}", inp["ig_b"][l])
        put(f"lam{l}", inp["lru_lam"][l])
        put(f"hgrn_lb{l}", inp["hgrn_lb"][l])
        put(f"mu{l}", inp["rwkv_mu"][l])
        put(f"w0{l}", inp["rwkv_w0"][l])
        put(f"a0{l}", inp["rwkv_a0"][l])
        put(f"k_k{l}", inp["rwkv_k_k"][l])
        put(f"k_a{l}", inp["rwkv_k_a"][l])
        put(f"r_k{l}", inp["rwkv_r_k"][l])
        put(f"lnx_w{l}", inp["rwkv_lnx_w"][l])
        put(f"lnx_b{l}", inp["rwkv_lnx_b"][l])
        put(f"beta{l}", inp["mix_beta"][l])
    return cols


def blockdiag(w):
    w = np.asarray(w, np.float32)
    out = np.zeros((L, 2, 128, 128), np.float32)
    for l in range(L):
        for g in range(4):
            cc, gg = g // 2, g % 2
            out[l, cc, gg * 64:(gg + 1) * 64, gg * 64:(gg + 1) * 64] = w[l, g]
    return out


def build_program(n_layers=L, taps=(), do_mixer=(True, True, True), do_ffn=True):
    nc = bass.Bass("TRN2", target_bir_lowering=False)
    es = ExitStack()

    def din(name, shape):
        return nc.dram_tensor(name, list(shape), F32, kind="ExternalInput").ap()
    xT_d = din("xT", [8, 128, T])
    cols_d = din("cols", [128, NCOL])
    cst_d = din("cst", [128, NCONST])
    ada_d = din("ada_w", [L, D, 6 * D])
    win_d = din("w_in", [L, D, PW])
    rgbd_d = din("rgbd", [L, 2, 128, 128])
    igbd_d = din("igbd", [L, 2, 128, 128])
    waup_d = din("wa_up", [L, 128, 384])
    gup_d = din("g_up", [L, 128, 384])
    wout_d = din("w_out", [L, D, D])
    wg_d = din("wg", [L, D, DFF])
    wu_d = din("wu", [L, D, DFF])
    wd_d = din("wd", [L, DFF, D])
    outT_d = nc.dram_tensor("outT", [8, 128, T], F32, kind="ExternalOutput").ap()
    tap_d = {}
    for name, shape in taps:
        tap_d[name] = nc.dram_tensor("tap_" + name, list(shape), F32, kind="ExternalOutput").ap()

    P = Prog(nc)

    class Tl:
        def __init__(self, name, shape, dt=F32, nd=1):
            self.t = es.enter_context(nc.sbuf_tensor(name, list(shape), dt))
            self.d = [Dep() for _ in range(nd)]
            self.D = self.d[0]

    EOBJ = {"dve": "vector", "pool": "gpsimd", "act": "scalar"}

    def tt(eng, out, in0, in1, op, R, W):
        P.op(eng, lambda e: e.tensor_tensor(out, in0, in1, op), reads=R, writes=W)

    def ts(eng, out, in0, s1, s2, op0, op1, R, W):
        if s2 is None:
            P.op(eng, lambda e: e.tensor_scalar(out, in0, s1, None, op0), reads=R, writes=W)
        else:
            P.op(eng, lambda e: e.tensor_scalar(out, in0, s1, s2, op0, op1), reads=R, writes=W)

    def stt(eng, out, in0, sc, in1, op0, op1, R, W):
        P.op(eng, lambda e: e.scalar_tensor_tensor(out, in0, sc, in1, op0, op1), reads=R, writes=W)

    def act(out, in_, func, R, W, bias=None, scale=None):
        kw = {}
        if bias is not None:
            kw["bias"] = bias
        if scale is not None:
            kw["scale"] = scale
        P.op("act", lambda e: e.activation(out, in_, func, **kw), reads=R, writes=W)

    def cp(eng, out, in_, R, W):
        if eng == "act":
            P.op("act", lambda e: e.copy(out, in_), reads=R, writes=W)
        else:
            P.op(eng, lambda e: e.tensor_copy(out, in_), reads=R, writes=W)

    def mm(out, lhsT, rhs, start, stop, R, W):
        P.op("pe", lambda e: e.matmul(out, lhsT, rhs, start=start, stop=stop), reads=R, writes=W)

    def memset(eng, ap, val, W):
        P.op(eng, lambda e: e.memset(ap, val), writes=W)

    def scan(out, d0, d1, init, R, W):
        P.op("dve", lambda e: e.tensor_tensor_scan(out, d0, d1, init, ALU.mult, ALU.add), reads=R, writes=W)

    banks = []
    for i in range(8):
        banks.append((es.enter_context(nc.psum_tensor(f"bank{i}", [128, 512], F32)), BDep()))
    bctr = [0]

    def nps():
        b = banks[bctr[0] % 8]
        bctr[0] += 1
        return b

    X = Tl("X", [128, 8, T], F32, nd=8)
    COLS = Tl("COLS", [128, NCOL])
    CST = Tl("CST", [128, NCONST])
    ident = CST.t[:, K_ID:K_ID + 128]
    bones = CST.t[:, K_BO:K_BO + 128]
    m_inc = CST.t[:, K_MINC:K_MINC + 64]
    m_str = CST.t[:, K_MSTR:K_MSTR + 64]
    m_tril = CST.t[:, K_MTRIL:K_MTRIL + 64]
    idp = CST.t[:, K_IDP:K_IDP + 64]
    rst = CST.t[:, K_RST:K_RST + TB]

    def col(name, j=0, n=1):
        return COLS.t[:, CI[name] + j:CI[name] + j + n]

    P.dma("sp", COLS.t[:], cols_d, writes=[COLS.D])
    P.dma("sp", CST.t[:], cst_d, writes=[CST.D])
    for j in range(8):
        P.dma("act", X.t[:, j, :], xT_d[j], writes=[X.d[j]])

    NSCR = 12
    scr = [Tl(f"scr{i}", [128, TB + 4]) for i in range(NSCR)]
    sctr = [0]

    def tmp():
        s = scr[sctr[0] % NSCR]
        sctr[0] += 1
        return s

    def interleave(gens):
        gens = list(gens)
        while gens:
            for g in list(gens):
                try:
                    next(g)
                except StopIteration:
                    gens.remove(g)

    CSI = Tl("CSI", [128, 8])
    MOD = Tl("MOD", [128, 48])
    ADA = Tl("ADA", [128, 32])
    DER = Tl("DER", [128, 16])

    act(CSI.t[:], col("c", 0, 8), AF.Silu, [COLS.D], [CSI.D])
    CSB = Tl("CSB", [128, 8], BF16)
    cp("dve", CSB.t[:], CSI.t[:], [CSI.D], [CSB.D])


    NST = 2
    NWB = 7
    wst = [Tl(f"wst{i}", [128, 8, 128]) for i in range(NST)]
    wbf = [Tl(f"wbf{i}", [128, 8, 128], BF16) for i in range(NWB)]
    wctr = [0]
    sctr2 = [0]

    def wchunk(src_ap, nk=8, cast=True):
        if not cast:
            i = sctr2[0] % NST
            sctr2[0] += 1
            P.dma("sp", wst[i].t[:, 0:nk, :], src_ap, writes=[wst[i].D])
            return wst[i]
        i = wctr[0] % NWB
        wctr[0] += 1
        P.dma("pool", wbf[i].t[:, 0:nk, :], src_ap, writes=[wbf[i].D])
        return wbf[i]

    WOUT = Tl("WOUT", [128, 8, D], BF16)
    HBF = Tl("HBF", [128, 8, TB], BF16)
    YT = Tl("YT", [128, 8, TB], BF16, nd=8)
    RSTD = Tl("RSTD", [128, 512])

    def tap(name, ap, deps):
        if name in tap_d:
            t = P.dma("sp", tap_d[name], ap, reads=deps)
            tap_toks.append(t)
    tap_toks = []

    ONES = Tl("ONES", [128, 128])
    memset("pool", ONES.t[:], 1.0, [ONES.D])
    ONESB = Tl("ONESB", [128, 128], BF16)
    BONESB = Tl("BONESB", [128, 128], BF16)
    memset("pool", ONESB.t[:], 1.0, [ONESB.D])
    cp("dve", BONESB.t[:], bones, [CST.D], [BONESB.D])
    bonesb = BONESB.t[:]

    def bfv(tile):
        return tile.t[:, 0:TB // 2].bitcast(BF16)

    EPS = Tl("EPS", [128, 8])
    memset("pool", EPS.t[:, 0:1], 1e-6, [EPS.D])
    memset("pool", EPS.t[:, 1:2], 1e-12, [EPS.D])
    memset("pool", EPS.t[:, 2:3], 64e-5, [EPS.D])
    memset("pool", EPS.t[:, 3:4], 1.0, [EPS.D])
    memset("pool", EPS.t[:, 4:5], 0.0, [EPS.D])
    eps6, eps12, epsgn, one_c, zero_c = EPS.t[:, 0:1], EPS.t[:, 1:2], EPS.t[:, 2:3], EPS.t[:, 3:4], EPS.t[:, 4:5]

    def norm_block(t0, ntok, Acol, Bcol, dstbf, dstdep, doff=0):
        bk, bd = nps()
        for h0 in range(0, ntok, TB):
            for j in range(8):
                s = tmp()
                act(bfv(s), X.t[:, j, t0 + h0:t0 + h0 + TB], AF.Square, [X.d[j]], [s.D])
                mm(bk[:, h0:h0 + TB], ONESB.t[:], bfv(s), j == 0, j == 7, [s.D, ONESB.D], [bd])
        act(RSTD.t[:, :ntok], bk[:, :ntok], AF.Ln, [bd, EPS.D], [RSTD.D], bias=eps6, scale=1.0 / D)
        act(RSTD.t[:, :ntok], RSTD.t[:, :ntok], AF.Exp, [RSTD.D], [RSTD.D], scale=-0.5)
        for j in range(8):
            for h0 in range(0, ntok, TB):
                s = tmp()
                stt("dve", s.t[:, :TB], X.t[:, j, t0 + h0:t0 + h0 + TB], Acol(j), RSTD.t[:, h0:h0 + TB],
                    ALU.mult, ALU.mult, [X.d[j], RSTD.D, ADA.D], [s.D])
                act(dstbf[:, j, doff + h0:doff + h0 + TB], s.t[:, :TB], AF.Identity, [s.D, MOD.D], [dstdep], bias=Bcol(j))

    class V:
        def __init__(self, ap, nd=1):
            self.t = ap
            self.d = [Dep() for _ in range(nd)]
            self.D = self.d[0]
    NBIG = 9728
    BIG = Tl("BIG", [128, NBIG])
    H2 = V(BIG.t[:, 0:4096].bitcast(BF16).rearrange("p (k t) -> p k t", k=8))
    ACTT = V(BIG.t[:, 4096:9728].bitcast(BF16).rearrange("p (m t) -> p m t", m=11), nd=11)

    scr2 = [V(BIG.t[:, 6060 + i * 260:6060 + (i + 1) * 260]) for i in range(12)]
    SSETS = [scr[0:4] + scr2[0:4], scr[4:8] + scr2[4:8], scr[8:12] + scr2[8:12]]
    PT = [V(BIG.t[:, i * 260:(i + 1) * 260]) for i in range(11)]
    CARRY = Tl("CARRY", [128, 11])
    HST = Tl("HST", [128, 2])
    XAH = Tl("XAH", [128, 2, 3])
    SB = [Tl(f"SB{i}", [128, 64]) for i in range(3)]
    QTs = [Tl(f"QT{i}", [128, TB], BF16) for i in range(3)]
    KTs = [Tl(f"KT{i}", [128, TB], BF16) for i in range(3)]
    QAs = [Tl(f"QA{i}", [128, TB], BF16) for i in range(3)]
    KAs = [Tl(f"KA{i}", [128, TB], BF16) for i in range(3)]
    PVBs = [Tl(f"PVB{i}", [128, TB], BF16) for i in range(3)]
    IDB = Tl("IDB", [128, 128], BF16)
    HBs = [[Tl(f"HB{j}_{i}", [128, 64], BF16) for i in range(5)] for j in range(3)]
    SNs = [Tl(f"SN{i}", [128, 64]) for i in range(3)]
    OBD = [Dep() for _ in range(3)]
    GBs = [Tl(f"GB{i}", [128, 16]) for i in range(3)]
    TOK = [V(BIG.t[:, 4908 + i * 384:4908 + (i + 1) * 384].bitcast(BF16).rearrange("p (c a b) -> p c a b", c=NCH, a=3))
           for i in range(3)]
    ZC = Tl("ZC", [128, 3, 64])
    BH = [Tl(f"BH{i}", [128, TB], BF16) for i in range(3)]
    AH = [Tl(f"AH{i}", [128, TB], BF16) for i in range(3)]
    RH = [Tl(f"RH{i}", [128, TB], BF16) for i in range(3)]
    KH = [Tl(f"KH{i}", [128, TB], BF16) for i in range(3)]
    VB = [Tl(f"VB{i}", [128, TB], BF16) for i in range(3)]
    ZB = Tl("ZB", [128, 3, 64], BF16)
    SGG = V(BIG.t[:, 4396:4652])
    TW = V(BIG.t[:, 4652:4908])
    BON = [V(BIG.t[:, 3628 + i * 256:3628 + (i + 1) * 256]) for i in range(3)]
    YC = V(BIG.t[:, 2860:3628].rearrange("p (a t) -> p a t", a=3))
    GAM = Tl("GAM", [128, 3, NCH])
    MQ = [[Tl(f"MQ{c}_{i}", [128, 3, 64], BF16) for i in range(2)] for c in range(NCH)]
    MP = [[Tl(f"MP{c}_{i}", [128, 3, 64], BF16) for i in range(2)] for c in range(NCH)]
    MT = [Tl(f"MT{i}", [128, 3, 64], BF16) for i in range(NCH)]
    MAK = [Tl(f"MAK{i}", [128, 3, 64], BF16) for i in range(NCH)]
    MRB = [Tl(f"MRB{i}", [128, 3, 64], BF16) for i in range(NCH)]
    MRK = [Tl(f"MRK{i}", [128, 3, 64], BF16) for i in range(NCH)]
    XS = Tl("XS", [128, 3, 64], BF16); US = Tl("US", [128, 3, 64], BF16); ZTMP = Tl("ZTMP", [128, 3, 64])
    RGBD = Tl("RGBD", [128, 2, 128]); IGBD = Tl("IGBD", [128, 2, 128])
    WAUP = Tl("WAUP", [128, 384]); GUP = Tl("GUP", [128, 384])

    mqctr = [0]
    cp("dve", IDB.t[:], ident, [CST.D], [IDB.D])
    identb = IDB.t

    def bc3(ap2):
        return ap2.unsqueeze(1).to_broadcast([128, 3, 64])

    for l in range(n_layers):
        if l == 0:
            memset("dve", DER.t[:, 2:5], 0.0, [DER.D])
        else:
            tt("dve", DER.t[:, 2:5], col("hgrn_lb1", 0, 3), col("hgrn_lb0", 0, 3), ALU.subtract, [COLS.D], [DER.D])
            act(DER.t[:, 2:5], DER.t[:, 2:5], AF.Sigmoid, [DER.D], [DER.D])
        ts("dve", DER.t[:, 5:8], DER.t[:, 2:5], -1.0, 1.0, ALU.mult, ALU.add, [DER.D], [DER.D])
        act(DER.t[:, 0:2], col(f"lam{l}", 0, 2), AF.Exp, [COLS.D], [DER.D], scale=-1.0)
        act(DER.t[:, 0:2], DER.t[:, 0:2], AF.Ln, [DER.D, EPS.D], [DER.D], bias=one_c)
        ts("dve", DER.t[:, 0:2], DER.t[:, 0:2], -8.0, None, ALU.mult, None, [DER.D], [DER.D])
        lbc = lambda hp: DER.t[:, 2 + hp:3 + hp]
        omlc = lambda hp: DER.t[:, 5 + hp:6 + hp]
        nsp8 = lambda cc: DER.t[:, cc:cc + 1]

        P.dma("sp", RGBD.t[:], rgbd_d[l].rearrange("c p n -> p c n"), writes=[RGBD.D])
        P.dma("sp", IGBD.t[:], igbd_d[l].rearrange("c p n -> p c n"), writes=[IGBD.D])
        P.dma("sp", WAUP.t[:], waup_d[l], writes=[WAUP.D])
        P.dma("sp", GUP.t[:], gup_d[l], writes=[GUP.D])

        adv = ada_d[l].rearrange("(k p) m -> p k m", p=128)
        mbk, mbd = nps()
        for j in range(48):
            st = wchunk(adv[:, :, j * 128:(j + 1) * 128])
            for k in range(8):
                mm(mbk[:, j:j + 1], st.t[:, k, :], CSB.t[:, k:k + 1], k == 0, k == 7, [st.D, CSB.D], [mbd])
        tt("dve", MOD.t[:], mbk[:, 0:48], col(f"ada_b{l}", 0, 48), ALU.add, [mbd, COLS.D], [MOD.D])
        stt("dve", ADA.t[:, 0:8], MOD.t[:, 8:16], 1.0, col(f"norm1_g{l}", 0, 8), ALU.add, ALU.mult, [MOD.D, COLS.D], [ADA.D])
        stt("dve", ADA.t[:, 8:16], MOD.t[:, 32:40], 1.0, col(f"norm2_g{l}", 0, 8), ALU.add, ALU.mult, [MOD.D, COLS.D], [ADA.D])
        tap(f"mod{l}", MOD.t[:], [MOD.D])

        wov = wout_d[l].rearrange("(k p) n -> p k n", p=128)
        for f in range(8):
            wb = wchunk(wov[:, :, f * 128:(f + 1) * 128])
            cp("act", WOUT.t[:, :, f * 128:(f + 1) * 128], wb.t[:], [wb.D], [WOUT.D])

        wiv = win_d[l].rearrange("(k p) n -> p k n", p=128)

        def project(chunk, dst_ap, dst_deps, evac="act"):
            wb = wchunk(wiv[:, :, chunk * 128:(chunk + 1) * 128])
            bk, bd = nps()
            for k in range(8):
                mm(bk[:, :TB], wb.t[:, k, :], HBF.t[:, k, :], k == 0, k == 7, [wb.D, HBF.D], [bd])
            cp(evac, dst_ap, bk[:, :TB], [bd], dst_deps)

        memset("dve", HST.t[:], 0.0, [HST.D])
        memset("dve", CARRY.t[:], 0.0, [CARRY.D])
        for hp in range(3):
            memset("dve", SB[hp].t[:], 0.0, [SB[hp].D])
        memset("dve", ZC.t[:], 0.0, [ZC.D])
        memset("dve", ZB.t[:], 0.0, [ZB.D])
        memset("dve", XAH.t[:], 0.0, [XAH.D])

        for tb in range(NTB):
            t0 = tb * TB
            norm_block(t0, TB, lambda j: ADA.t[:, j:j + 1], lambda j: MOD.t[:, j:j + 1], HBF.t, HBF.D)

            if do_mixer[0]:
                def _gen_mixa(cc):
                    S = SSETS[cc]
                    XA, YA = PT[cc], PT[2 + cc]
                    yield
                    project(cc, XA.t[:, 3:3 + TB], [XA.D])
                    yield
                    cp("act", XA.t[:, 0:3], XAH.t[:, cc, :], [XAH.D], [XA.D])
                    yield
                    cp("act", XAH.t[:, cc, :], XA.t[:, TB:TB + 3], [XA.D], [XAH.D])
                    yield
                    project(2 + cc, YA.t[:, :TB], [YA.D], evac="dve")
                    yield
                    u = S[0]
                    yield
                    cw = lambda j: col(f"conv_w{l}", cc * 4 + j)
                    yield
                    ts("dve", u.t[:, :TB], XA.t[:, 0:TB], cw(0), col(f"conv_b{l}", cc), ALU.mult, ALU.add, [XA.D, COLS.D], [u.D])
                    yield
                    for j in range(1, 4):
                        stt("dve", u.t[:, :TB], XA.t[:, j:j + TB], cw(j), u.t[:, :TB], ALU.mult, ALU.add, [XA.D, COLS.D, u.D], [u.D])
                    rb, rbd = nps()
                    yield
                    mm(rb[:, :TB], RGBD.t[:, cc, :], u.t[:, :TB], True, True, [RGBD.D, u.D], [rbd])
                    yield
                    ib, ibd = nps()
                    yield
                    mm(ib[:, :TB], IGBD.t[:, cc, :], u.t[:, :TB], True, True, [IGBD.D, u.D], [ibd])
                    yield
                    r = S[1]; ig = S[2]
                    yield
                    act(r.t[:, :TB], rb[:, :TB], AF.Sigmoid, [rbd, COLS.D], [r.D], bias=col(f"rg_b{l}", cc))
                    yield
                    act(ig.t[:, :TB], ib[:, :TB], AF.Sigmoid, [ibd, COLS.D], [ig.D], bias=col(f"ig_b{l}", cc))
                    yield
                    a = S[3]
                    yield
                    act(a.t[:, :TB], r.t[:, :TB], AF.Exp, [r.D, DER.D], [a.D], scale=nsp8(cc))
                    yield
                    m = S[4]
                    yield
                    act(m.t[:, :TB], a.t[:, :TB], AF.Square, [a.D], [m.D])
                    yield
                    act(m.t[:, :TB], m.t[:, :TB], AF.Sqrt, [m.D, EPS.D], [m.D], bias=one_c, scale=-1.0)
                    yield
                    if tb == 0:
                        memset("dve", m.t[:, 0:1], 1.0, [m.D])
                    tt("dve", m.t[:, :TB], m.t[:, :TB], ig.t[:, :TB], ALU.mult, [m.D, ig.D], [m.D])
                    yield
                    tt("dve", m.t[:, :TB], m.t[:, :TB], u.t[:, :TB], ALU.mult, [m.D, u.D], [m.D])
                    yield
                    h = S[5]
                    yield
                    scan(h.t[:, :TB], a.t[:, :TB], m.t[:, :TB], HST.t[:, cc:cc + 1], [a.D, m.D, HST.D], [h.D])
                    yield
                    cp("dve", HST.t[:, cc:cc + 1], h.t[:, TB - 1:TB], [h.D], [HST.D])
                    yield
                    g1 = S[6]
                    yield
                    act(g1.t[:, :TB], YA.t[:, :TB], AF.Square, [YA.D], [g1.D])
                    yield
                    ts("dve", g1.t[:, :TB], g1.t[:, :TB], 0.044715, 1.0, ALU.mult, ALU.add, [g1.D], [g1.D])
                    yield
                    tt("dve", g1.t[:, :TB], g1.t[:, :TB], YA.t[:, :TB], ALU.mult, [g1.D, YA.D], [g1.D])
                    yield
                    act(g1.t[:, :TB], g1.t[:, :TB], AF.Sigmoid, [g1.D], [g1.D], scale=1.5957691216057308)
                    yield
                    tt("dve", g1.t[:, :TB], g1.t[:, :TB], YA.t[:, :TB], ALU.mult, [g1.D, YA.D], [g1.D])
                    yield
                    tt("dve", h.t[:, :TB], h.t[:, :TB], g1.t[:, :TB], ALU.mult, [h.D, g1.D], [h.D])
                    yield
                    sq = S[7]
                    yield
                    act(bfv(sq), h.t[:, :TB], AF.Square, [h.D], [sq.D])
                    yield
                    sb_, sbd = nps()
                    yield
                    mm(sb_[:, :TB], bonesb, bfv(sq), True, True, [BONESB.D, sq.D], [sbd])
                    yield
                    act(sq.t[:, :TB], sb_[:, :TB], AF.Ln, [sbd, EPS.D], [sq.D], bias=eps6, scale=1.0 / 64)
                    yield
                    act(sq.t[:, :TB], sq.t[:, :TB], AF.Exp, [sq.D], [sq.D], scale=-0.5)
                    yield
                    stt("dve", YT.t[:, cc, :], h.t[:, :TB], col(f"beta{l}", cc), sq.t[:, :TB], ALU.mult, ALU.mult,
                        [h.D, sq.D, COLS.D], [YT.d[cc]])
                    yield
                interleave([_gen_mixa(_i) for _i in range(2)])
            else:
                for cc in range(2):
                    memset("dve", YT.t[:, cc, :], 0.0, [YT.d[cc]])

            if not do_mixer[1]:
                for hp in range(3):
                    memset("dve", YT.t[:, 2 + hp, :], 0.0, [YT.d[2 + hp]])
            else:
                PGs = [PT[6], PT[7], PT[8]]
                def _gen_bprep(hp):
                    S = SSETS[hp]
                    Pq, Pf, Pg = PT[hp], PT[3 + hp], PGs[hp]
                    yield
                    qa, ka, QT, KT, PVB, GB = QAs[hp], KAs[hp], QTs[hp], KTs[hp], PVBs[hp], GBs[hp]
                    yield
                    project(4 + hp, Pq.t[:, :TB], [Pq.D])
                    yield
                    project(7 + hp, Pf.t[:, :TB], [Pf.D], evac="dve")
                    yield
                    project(10 + hp, PVB.t[:], [PVB.D])
                    yield
                    project(13 + hp, Pg.t[:, :TB], [Pg.D], evac="dve")
                    yield
                    s1 = S[0]; kg = S[1]; bc = S[2]; dd = S[3]; e1 = S[4]
                    yield
                    act(s1.t[:, :TB], Pf.t[:, :TB], AF.Sigmoid, [Pf.D], [s1.D])
                    yield
                    act(s1.t[:, :TB], s1.t[:, :TB], AF.Identity, [s1.D, DER.D], [s1.D], bias=lbc(hp), scale=omlc(hp))
                    yield
                    act(s1.t[:, :TB], s1.t[:, :TB], AF.Ln, [s1.D], [s1.D])
                    yield
                    act(kg.t[:, :TB], Pf.t[:, :TB], AF.Sigmoid, [Pf.D], [kg.D], scale=-1.0)
                    yield
                    ts("dve", kg.t[:, :TB], kg.t[:, :TB], omlc(hp), None, ALU.mult, None, [kg.D, DER.D], [kg.D])
                    yield
                    scan(bc.t[:, :TB], rst, s1.t[:, :TB], zero_c, [CST.D, s1.D, EPS.D], [bc.D])
                    yield
                    bc3v = bc.t[:, :TB].rearrange("p (c s) -> p c s", s=CH)
                    yield
                    bmid = bc3v[:, :, 31:32]
                    yield
                    blast = bc3v[:, :, 63:64]
                    yield
                    act(e1.t[:, :TB], bc.t[:, :TB], AF.Exp, [bc.D], [e1.D])
                    yield
                    stt("dve", qa.t[:], Pq.t[:, :TB], 0.125, e1.t[:, :TB], ALU.mult, ALU.mult, [Pq.D, e1.D], [qa.D])
                    yield
                    ts("dve", e1.t[:, :TB], bc.t[:, :TB], -1.0, 60.0, ALU.mult, ALU.min, [bc.D], [e1.D])
                    yield
                    act(e1.t[:, :TB], e1.t[:, :TB], AF.Exp, [e1.D], [e1.D])
                    yield
                    tt("dve", ka.t[:], kg.t[:, :TB], e1.t[:, :TB], ALU.mult, [kg.D, e1.D], [ka.D])
                    yield
                    tt("dve", dd.t[:, :TB].rearrange("p (c s) -> p c s", s=CH), bc3v, bmid.to_broadcast([128, NCH, CH]),
                       ALU.subtract, [bc.D], [dd.D])
                    act(e1.t[:, :TB], dd.t[:, :TB], AF.Exp, [dd.D], [e1.D])
                    yield
                    stt("dve", QT.t[:], Pq.t[:, :TB], 0.125, e1.t[:, :TB], ALU.mult, ALU.mult, [Pq.D, e1.D], [QT.D])
                    yield
                    act(e1.t[:, :TB], dd.t[:, :TB], AF.Exp, [dd.D], [e1.D], scale=-1.0)
                    yield
                    tt("dve", KT.t[:], kg.t[:, :TB], e1.t[:, :TB], ALU.mult, [kg.D, e1.D], [KT.D])
                    yield
                    gbv = GB.t[:, 0:12].rearrange("p (a c) -> p a c", c=NCH)
                    yield
                    act(gbv[:, 0, :].unsqueeze(2), bmid, AF.Exp, [bc.D], [GB.D])
                    yield
                    act(gbv[:, 1, :].unsqueeze(2), blast, AF.Exp, [bc.D], [GB.D])
                    yield
                    tt("dve", gbv[:, 2, :].unsqueeze(2), blast, bmid, ALU.subtract, [bc.D], [GB.D])
                    yield
                    act(gbv[:, 2, :], gbv[:, 2, :], AF.Exp, [GB.D], [GB.D])
                    yield
                    yield
                interleave([_gen_bprep(_i) for _i in range(3)])
                for c in range(NCH):
                    cs_ = slice(c * CH, (c + 1) * CH)
                    ca_ = slice(c * CH, c * CH + 32)
                    cb_ = slice(c * CH + 32, (c + 1) * CH)
                    abs_ = []
                    for hp in range(3):
                        qa, ka, QT, KT, PVB = QAs[hp], KAs[hp], QTs[hp], KTs[hp], PVBs[hp]
                        ab, abd = nps()
                        abs_.append((ab, abd))
                        for hh in range(2):
                            ps_ = slice(hh * 64, hh * 64 + 64)
                            mm(ab[ps_, 0:32], ka.t[ps_, cs_], qa.t[ps_, ca_], True, True, [ka.D, qa.D], [abd])
                            mm(ab[ps_, 32:64], KT.t[ps_, cs_], QT.t[ps_, cb_], True, True, [KT.D, QT.D], [abd])
                            mm(ab[ps_, 64:128], PVB.t[ps_, cs_], identb[ps_, ps_], True, True, [PVB.D, IDB.D], [abd])
                            mm(ab[ps_, 128:192], KT.t[ps_, cs_], identb[ps_, ps_], True, True, [KT.D, IDB.D], [abd])
                    for hp in range(3):
                        ATS, VTK, KTK, STL, SBb = HBs[hp]
                        ab, abd = abs_[hp]
                        tt("dve", ATS.t[:], ab[:, 0:64], m_inc, ALU.mult, [abd, CST.D], [ATS.D])
                        cp("act", VTK.t[:], ab[:, 64:128], [abd], [VTK.D])
                        cp("act", KTK.t[:], ab[:, 128:192], [abd], [KTK.D])
                        ts("dve", STL.t[:], SB[hp].t[:], GBs[hp].t[:, c:c + 1], None, ALU.mult, None, [SB[hp].D, GBs[hp].D], [STL.D])
                        cp("act", SBb.t[:], SB[hp].t[:], [SB[hp].D], [SBb.D])
                    obs_ = []
                    for hp in range(3):
                        ATS, VTK, KTK, STL, SBb = HBs[hp]
                        qa, QT = QAs[hp], QTs[hp]
                        snb, snd = nps()
                        for hh in range(2):
                            ps_ = slice(hh * 64, hh * 64 + 64)
                            mm(snb[ps_, 0:64], KTK.t[ps_, :], VTK.t[ps_, :], True, True, [KTK.D, VTK.D], [snd])
                        ob, obd = nps()
                        obs_.append((ob, obd, snb, snd))
                        for hh in range(2):
                            ps_ = slice(hh * 64, hh * 64 + 64)
                            mm(ob[ps_, 0:64], VTK.t[ps_, :], ATS.t[ps_, :], True, False, [VTK.D, ATS.D], [obd])
                            mm(ob[ps_, 0:32], SBb.t[ps_, :], qa.t[ps_, ca_], False, False, [SBb.D, qa.D], [obd])
                            mm(ob[ps_, 32:64], STL.t[ps_, :], QT.t[ps_, cb_], False, True, [STL.D, QT.D], [obd])
                    for hp in range(3):
                        ob, obd, snb, snd = obs_[hp]
                        GB = GBs[hp]
                        ts("dve", SNs[hp].t[:], snb[:, 0:64], GB.t[:, 8 + c:9 + c], None, ALU.mult, None, [snd, GB.D], [SNs[hp].D])
                        cp("act", YC.t[:, hp, cs_], ob[:, 0:64], [obd], [OBD[hp]])
                        stt("dve", SB[hp].t[:], SB[hp].t[:], GB.t[:, 4 + c:5 + c], SNs[hp].t[:], ALU.mult, ALU.add,
                            [SB[hp].D, GB.D, SNs[hp].D], [SB[hp].D])
                def _gen_bepi(hp):
                    S = SSETS[hp]
                    Pg = PGs[hp]
                    yield
                    OBt = YC.t[:, hp, :]
                    yield
                    sq = S[0]
                    yield
                    act(bfv(sq), OBt, AF.Square, [OBD[hp]], [sq.D])
                    yield
                    sb_, sbd = nps()
                    yield
                    mm(sb_[:, :TB], bonesb, bfv(sq), True, True, [BONESB.D, sq.D], [sbd])
                    yield
                    act(sq.t[:, :TB], sb_[:, :TB], AF.Ln, [sbd, EPS.D], [sq.D], bias=eps6, scale=1.0 / 64)
                    yield
                    act(sq.t[:, :TB], sq.t[:, :TB], AF.Exp, [sq.D], [sq.D], scale=-0.5)
                    yield
                    sl = S[1]
                    yield
                    act(sl.t[:, :TB], Pg.t[:, :TB], AF.Silu, [Pg.D], [sl.D])
                    yield
                    tt("dve", sq.t[:, :TB], sq.t[:, :TB], OBt, ALU.mult, [sq.D, OBD[hp]], [sq.D])
                    yield
                    stt("dve", YT.t[:, 2 + hp, :], sq.t[:, :TB], col(f"beta{l}", 2 + hp), sl.t[:, :TB], ALU.mult, ALU.mult,
                        [sq.D, sl.D, COLS.D], [YT.d[2 + hp]])

                    yield
                interleave([_gen_bepi(_i) for _i in range(3)])
            if do_mixer[2]:
                for i in range(11):
                    Xt = PT[i]
                    project(16 + i, Xt.t[:, 1:1 + TB], [Xt.D], evac=("act" if i % 2 == 0 else "dve"))
                    cp("act", Xt.t[:, 0:1], CARRY.t[:, i:i + 1], [CARRY.D], [Xt.D])
                    cp("act", CARRY.t[:, i:i + 1], Xt.t[:, TB:TB + 1], [Xt.D], [CARRY.D])
                    d_ = tmp()
                    tt("dve", d_.t[:, :TB], Xt.t[:, 0:TB], Xt.t[:, 1:1 + TB], ALU.subtract, [Xt.D], [d_.D])
                    stt("dve", Xt.t[:, 1:1 + TB], d_.t[:, :TB], col(f"mu{l}", i), Xt.t[:, 1:1 + TB], ALU.mult, ALU.add,
                        [d_.D, Xt.D, COLS.D], [Xt.D])
                sR = [PT[i] for i in range(0, 3)]
                sK = [PT[i] for i in range(3, 6)]
                sV = [PT[i] for i in range(6, 9)]
                sWA, sG = PT[9], PT[10]
                V1 = lambda t_: t_.t[:, 1:1 + TB]
                tw = TW
                act(tw.t[0:64, :TB], sWA.t[0:64, 1:1 + TB], AF.Tanh, [sWA.D], [tw.D])
                act(SGG.t[:], V1(sG), AF.Sigmoid, [sG.D], [SGG.D])
                def _gen_cprep(hp):
                    S = SSETS[hp]
                    R_, K_, V_ = sR[hp], sK[hp], sV[hp]
                    yield
                    kk = S[0]; sq = S[1]; k2 = S[2]; e_ = S[3]; cum = S[4]; ex = S[5]; lw = S[6]; aa = S[7]
                    yield
                    hc = slice(hp * 128, (hp + 1) * 128)
                    yield
                    zb, zbd = nps()
                    yield
                    mm(zb[:, :TB], WAUP.t[0:64, hc], tw.t[0:64, :TB], True, True, [WAUP.D, tw.D], [zbd])
                    yield
                    act(lw.t[:, :TB], zb[:, :TB], AF.Sigmoid, [zbd, COLS.D], [lw.D], bias=col(f"w0{l}", hp))
                    yield
                    ts("dve", lw.t[:, :TB], lw.t[:, :TB], -0.6065306597126334, None, ALU.mult, None, [lw.D], [lw.D])
                    yield
                    ab_, abd_ = nps()
                    yield
                    mm(ab_[:, :TB], WAUP.t[64:128, hc], sWA.t[64:128, 1:1 + TB], True, True, [WAUP.D, sWA.D], [abd_])
                    yield
                    act(aa.t[:, :TB], ab_[:, :TB], AF.Sigmoid, [abd_, COLS.D], [aa.D], bias=col(f"a0{l}", hp))
                    yield
                    ts("dve", kk.t[:, :TB], V1(K_), col(f"k_k{l}", hp), None, ALU.mult, None, [K_.D, COLS.D], [kk.D])
                    yield
                    act(bfv(sq), kk.t[:, :TB], AF.Square, [kk.D], [sq.D])
                    yield
                    nb_, nbd_ = nps()
                    yield
                    mm(nb_[:, :TB], bonesb, bfv(sq), True, True, [BONESB.D, sq.D], [nbd_])
                    yield
                    act(sq.t[:, :TB], nb_[:, :TB], AF.Ln, [nbd_, EPS.D], [sq.D], bias=eps12)
                    yield
                    act(sq.t[:, :TB], sq.t[:, :TB], AF.Exp, [sq.D], [sq.D], scale=-0.5)
                    yield
                    tt("dve", kk.t[:, :TB], kk.t[:, :TB], sq.t[:, :TB], ALU.mult, [kk.D, sq.D], [kk.D])
                    yield
                    ts("dve", k2.t[:, :TB], aa.t[:, :TB], -1.0, col(f"k_a{l}", hp), ALU.add, ALU.mult, [aa.D, COLS.D], [k2.D])
                    yield
                    stt("dve", k2.t[:, :TB], k2.t[:, :TB], 1.0, V1(K_), ALU.add, ALU.mult, [k2.D, K_.D], [k2.D])
                    yield
                    tt("dve", e_.t[:, :TB], V1(R_), k2.t[:, :TB], ALU.mult, [R_.D, k2.D], [e_.D])
                    yield
                    ts("dve", bfv(sq), e_.t[:, :TB], col(f"r_k{l}", hp), None, ALU.mult, None, [e_.D, COLS.D], [sq.D])
                    yield
                    bb_, bbd_ = nps()
                    yield
                    mm(bb_[:, :TB], bonesb, bfv(sq), True, True, [BONESB.D, sq.D], [bbd_])
                    yield
                    tt("dve", BON[hp].t[:], bb_[:, :TB], V1(V_), ALU.mult, [bbd_, V_.D], [BON[hp].D])
                    yield
                    scan(cum.t[:, :TB], rst, lw.t[:, :TB], zero_c, [CST.D, lw.D, EPS.D], [cum.D])
                    yield
                    act(ex.t[:, :TB], cum.t[:, :TB], AF.Exp, [cum.D], [ex.D])
                    yield
                    cp("act", GAM.t[:, hp, :].unsqueeze(2), ex.t[:, :TB].rearrange("p (c s) -> p c s", s=CH)[:, :, 63:64], [ex.D], [GAM.D])
                    yield
                    tt("dve", RH[hp].t[:], V1(R_), ex.t[:, :TB], ALU.mult, [R_.D, ex.D], [RH[hp].D])
                    yield
                    cp("act", VB[hp].t[:], V1(V_), [V_.D], [VB[hp].D])
                    yield
                    act(ex.t[:, :TB], cum.t[:, :TB], AF.Exp, [cum.D], [ex.D], scale=-1.0)
                    yield
                    tt("dve", KH[hp].t[:], k2.t[:, :TB], ex.t[:, :TB], ALU.mult, [k2.D, ex.D], [KH[hp].D])
                    yield
                    tt("dve", e_.t[:, :TB], kk.t[:, :TB], aa.t[:, :TB], ALU.mult, [kk.D, aa.D], [e_.D])
                    yield
                    tt("dve", BH[hp].t[:], e_.t[:, :TB], ex.t[:, :TB], ALU.mult, [e_.D, ex.D], [BH[hp].D])
                    yield
                    tt("dve", cum.t[:, :TB], cum.t[:, :TB], lw.t[:, :TB], ALU.subtract, [cum.D, lw.D], [cum.D])
                    yield
                    act(ex.t[:, :TB], cum.t[:, :TB], AF.Exp, [cum.D], [ex.D])
                    yield
                    stt("dve", AH[hp].t[:], kk.t[:, :TB], -1.0, ex.t[:, :TB], ALU.mult, ALU.mult, [kk.D, ex.D], [AH[hp].D])
                    yield
                    yield
                interleave([_gen_cprep(_i) for _i in range(3)])
                for c in range(NCH):
                    cs1 = slice(1 + c * CH, 1 + (c + 1) * CH)
                    cs_ = slice(c * CH, (c + 1) * CH)
                    srcs = [(VB, False), (BH, False), (KH, False)]
                    for qi, (src, halo) in enumerate(srcs):
                        tb_, tbd_ = nps()
                        for hp in range(3):
                            for hh in range(2):
                                ps_ = slice(hh * 64, hh * 64 + 64)
                                sap = src[hp].t[ps_, cs1] if halo else src[hp].t[ps_, cs_]
                                mm(tb_[ps_, hp * 64:(hp + 1) * 64], sap, identb[ps_, ps_], True, True, [src[hp].D, IDB.D], [tbd_])
                        cp("act" if qi != 1 else "dve", TOK[qi].t[:, c, :, :], tb_[:, 0:192].rearrange("p (a b) -> p a b", b=64),
                           [tbd_], [TOK[qi].D])
                cur = [None] * NCH
                for c in range(NCH):
                    cs_ = slice(c * CH, (c + 1) * CH)
                    q0 = MQ[c][0]; p0 = MP[c][0]
                    specs = [
                        (q0, lambda hp, ps_: BH[hp].t[ps_, cs_], lambda hp, ps_: AH[hp].t[ps_, cs_], m_str, lambda hp: [BH[hp].D, AH[hp].D]),
                        (p0, lambda hp, ps_: AH[hp].t[ps_, cs_], lambda hp, ps_: BH[hp].t[ps_, cs_], m_tril, lambda hp: [BH[hp].D, AH[hp].D]),
                        (MAK[c], lambda hp, ps_: KH[hp].t[ps_, cs_], lambda hp, ps_: AH[hp].t[ps_, cs_], m_str, lambda hp: [KH[hp].D, AH[hp].D]),
                        (MRB[c], lambda hp, ps_: BH[hp].t[ps_, cs_], lambda hp, ps_: RH[hp].t[ps_, cs_], m_inc, lambda hp: [BH[hp].D, RH[hp].D]),
                        (MRK[c], lambda hp, ps_: KH[hp].t[ps_, cs_], lambda hp, ps_: RH[hp].t[ps_, cs_], m_inc, lambda hp: [KH[hp].D, RH[hp].D]),
                    ]
                    for si, (dst, lf, rf, msk, dps) in enumerate(specs):
                        b_, bd_ = nps()
                        for hp in range(3):
                            for hh in range(2):
                                ps_ = slice(hh * 64, hh * 64 + 64)
                                mm(b_[ps_, hp * 64:(hp + 1) * 64], lf(hp, ps_), rf(hp, ps_), True, True, dps(hp), [bd_])
                        tt("dve", dst.t[:], b_[:, 0:192].rearrange("p (a b) -> p a b", b=64), bc3(msk), ALU.mult, [bd_, CST.D], [dst.D])
                    tt("dve", MT[c].t[:], q0.t[:], bc3(idp), ALU.add, [q0.D, CST.D], [MT[c].D])
                    cur[c] = (q0, p0)
                for lev in range(1, 6):
                    pbs = []
                    for c in range(NCH):
                        qc, pc = cur[c]
                        pb_, pbd_ = nps()
                        pbs.append((pb_, pbd_))
                        for hp in range(3):
                            for hh in range(2):
                                ps_ = slice(hh * 64, hh * 64 + 64)
                                mm(pb_[ps_, hp * 64:(hp + 1) * 64], qc.t[ps_, hp, :], pc.t[ps_, hp, :], True, True, [qc.D, pc.D], [pbd_])
                                if lev < 5:
                                    mm(pb_[ps_, 192 + hp * 64:192 + (hp + 1) * 64], pc.t[ps_, hp, :], qc.t[ps_, hp, :], True, True,
                                       [qc.D, pc.D], [pbd_])
                    for c in range(NCH):
                        pb_, pbd_ = pbs[c]
                        qn = MQ[c][lev % 2]; pn = MP[c][lev % 2]
                        cp("act", pn.t[:], pb_[:, 0:192].rearrange("p (a b) -> p a b", b=64), [pbd_], [pn.D])
                        if lev < 5:
                            cp("dve", qn.t[:], pb_[:, 192:384].rearrange("p (a b) -> p a b", b=64), [pbd_], [qn.D])
                        cur[c] = (qn, pn)
                    tbs = []
                    for c in range(NCH):
                        qn, pn = cur[c]
                        tb2, tbd2 = nps()
                        tbs.append((tb2, tbd2))
                        for hp in range(3):
                            for hh in range(2):
                                ps_ = slice(hh * 64, hh * 64 + 64)
                                mm(tb2[ps_, hp * 64:(hp + 1) * 64], pn.t[ps_, hp, :], MT[c].t[ps_, hp, :], True, True, [pn.D, MT[c].D], [tbd2])
                    for c in range(NCH):
                        tb2, tbd2 = tbs[c]
                        tt("dve", MT[c].t[:], MT[c].t[:], tb2[:, 0:192].rearrange("p (a b) -> p a b", b=64), ALU.add, [MT[c].D, tbd2], [MT[c].D])
                for c in range(NCH):
                    cs1 = slice(1 + c * CH, 1 + (c + 1) * CH)
                    cs_ = slice(c * CH, (c + 1) * CH)
                    xb, xbd = nps()
                    for hp in range(3):
                        for hh in range(2):
                            ps_ = slice(hh * 64, hh * 64 + 64)
                            o_ = xb[ps_, hp * 64:(hp + 1) * 64]
                            mm(o_, AH[hp].t[ps_, cs_], ZB.t[ps_, hp, :], True, False, [AH[hp].D, ZB.D], [xbd])
                            mm(o_, MAK[c].t[ps_, hp, :], TOK[0].t[ps_, c, hp, :], False, True, [MAK[c].D, TOK[0].D], [xbd])
                    cp("act", XS.t[:], xb[:, 0:192].rearrange("p (a b) -> p a b", b=64), [xbd], [XS.D])
                    ub, ubd = nps()
                    for hp in range(3):
                        for hh in range(2):
                            ps_ = slice(hh * 64, hh * 64 + 64)
                            mm(ub[ps_, hp * 64:(hp + 1) * 64], MT[c].t[ps_, hp, :], XS.t[ps_, hp, :], True, True, [MT[c].D, XS.D], [ubd])
                    cp("dve", US.t[:], ub[:, 0:192].rearrange("p (a b) -> p a b", b=64), [ubd], [US.D])
                    yb, ybd = nps()
                    znb, znd = nps()
                    for hp in range(3):
                        for hh in range(2):
                            ps_ = slice(hh * 64, hh * 64 + 64)
                            o2 = znb[ps_, hp * 64:(hp + 1) * 64]
                            mm(o2, TOK[1].t[ps_, c, hp, :], US.t[ps_, hp, :], True, False, [TOK[1].D, US.D], [znd])
                            mm(o2, TOK[2].t[ps_, c, hp, :], TOK[0].t[ps_, c, hp, :], False, True, [TOK[2].D, TOK[0].D], [znd])
                    for hp in range(3):
                        for hh in range(2):
                            ps_ = slice(hh * 64, hh * 64 + 64)
                            o_ = yb[ps_, hp * 64:(hp + 1) * 64]
                            mm(o_, ZB.t[ps_, hp, :], RH[hp].t[ps_, cs_], True, False, [ZB.D, RH[hp].D], [ybd])
                            mm(o_, US.t[ps_, hp, :], MRB[c].t[ps_, hp, :], False, False, [US.D, MRB[c].D], [ybd])
                            mm(o_, TOK[0].t[ps_, c, hp, :], MRK[c].t[ps_, hp, :], False, True, [TOK[0].D, MRK[c].D], [ybd])
                    cp("act", YC.t[:, :, cs_], yb[:, 0:192].rearrange("p (a b) -> p a b", b=64), [ybd], [YC.D] + OBD)
                    tt("dve", ZTMP.t[:], znb[:, 0:192].rearrange("p (a b) -> p a b", b=64), ZC.t[:], ALU.add, [znd, ZC.D], [ZTMP.D])
                    tt("dve", ZB.t[:], ZTMP.t[:], GAM.t[:, :, c:c + 1].to_broadcast([128, 3, 64]), ALU.mult, [ZTMP.D, GAM.D], [ZB.D])
                    tt("dve", ZC.t[:], ZTMP.t[:], GAM.t[:, :, c:c + 1].to_broadcast([128, 3, 64]), ALU.mult, [ZTMP.D, GAM.D], [ZC.D])
                def _gen_cgn(hp):
                    S = SSETS[hp]
                    mb_, mbd_ = nps()
                    yield
                    mm(mb_[:, :TB], bones, YC.t[:, hp, :], True, True, [CST.D, YC.D, OBD[hp]], [mbd_])
                    yield
                    yc = S[0]; sq = S[1]
                    yield
                    stt("dve", yc.t[:, :TB], mb_[:, :TB], -1.0 / 64, YC.t[:, hp, :], ALU.mult, ALU.add, [mbd_, YC.D, OBD[hp]], [yc.D])
                    yield
                    act(bfv(sq), yc.t[:, :TB], AF.Square, [yc.D], [sq.D])
                    yield
                    vb_, vbd_ = nps()
                    yield
                    mm(vb_[:, :TB], bonesb, bfv(sq), True, True, [BONESB.D, sq.D], [vbd_])
                    yield
                    act(sq.t[:, :TB], vb_[:, :TB], AF.Ln, [vbd_, EPS.D], [sq.D], bias=epsgn, scale=1.0 / 64)
                    yield
                    act(sq.t[:, :TB], sq.t[:, :TB], AF.Exp, [sq.D], [sq.D], scale=-0.5)
                    yield
                    tt("dve", yc.t[:, :TB], yc.t[:, :TB], sq.t[:, :TB], ALU.mult, [yc.D, sq.D], [yc.D])
                    yield
                    act(yc.t[:, :TB], yc.t[:, :TB], AF.Identity, [yc.D, COLS.D], [yc.D], bias=col(f"lnx_b{l}", hp), scale=col(f"lnx_w{l}", hp))
                    yield
                    tt("dve", yc.t[:, :TB], yc.t[:, :TB], BON[hp].t[:], ALU.add, [yc.D, BON[hp].D], [yc.D])
                    yield
                    gb_, gbd_ = nps()
                    yield
                    mm(gb_[:, :TB], GUP.t[:, hp * 128:(hp + 1) * 128], SGG.t[:], True, True, [GUP.D, SGG.D], [gbd_])
                    yield
                    stt("dve", YT.t[:, 5 + hp, :], yc.t[:, :TB], col(f"beta{l}", 5 + hp), gb_[:, :TB], ALU.mult, ALU.mult,
                        [yc.D, gbd_, COLS.D], [YT.d[5 + hp]])
                    yield
                interleave([_gen_cgn(_i) for _i in range(3)])
            else:
                for hp in range(3):
                    memset("dve", YT.t[:, 5 + hp, :], 0.0, [YT.d[5 + hp]])

            if tb == 0 and f"y{l}" in tap_d:
                ytmp = tmp()
                for j in range(8):
                    cp("dve", ytmp.t[:, :TB], YT.t[:, j, :], [YT.d[j]], [ytmp.D])
                    t = P.dma("sp", tap_d[f"y{l}"][j], ytmp.t[:, :TB], reads=[ytmp.D])
                    tap_toks.append(t)

            for f in range(8):
                bk, bd = nps()
                for k in range(8):
                    mm(bk[:, :TB], WOUT.t[:, k, f * 128:(f + 1) * 128], YT.t[:, k, :], k == 0, k == 7, [WOUT.D, YT.d[k]], [bd])
                stt("dve", X.t[:, f, t0:t0 + TB], bk[:, :TB], MOD.t[:, 16 + f:17 + f], X.t[:, f, t0:t0 + TB], ALU.mult, ALU.add,
                    [bd, MOD.D, X.d[f]], [X.d[f]])

        if f"x1_{l}" in tap_d:
            for j in range(8):
                for q in range(0, T, 512):
                    tap_toks.append(P.dma("sp", tap_d[f"x1_{l}"][j, :, q:q + 512], X.t[:, j, q:q + 512], reads=[X.d[j]]))

        if do_ffn:
            wgv = wg_d[l].rearrange("(k p) n -> p k n", p=128)
            wuv = wu_d[l].rearrange("(k p) n -> p k n", p=128)
            wdv = wd_d[l].rearrange("(m p) n -> p m n", p=128)
            P.barrier()
            def ffn_norm(fb):
                for th in range(TBF // 512):
                    norm_block(fb * TBF + th * 512, 512, lambda j: ADA.t[:, 8 + j:9 + j], lambda j: MOD.t[:, 24 + j:25 + j], H2.t, H2.D, doff=th * 512)
            ffn_norm(0)
            for fb in range(NTBF):
                t0 = fb * TBF
                for half in range(2):
                    for mi in range(11):
                        m = half * 11 + mi
                        wgb = wchunk(wgv[:, :, m * 128:(m + 1) * 128])
                        wub = wchunk(wuv[:, :, m * 128:(m + 1) * 128])
                        for th in range(TBF // 512):
                            tsl = slice(th * 512, (th + 1) * 512)
                            gk, gd = nps()
                            for k in range(8):
                                mm(gk[:, :], wgb.t[:, k, :], H2.t[:, k, tsl], k == 0, k == 7, [wgb.D, H2.D], [gd])
                            uk, ud = nps()
                            for k in range(8):
                                mm(uk[:, :], wub.t[:, k, :], H2.t[:, k, tsl], k == 0, k == 7, [wub.D, H2.D], [ud])
                            for h0 in range(0, 512, TB):
                                sg = tmp()
                                act(sg.t[:, :TB], gk[:, h0:h0 + TB], AF.Silu, [gd], [sg.D])
                                tt("dve", ACTT.t[:, mi, th * 512 + h0:th * 512 + h0 + TB], sg.t[:, :TB], uk[:, h0:h0 + TB], ALU.mult,
                                   [sg.D, ud], [ACTT.d[mi]])
                    if half == 1 and fb + 1 < NTBF:
                        ffn_norm(fb + 1)
                    for f in range(8):
                        w1 = wchunk(wdv[:, half * 11:half * 11 + 8, f * 128:(f + 1) * 128])
                        w2 = wchunk(wdv[:, half * 11 + 8:half * 11 + 11, f * 128:(f + 1) * 128], nk=3)
                        for th in range(TBF // 512):
                            tsl = slice(th * 512, (th + 1) * 512)
                            bk, bd = nps()
                            for mi in range(11):
                                wt = w1.t[:, mi, :] if mi < 8 else w2.t[:, mi - 8, :]
                                wd_ = w1.D if mi < 8 else w2.D
                                mm(bk[:, :], wt, ACTT.t[:, mi, tsl], mi == 0, mi == 10, [wd_, ACTT.d[mi]], [bd])
                            xs_ = slice(t0 + th * 512, t0 + (th + 1) * 512)
                            stt("dve", X.t[:, f, xs_], bk[:, :], MOD.t[:, 40 + f:41 + f], X.t[:, f, xs_], ALU.mult, ALU.add,
                                [bd, MOD.D, X.d[f]], [X.d[f]])
            P.barrier()
        if f"x2_{l}" in tap_d:
            for j in range(8):
                for q in range(0, T, 512):
                    tap_toks.append(P.dma("sp", tap_d[f"x2_{l}"][j, :, q:q + 512], X.t[:, j, q:q + 512], reads=[X.d[j]]))

    out_toks = []
    for fb in range(T // 512):
        t0 = fb * 512
        bk, bd = nps()
        for h0 in range(0, 512, TB):
            for j in range(8):
                s = tmp()
                act(bfv(s), X.t[:, j, t0 + h0:t0 + h0 + TB], AF.Square, [X.d[j]], [s.D])
                mm(bk[:, h0:h0 + TB], ONESB.t[:], bfv(s), j == 0, j == 7, [s.D, ONESB.D], [bd])
        act(RSTD.t[:, :512], bk[:, :512], AF.Ln, [bd, EPS.D], [RSTD.D], bias=eps6, scale=1.0 / D)
        act(RSTD.t[:, :512], RSTD.t[:, :512], AF.Exp, [RSTD.D], [RSTD.D], scale=-0.5)
        for j in range(8):
            for h0 in range(0, 512, TB):
                o = tmp()
                stt("dve", o.t[:, :TB], X.t[:, j, t0 + h0:t0 + h0 + TB], col("final_g", j), RSTD.t[:, h0:h0 + TB], ALU.mult, ALU.mult,
                    [X.d[j], RSTD.D, COLS.D], [o.D])
                out_toks.append(P.dma("sp", outT_d[j, :, t0 + h0:t0 + h0 + TB], o.t[:, :TB], reads=[o.D]))
    for t in out_toks + tap_toks:
        P.wait_tok("sp", t)
    build_program.last_counts = (dict(P.cnt), dict(P.dcnt))
    P.emit()
    P.close()
    es.close()
    return nc


def make_in_maps(inp):
    f = lambda a: np.ascontiguousarray(np.asarray(a, np.float32))
    cst = make_consts()
    rgbd = blockdiag(inp["rg_w"])
    igbd = blockdiag(inp["ig_w"])
    wa_up = f(np.concatenate([np.asarray(inp["rwkv_w_up"]), np.asarray(inp["rwkv_a_up"])], axis=1))
    shared = {
        "cst": cst, "ada_w": f(inp["ada_w"]), "w_in": f(inp["w_in"]), "rgbd": rgbd, "igbd": igbd,
        "wa_up": wa_up, "g_up": f(inp["rwkv_g_up"]), "w_out": f(inp["w_out"]),
        "wg": f(inp["ffn_w_gate"]), "wu": f(inp["ffn_w_up"]), "wd": f(inp["ffn_w_down"]),
    }
    x = np.asarray(inp["x"], np.float32)
    maps = []
    for b in range(NB):
        m = dict(shared)
        m["xT"] = np.ascontiguousarray(x[b].T.reshape(8, 128, T))
        m["cols"] = pack_cols(inp, b)
        maps.append(m)
    return maps


def kernel(**inputs):
    nc = build_program()
    maps = make_in_maps(inputs)
    res = run_bass_kernel_spmd(nc, maps, core_ids=list(range(NB)))
    out = np.empty((NB, T, D), np.float32)
    for b in range(NB):
        out[b] = res.results[b]["outT"].reshape(D, T).T
    return out
```
